# Optimizing a Trainium2 kernel written in Bass

```python
import math
import jax, jax.numpy as jnp
from jax import lax
import numpy as np

D_MODEL = 2048
BATCH = 4
SEQ = 4096
DEPTH = 1

D_MIX = D_MODEL
ATTN_W = D_MIX // 2
SSM_W = D_MIX - ATTN_W
HEAD_DIM = 64
N_HEADS = ATTN_W // HEAD_DIM
N_KV_HEADS = 4
KV_REP = N_HEADS // N_KV_HEADS
KV_W = N_KV_HEADS * HEAD_DIM
WINDOW = 128
BLOCK = 128
ROPE_THETA = 10000.0
SSM_H = 16
SSM_G = SSM_W // SSM_H
SSM_P = 64
SSM_CHUNK = 128
DT_MIN = 1e-3
DT_MAX = 1e-1
NORM_EPS = 1e-6
IN_W = ATTN_W + 2 * KV_W + ATTN_W + SSM_W + SSM_W

kernel_name = "hymba_swa_sink_s5_hybrid"


def rms_norm(x, w):
    xf = x.astype(jnp.float32)
    y = xf * lax.rsqrt(jnp.mean(xf * xf, axis=-1, keepdims=True) + NORM_EPS)
    return (y * w.astype(jnp.float32)).astype(x.dtype)


def rope_tables(positions):
    inv_freq = ROPE_THETA ** (-jnp.arange(0, HEAD_DIM, 2, dtype=jnp.float32) / HEAD_DIM)
    ang = positions.astype(jnp.float32)[..., None] * inv_freq
    return jnp.cos(ang)[:, :, None, :], jnp.sin(ang)[:, :, None, :]


def apply_rope(t, cos, sin):
    tf = t.astype(jnp.float32)
    t1, t2 = tf[..., : HEAD_DIM // 2], tf[..., HEAD_DIM // 2 :]
    out = jnp.concatenate([t1 * cos - t2 * sin, t2 * cos + t1 * sin], axis=-1)
    return out.astype(t.dtype)


def swa_sink_attention(q, k, v, positions, q_norm_w, k_norm_w, sinks):
    B, L = q.shape[0], q.shape[1]
    nb = L // BLOCK
    q = q.reshape(B, L, N_HEADS, HEAD_DIM)
    k = k.reshape(B, L, N_KV_HEADS, HEAD_DIM)
    v = v.reshape(B, L, N_KV_HEADS, HEAD_DIM)
    cos, sin = rope_tables(positions)
    q = apply_rope(rms_norm(q, q_norm_w), cos, sin)
    k = apply_rope(rms_norm(k, k_norm_w), cos, sin)

    qb = q.reshape(B, nb, BLOCK, N_KV_HEADS, KV_REP, HEAD_DIM)

    def with_prev(t):
        t = t.reshape(B, nb, BLOCK, N_KV_HEADS, HEAD_DIM)
        prev = jnp.pad(t[:, :-1], ((0, 0), (1, 0), (0, 0), (0, 0), (0, 0)))
        return jnp.concatenate([prev, t], axis=2)

    kb, vb = with_prev(k), with_prev(v)
    scale = 1.0 / math.sqrt(HEAD_DIM)
    s = jnp.einsum('bnqgrd,bnkgd->bngrqk', qb, kb).astype(jnp.float32) * scale

    qi = jnp.arange(BLOCK)[:, None] + BLOCK
    ki = jnp.arange(2 * BLOCK)[None, :]
    rel = qi - ki
    band = (rel >= 0) & (rel < WINDOW)
    has_prev = (jnp.arange(nb)[:, None, None] > 0) | (ki >= BLOCK)[None]
    mask = band[None] & has_prev
    s = jnp.where(mask[None, :, None, None], s, jnp.float32(-1e30))

    sink = jnp.broadcast_to(
        sinks.astype(jnp.float32).reshape(1, 1, N_KV_HEADS, KV_REP, 1, 1),
        s.shape[:-1] + (1,))
    p = jax.nn.softmax(jnp.concatenate([s, sink], axis=-1), axis=-1)[..., :-1]
    o = jnp.einsum('bngrqk,bnkgd->bnqgrd', p.astype(vb.dtype), vb)
    return o.reshape(B, L, ATTN_W)


def s5_ssm(u, a_re, a_im, log_step, b_re, b_im, c_re, c_im, d_skip):
    B, L = u.shape[0], u.shape[1]
    nc = L // SSM_CHUNK
    f32 = jnp.float32
    lam = lax.complex(a_re.astype(f32), a_im.astype(f32))
    delta = jnp.exp(log_step.astype(f32))[:, None]
    lam_bar = jnp.exp(lam * delta)
    b = lax.complex(b_re.astype(f32), b_im.astype(f32))
    b_bar = ((lam_bar - 1.0) / lam)[..., None] * b
    c = lax.complex(c_re.astype(f32), c_im.astype(f32))

    uf = u.astype(f32)
    ug = uf.reshape(B, nc, SSM_CHUNK, SSM_G, SSM_H).transpose(1, 0, 2, 3, 4)

    def combine(left, right):
        a_l, x_l = left
        a_r, x_r = right
        return a_r * a_l, a_r * x_l + x_r

    def chunk_step(h0, u_c):
        bu = jnp.einsum('gph,bigh->bigp', b_bar, u_c.astype(jnp.complex64))
        a = jnp.broadcast_to(lam_bar, bu.shape)
        a_cum, h_loc = lax.associative_scan(combine, (a, bu), axis=1)
        h = h_loc + a_cum * h0[:, None]
        y = jnp.einsum('ghp,bigp->bigh', c, h).real
        return h[:, -1], y

    h0 = jnp.zeros((B, SSM_G, SSM_P), jnp.complex64)
    _, ys = lax.scan(chunk_step, h0, ug)
    y = ys.transpose(1, 0, 2, 3, 4).reshape(B, L, SSM_W)
    return y + d_skip.astype(f32) * uf


def setup_inputs(seed: int = 0) -> dict:
    key = jax.random.key(seed)
    ks = jax.random.split(key, 20)
    f32 = jnp.float32
    x = jax.random.normal(ks[0], (BATCH, SEQ, D_MODEL), f32)
    offs = jax.random.randint(ks[1], (BATCH, 1), 0, 1024, dtype=jnp.int32)
    positions = (jnp.arange(SEQ, dtype=jnp.int32)[None, :] + offs).astype(jnp.int32)
    norm_w = 1.0 + 0.02 * jax.random.normal(ks[2], (D_MODEL,), f32)
    w_in = jax.random.normal(ks[3], (D_MODEL, IN_W), f32) * D_MODEL ** -0.5
    q_norm_w = 1.0 + 0.02 * jax.random.normal(ks[4], (HEAD_DIM,), f32)
    k_norm_w = 1.0 + 0.02 * jax.random.normal(ks[5], (HEAD_DIM,), f32)
    sinks = jax.random.normal(ks[6], (N_HEADS,), f32)
    n = jnp.arange(SSM_P, dtype=f32)[None, :]
    a_re = -0.5 + 0.01 * jax.random.normal(ks[7], (SSM_G, SSM_P), f32)
    a_im = math.pi * n + 0.01 * jax.random.normal(ks[8], (SSM_G, SSM_P), f32)
    log_step = jax.random.uniform(ks[9], (SSM_G,), f32, math.log(DT_MIN), math.log(DT_MAX))
    b_scale = (2.0 * SSM_H) ** -0.5
    b_re = jax.random.normal(ks[10], (SSM_G, SSM_P, SSM_H), f32) * b_scale
    b_im = jax.random.normal(ks[11], (SSM_G, SSM_P, SSM_H), f32) * b_scale
    c_scale = (2.0 * SSM_P) ** -0.5
    c_re = jax.random.normal(ks[12], (SSM_G, SSM_H, SSM_P), f32) * c_scale
    c_im = jax.random.normal(ks[13], (SSM_G, SSM_H, SSM_P), f32) * c_scale
    d_skip = jax.random.normal(ks[14], (SSM_W,), f32)
    w_glu = jax.random.normal(ks[15], (SSM_W, SSM_W), f32) * SSM_W ** -0.5
    b_glu = 0.02 * jax.random.normal(ks[16], (SSM_W,), f32)
    attn_out_norm_w = 1.0 + 0.02 * jax.random.normal(ks[17], (ATTN_W,), f32)
    ssm_out_norm_w = 1.0 + 0.02 * jax.random.normal(ks[18], (SSM_W,), f32)
    w_out = jax.random.normal(ks[19], (D_MIX, D_MODEL), f32) * D_MIX ** -0.5
    return {"x": x, "positions": positions, "norm_w": norm_w, "w_in": w_in,
            "q_norm_w": q_norm_w, "k_norm_w": k_norm_w, "sinks": sinks,
            "a_re": a_re, "a_im": a_im, "log_step": log_step,
            "b_re": b_re, "b_im": b_im, "c_re": c_re, "c_im": c_im,
            "d_skip": d_skip, "w_glu": w_glu, "b_glu": b_glu,
            "attn_out_norm_w": attn_out_norm_w, "ssm_out_norm_w": ssm_out_norm_w,
            "w_out": w_out}


def reference(x, positions, norm_w, w_in, q_norm_w, k_norm_w, sinks,
              a_re, a_im, log_step, b_re, b_im, c_re, c_im, d_skip, w_glu, b_glu,
              attn_out_norm_w, ssm_out_norm_w, w_out):
    for _ in range(DEPTH):
        h = rms_norm(x, norm_w)
        proj = jnp.einsum('bld,de->ble', h, w_in)
        splits = np.cumsum([ATTN_W, KV_W, KV_W, ATTN_W, SSM_W])
        q, k, v, z_attn, u, z_ssm = jnp.split(proj, splits, axis=-1)

        o_attn = swa_sink_attention(q, k, v, positions, q_norm_w, k_norm_w, sinks)
        o_attn = o_attn * jax.nn.silu(z_attn)

        y = s5_ssm(u, a_re, a_im, log_step, b_re, b_im, c_re, c_im, d_skip)
        y = jax.nn.gelu(y)
        y = y * jax.nn.sigmoid(y @ w_glu.astype(jnp.float32) + b_glu.astype(jnp.float32))
        o_ssm = y.astype(x.dtype) * jax.nn.silu(z_ssm)

        merged = jnp.concatenate([rms_norm(o_attn, attn_out_norm_w),
                                  rms_norm(o_ssm, ssm_out_norm_w)], axis=-1)
        x = x + jnp.einsum('ble,ed->bld', merged, w_out).astype(x.dtype)
    return x
```

```python
import math
import os
KF = os.environ.get('KFLAGS', 'abcd')
from contextlib import ExitStack
import numpy as np
import concourse.bass as bass
import concourse.mybir as mybir
from concourse.bass_utils import run_bass_kernel_spmd

F32 = mybir.dt.float32
BF16 = mybir.dt.bfloat16
I32 = mybir.dt.int32
AF = mybir.ActivationFunctionType
ALU = mybir.AluOpType
AX = mybir.AxisListType

NDSEM = 24
S2PI = 6.28318
D = 2048
TC = 2048
SEG = 512
NT = SEG // 128
NSEG = TC // SEG
NPRE = 4
LB_ = int(os.environ.get('KBLK', '8'))
NB = SEG // LB_
IN_W = 4608
WCB = 256
KS = list(range(LB_ + 1)) + [SEG]
NK = len(KS)


class Vw:
    def __init__(s, t, ap):
        s.t = t
        s.ap = ap

    def __getitem__(s, k):
        return Vw(s.t, s.ap[k])

    def r(s, pat, **kw):
        return Vw(s.t, s.ap.rearrange(pat, **kw))

    def bc(s, shape):
        return Vw(s.t, s.ap.broadcast_to(list(shape)))

    def us(s, ax):
        return Vw(s.t, s.ap.unsqueeze(ax))


class T:
    def __init__(self, h, name):
        self.h = h
        self.name = name
        self.lw = None
        self.rd = []

    def __getitem__(self, k):
        return Vw(self, self.h[k])


def _tl(*xs):
    return [x.t for x in xs if isinstance(x, Vw)]


def _ap(x):
    return x.ap if isinstance(x, Vw) else x


class Prog:
    def __init__(self, nc, es):
        self.nc = nc
        self.es = es
        self.ops = {e: [] for e in ("tensor", "vector", "scalar", "gpsimd", "sync")}
        self.cnt = {e: 0 for e in self.ops}
        self.sem = {e: es.enter_context(nc.semaphore("s_" + e)) for e in self.ops}
        self.dsem = [es.enter_context(nc.semaphore("d%d" % i)) for i in range(NDSEM)]
        self.duse = [0] * NDSEM
        self.dpool = {"sync": list(range(0, 16)), "scalar": list(range(16, NDSEM))}
        self.dnext = {"sync": 0, "scalar": 0}
        self.seen = {e: {} for e in self.ops}
        self.nt = 0
        self.pending = []

    def sb(self, shape, dt, name=None):
        self.nt += 1
        name = (name or "t") + "_%d" % self.nt
        return T(self.es.enter_context(self.nc.sbuf_tensor(name, list(shape), dt)), name)

    def ps(self, shape, dt, name=None):
        self.nt += 1
        name = (name or "p") + "_%d" % self.nt
        return T(self.es.enter_context(self.nc.psum_tensor(name, list(shape), dt)), name)

    def dram(self, name, shape, dt, kind):
        return T(self.nc.dram_tensor(name, list(shape), dt, kind=kind).ap(), name)

    def _deps(self, eng, reads, writes):
        toks = []
        for t in reads:
            if t.lw is not None:
                toks.append(t.lw)
        for t in writes:
            if t.lw is not None:
                toks.append(t.lw)
            toks.extend(t.rd)
        out = []
        for tk in toks:
            (semkey, val, teng) = tk
            if teng == "tensor" and eng == "tensor":
                continue
            if self.seen[eng].get(semkey, 0) >= val:
                continue
            self.seen[eng][semkey] = val
            out.append(tk)
        return out

    def _commit(self, tok, reads, writes):
        for t in writes:
            t.lw = tok
            t.rd = []
        for t in reads:
            if t not in writes:
                t.rd.append(tok)
                if len(t.rd) > 12:
                    best = {}
                    for k in t.rd:
                        if k[0] not in best or best[k[0]][1] < k[1]:
                            best[k[0]] = k
                    t.rd = list(best.values())

    def op(self, eng, fn, reads=(), writes=()):
        reads = list(dict.fromkeys(reads))
        writes = list(dict.fromkeys(writes))
        waits = self._deps(eng, reads, writes)
        self.cnt[eng] += 1
        tok = (("e", eng), self.cnt[eng], eng)
        self.ops[eng].append((fn, waits, ("e", eng), 1))
        self._commit(tok, reads, writes)
        return tok

    def flush(self):
        pend, self.pending = self.pending, []
        for (o, i) in pend:
            self.dma(o, i, q="sync", _flush=False)

    def dma(self, out, in_, q="sync", _flush=True):
        if q == "store":
            self.pending.append((out, in_))
            return None
        pool = self.dpool[q]
        k = pool[self.dnext[q] % len(pool)]
        self.dnext[q] += 1
        waits = self._deps(q, [in_.t], [out.t])
        prev = self.duse[k] * 16
        if prev > 0 and self.seen[q].get(("d", k), 0) < prev:
            self.seen[q][("d", k)] = prev
            waits.append((("d", k), prev, "dma"))
        self.duse[k] += 1
        tok = (("d", k), self.duse[k] * 16, "dma")
        oa, ia = out.ap, in_.ap
        self.ops[q].append((lambda e: e.dma_start(out=oa, in_=ia), waits, ("d", k), 16))
        self._commit(tok, [in_.t], [out.t])
        if _flush:
            self.flush()
        return tok

    def barrier(self):
        self.flush()
        latest = []
        for e in self.ops:
            if self.cnt[e] > 0:
                latest.append((("e", e), self.cnt[e], e))
        for k in range(NDSEM):
            if self.duse[k] > 0:
                latest.append((("d", k), self.duse[k] * 16, "dma"))
        for e in self.ops:
            waits = []
            for tk in latest:
                if self.seen[e].get(tk[0], 0) >= tk[1]:
                    continue
                self.seen[e][tk[0]] = tk[1]
                waits.append(tk)
            self.ops[e].append((None, waits, None, 0))

    def _semh(self, key):
        return self.sem[key[1]] if key[0] == "e" else self.dsem[key[1]]

    def emit(self, block, final_tokens):
        P = self
        waited = {e: set() for e in P.ops}
        for eng in P.ops:
            for fn, waits, semkey, inc in P.ops[eng]:
                for (sk, val, _) in waits:
                    if sk[0] == "e":
                        waited[sk[1]].add(val)
        for (sk, val, _) in final_tokens:
            if sk[0] == "e":
                waited[sk[1]].add(val)
        sigmap = {}
        for eng in P.ops:
            n = 0
            sig = 0
            m = {}
            for fn, waits, semkey, inc in P.ops[eng]:
                if fn is None or semkey[0] != "e":
                    continue
                n += 1
                if n in waited[eng]:
                    sig += 1
                    m[n] = sig
            sigmap[eng] = m

        def wait(e, sk, val):
            if sk[0] == "e":
                e.wait_ge(P.sem[sk[1]], sigmap[sk[1]][val])
            else:
                e.wait_ge(P.dsem[sk[1]], val)

        def run(engname, e):
            n = 0
            for fn, waits, semkey, inc in P.ops[engname]:
                for (sk, val, _) in waits:
                    wait(e, sk, val)
                if fn is None:
                    continue
                ins = fn(e)
                if semkey[0] == "e":
                    n += 1
                    if n in sigmap[engname]:
                        ins.then_inc(P.sem[engname], 1)
                else:
                    ins.then_inc(P.dsem[semkey[1]], inc)
            if engname == "sync":
                for (sk, val, _) in final_tokens:
                    wait(e, sk, val)

        @block.sync
        def _(e):
            run("sync", e)

        @block.tensor
        def _(e):
            run("tensor", e)

        @block.vector
        def _(e):
            run("vector", e)

        @block.scalar
        def _(e):
            run("scalar", e)

        @block.gpsimd
        def _(e):
            run("gpsimd", e)

    def tt(self, eng, out, a, b, op):
        self.op(eng, lambda e: e.tensor_tensor(out=out.ap, in0=a.ap, in1=b.ap, op=op), _tl(a, b), _tl(out))

    def ts(self, eng, out, a, s1, op0, s2=None, op1=None):
        if op1 is None:
            if eng == "gpsimd":
                fn = lambda e: e.tensor_scalar(out=out.ap, in0=a.ap, scalar1=_ap(s1), scalar2=(0.0 if op0 == ALU.mult else 1.0),
                                               op0=op0, op1=(ALU.add if op0 == ALU.mult else ALU.mult))
            else:
                fn = lambda e: e.tensor_scalar(out=out.ap, in0=a.ap, scalar1=_ap(s1), scalar2=None, op0=op0)
        else:
            fn = lambda e: e.tensor_scalar(out=out.ap, in0=a.ap, scalar1=_ap(s1), scalar2=_ap(s2), op0=op0, op1=op1)
        self.op(eng, fn, _tl(a, s1, s2), _tl(out))

    def stt(self, out, a, s, b, op0, op1):
        self.op("vector", lambda e: e.scalar_tensor_tensor(out=out.ap, in0=a.ap, scalar=_ap(s), in1=b.ap, op0=op0, op1=op1),
                _tl(a, s, b), _tl(out))

    def act(self, out, a, func, scale=None, bias=None, accum=None):
        kw = {}
        if scale is not None:
            kw["scale"] = _ap(scale)
        if bias is not None:
            kw["bias"] = _ap(bias)
        if accum is not None:
            kw["accum_out"] = accum.ap
        self.op("scalar", lambda e: e.activation(out=out.ap, in_=a.ap, func=func, **kw), _tl(a, scale, bias), _tl(out, accum))

    def cp(self, eng, out, a):
        if eng == "scalar":
            self.op(eng, lambda e: e.copy(out=out.ap, in_=a.ap), _tl(a), _tl(out))
        else:
            self.op(eng, lambda e: e.tensor_copy(out=out.ap, in_=a.ap), _tl(a), _tl(out))

    def mm(self, out, l, r, start, stop, tp=None):
        kw = {} if tp is None else {"tile_position": tp}
        self.op("tensor", lambda e: e.matmul(out=out.ap, lhsT=l.ap, rhs=r.ap, start=start, stop=stop, skip_group_check=True, **kw),
                _tl(l, r), _tl(out))

    def tr(self, out, a, ident, tp=None):
        kw = {} if tp is None else {"tile_position": tp}
        self.op("tensor", lambda e: e.transpose(out=out.ap, in_=a.ap, identity=ident.ap, **kw), _tl(a, ident), _tl(out))

    def scan(self, out, d0, d1, init):
        self.op("vector", lambda e: e.tensor_tensor_scan(out=out.ap, data0=d0.ap, data1=d1.ap, initial=_ap(init), op0=ALU.mult, op1=ALU.add),
                _tl(d0, d1, init), _tl(out))

    def recip(self, out, a):
        self.op("vector", lambda e: e.reciprocal(out=out.ap, in_=a.ap), _tl(a), _tl(out))

    def red(self, out, a):
        self.op("vector", lambda e: e.tensor_reduce(out=out.ap, in_=a.ap, axis=AX.X, op=ALU.add), _tl(a), _tl(out))

    def memset(self, eng, out, val):
        self.op(eng, lambda e: e.memset(out.ap, val), [], _tl(out))


def build_program(dbg=False, stop_at=None):
    nc = bass.Bass("TRN2", target_bir_lowering=False)
    es = ExitStack()
    with es:
        P = Prog(nc, es)
        EI, EO, IN = "ExternalInput", "ExternalOutput", "Internal"
        d_xmain = P.dram("x_main", [TC, D], F32, EI)
        d_xpre = P.dram("x_pre", [NPRE * SEG, D], F32, EI)
        d_xhalo = P.dram("x_halo", [128, D], F32, EI)
        d_pos = P.dram("pos_main", [128, TC // 128], I32, EI)
        d_posh = P.dram("pos_halo", [128, 1], I32, EI)
        d_win = P.dram("w_in", [D, IN_W], F32, EI)
        d_wout = P.dram("w_out", [D, D], F32, EI)
        d_wglu = P.dram("w_glu", [1024, 1024], F32, EI)
        d_normw = P.dram("normw", [128, 16], F32, EI)
        d_outnw = P.dram("outnw", [128, 16], F32, EI)
        d_bglu = P.dram("bglu", [128, 8], F32, EI)
        d_dskip = P.dram("dskip", [128, 8], F32, EI)
        d_qw = P.dram("qw", [128, 64], F32, EI)
        d_kw = P.dram("kw", [128, 64], F32, EI)
        d_sinks = P.dram("sinks", [128, 16], F32, EI)
        d_invf = P.dram("invf", [128, 32], F32, EI)
        d_are = P.dram("are", [128, 32], F32, EI)
        d_aim = P.dram("aim", [128, 32], F32, EI)
        d_lst = P.dram("lst", [128, 32], F32, EI)
        d_bre = P.dram("bre", [128, 1024], F32, EI)
        d_bim = P.dram("bim", [128, 1024], F32, EI)
        d_cre = P.dram("cre", [128, 1024], F32, EI)
        d_cim = P.dram("cim", [128, 1024], F32, EI)
        d_ident = P.dram("ident", [128, 128], F32, EI)
        d_maskc = P.dram("maskc", [128, 128], F32, EI)
        d_maskp = P.dram("maskp", [128, 128], F32, EI)
        d_maskf = P.dram("maskf", [128, 128], F32, EI)
        d_kvec = P.dram("kvec", [128, NK], F32, EI)
        d_jrow = P.dram("jrow", [128, NB + 1], F32, EI)
        d_out = P.dram("out", [TC, D], F32, EO)
        s_win = P.dram("s_win", [D, IN_W], BF16, IN)
        s_wout = P.dram("s_wout", [D, D], BF16, IN)
        s_wglu = P.dram("s_wglu", [1024, 1024], BF16, IN)
        s_bs = P.dram("s_bs", [8, 128, 2 * LB_ * 128], BF16, IN)
        s_cs = P.dram("s_cs", [8, 128, 4 * 2 * LB_ * 32], BF16, IN)
        s_fir = P.dram("s_fir", [8, 128, LB_ * 128], BF16, IN)
        s_tab = P.dram("s_tab", [8, 128, 4 * 3 * (NB + 1)], F32, IN)
        dbg_t = {}
        if dbg:
            dbg_t["proj_q"] = P.dram("dbg_q", [128, 1024], F32, EO)

        def load_const(dram_t, shape, dt=F32, name=None):
            t = P.sb(shape, dt, name)
            P.dma(t[:], dram_t[:])
            return t

        identf = load_const(d_ident, [128, 128], name="identf")
        idb = P.sb([128, 128], BF16, "idb")
        P.cp("vector", idb[:], identf[:])
        normw = load_const(d_normw, [128, 16], name="normw")
        outnw = load_const(d_outnw, [128, 16], name="outnw")
        bglu = load_const(d_bglu, [128, 8], name="bglu")
        dskip = load_const(d_dskip, [128, 8], name="dskip")
        qw = load_const(d_qw, [128, 64], name="qw")
        kw = load_const(d_kw, [128, 64], name="kw")
        invf = load_const(d_invf, [128, 32], name="invf")
        sinks = load_const(d_sinks, [128, 16], name="sinks")
        esink = P.sb([128, 16], F32, "esink")
        P.act(esink[:], sinks[:], AF.Exp)
        mk_f = load_const(d_maskc, [128, 128])
        maskc = P.sb([128, 128], BF16, "maskc")
        P.cp("vector", maskc[:], mk_f[:])
        mk_f2 = load_const(d_maskp, [128, 128])
        maskp = P.sb([128, 128], BF16, "maskp")
        P.cp("vector", maskp[:], mk_f2[:])
        mk_f3 = load_const(d_maskf, [128, 128])
        maskf = P.sb([128, 128], BF16, "maskf")
        P.cp("vector", maskf[:], mk_f3[:])
        mask2 = P.sb([128, 2, 2, 128], BF16, "mask2")
        mask2f = P.sb([128, 2, 2, 128], BF16, "mask2f")
        for jj_ in range(2):
            P.cp("vector", mask2[:, 0, jj_, :], mk_f2[:])
            P.cp("vector", mask2[:, 1, jj_, :], mk_f[:])
            P.cp("vector", mask2f[:, 0, jj_, :], mk_f3[:])
            P.cp("vector", mask2f[:, 1, jj_, :], mk_f[:])
        posi = load_const(d_pos, [128, TC // 128], I32)
        posf = P.sb([128, TC // 128], F32, "posf")
        P.cp("vector", posf[:], posi[:])
        poshi = load_const(d_posh, [128, 1], I32)
        poshf = P.sb([128, 1], F32, "poshf")
        P.cp("vector", poshf[:], poshi[:])
        ones_b = P.sb([128, 1], BF16, "ones_b")
        P.memset("vector", ones_b[:], 1.0)
        NTL = TC // 128 + 1
        cos_all = P.sb([128, NTL, 32], F32, "cos_all")
        sin_all = P.sb([128, NTL, 32], F32, "sin_all")
        r8 = P.sb([128, 32], F32, "r8")
        c512 = P.sb([128, 32], F32, "c512")
        s512 = P.sb([128, 32], F32, "s512")
        g0re = P.sb([128, 32], F32, "g0re")
        g0im = P.sb([128, 32], F32, "g0im")
        P.memset("vector", g0re[:], 0.0)
        P.memset("vector", g0im[:], 0.0)

        pA = P.ps([128, 512], F32, "pA")
        pB = P.ps([128, 512], F32, "pB")
        pT = P.ps([128, 1024], BF16, "pT")
        pT2 = P.ps([128, 1024], BF16, "pT2")
        pS = P.ps([128, 512], F32, "pS")
        pO = P.ps([128, 512], F32, "pO")
        pY = P.ps([128, 512], F32, "pY")
        pM = P.ps([128, 512], F32, "pM")

        def phasor(f, cos_out, sin_out, n, scope_es):
            old = P.es
            P.es = scope_es
            fi = P.sb([128, n], I32)
            ff = P.sb([128, n], F32)
            f1 = P.sb([128, n], F32)
            f2 = P.sb([128, n], F32)
            P.es = old
            P.cp("vector", fi[:], f)
            P.cp("vector", ff[:], fi[:])
            P.tt("vector", f1[:], f, ff[:], ALU.subtract)
            P.act(sin_out, f1[:], AF.Sin, scale=S2PI)
            P.ts("vector", f2[:], f1[:], 0.25, ALU.add)
            P.cp("vector", fi[:], f2[:])
            P.cp("vector", ff[:], fi[:])
            P.tt("vector", f1[:], f2[:], ff[:], ALU.subtract)
            P.act(cos_out, f1[:], AF.Sin, scale=S2PI)

        class _Stop(Exception):
            pass

        def stage(name):
            if stop_at == name:
                raise _Stop()

        try:
            stage('wconv')
            with ExitStack() as es2:
                P.es = es2
                are = load_const(d_are, [128, 32])
                aim = load_const(d_aim, [128, 32])
                lst = load_const(d_lst, [128, 32])
                kvec = load_const(d_kvec, [128, NK])
                jrow = load_const(d_jrow, [128, NB + 1])
                dl = P.sb([128, 32], F32)
                P.act(dl[:], lst[:], AF.Exp)
                ard = P.sb([128, 32], F32)
                P.tt("vector", ard[:], are[:], dl[:], ALU.mult)
                th = P.sb([128, 32], F32)
                P.stt(th[:], aim[:], 1.0 / (2.0 * math.pi), dl[:], ALU.mult, ALU.mult)
                fk = P.sb([128, NK, 32], F32)
                ek = P.sb([128, NK, 32], F32)
                kb = kvec[:].us(2).bc([128, NK, 32])
                P.tt("vector", fk[:], th[:].us(1).bc([128, NK, 32]), kb, ALU.mult)
                P.tt("vector", ek[:], ard[:].us(1).bc([128, NK, 32]), kb, ALU.mult)
                rk = P.sb([128, NK, 32], F32)
                fl = lambda t: t[:].r("p a b -> p (a b)")
                P.act(fl(rk), fl(ek), AF.Exp)
                ck = P.sb([128, NK, 32], F32)
                sk = P.sb([128, NK, 32], F32)
                phasor(fl(fk), fl(ck), fl(sk), NK * 32, es2)
                Lr = P.sb([128, NK, 32], F32)
                Li = P.sb([128, NK, 32], F32)
                P.tt("vector", Lr[:], rk[:], ck[:], ALU.mult)
                P.tt("vector", Li[:], rk[:], sk[:], ALU.mult)
                P.cp("vector", r8[:], rk[:, LB_, :])
                P.cp("vector", c512[:], ck[:, LB_ + 1, :])
                P.cp("vector", s512[:], sk[:, LB_ + 1, :])
                nre = P.sb([128, 32], F32)
                P.ts("vector", nre[:], Lr[:, 1, :], -1.0, ALU.add)
                nim = Li[:, 1, :]
                t1 = P.sb([128, 32], F32)
                t2 = P.sb([128, 32], F32)
                den = P.sb([128, 32], F32)
                P.tt("vector", t1[:], are[:], are[:], ALU.mult)
                P.tt("vector", t2[:], aim[:], aim[:], ALU.mult)
                P.tt("vector", den[:], t1[:], t2[:], ALU.add)
                rden = P.sb([128, 32], F32)
                P.recip(rden[:], den[:])
                qre = P.sb([128, 32], F32)
                qim = P.sb([128, 32], F32)
                P.tt("vector", t1[:], nre[:], are[:], ALU.mult)
                P.tt("vector", t2[:], nim, aim[:], ALU.mult)
                P.tt("vector", t1[:], t1[:], t2[:], ALU.add)
                P.tt("vector", qre[:], t1[:], rden[:], ALU.mult)
                P.tt("vector", t1[:], nim, are[:], ALU.mult)
                P.tt("vector", t2[:], nre[:], aim[:], ALU.mult)
                P.tt("vector", t1[:], t1[:], t2[:], ALU.subtract)
                P.tt("vector", qim[:], t1[:], rden[:], ALU.mult)

                with ExitStack() as es3:
                    P.es = es3
                    posall = P.sb([128, NTL], F32)
                    P.cp("vector", posall[:, 0:1], poshf[:])
                    P.cp("vector", posall[:, 1:NTL], posf[:])
                    fro = P.sb([128, NTL, 32], F32)
                    P.tt("vector", fro[:], posall[:].us(2).bc([128, NTL, 32]), invf[:].us(1).bc([128, NTL, 32]), ALU.mult)
                    phasor(fro[:].r("p a b -> p (a b)"), cos_all[:].r("p a b -> p (a b)"), sin_all[:].r("p a b -> p (a b)"), NTL * 32, es3)
                    P.es = es2
                    P.barrier()
                P.es = es2
                with ExitStack() as es3:
                    P.es = es3
                    NJ = NB + 1
                    fj = P.sb([128, 32, NJ], F32)
                    P.tt("vector", fj[:], fk[:, LB_, :].us(2).bc([128, 32, NJ]), jrow[:].us(1).bc([128, 32, NJ]), ALU.mult)
                    tab = P.sb([128, 3, 32, NJ], F32, "tab")
                    phasor(fj[:].r("p a b -> p (a b)"), tab[:, 0].r("p a b -> p (a b)"), tab[:, 1].r("p a b -> p (a b)"), 32 * NJ, es3)
                    P.memset("vector", tab[:, 2].r("p a b -> p (a b)"), 0.0)
                    P.cp("vector", tab[:, 2, :, 1:NJ], r8[:].us(2).bc([128, 32, NB]))
                    for ft in range(8):
                        P.dma(s_tab[ft].r("p (c a b) -> p c a b", c=3, a=4), tab[:, :, 4 * ft:4 * ft + 4, :], q="sync")
                    P.es = es2
                    P.barrier()
                P.es = es2
                P.barrier()

                NSTG = 4
                stg = [P.sb([128, 1536], F32, "stg") for _ in range(NSTG)]
                stb = [P.sb([128, 1536], BF16, "stb") for _ in range(NSTG)]

                def wjob(src_t, dst_t, r, c0, n, scal):
                    return (src_t[r * 128:(r + 1) * 128, c0:c0 + n], dst_t[r * 128:(r + 1) * 128, c0:c0 + n], scal, n)

                jobs = [wjob(d_win, s_win, dk, c * 1536, 1536, normw[:, dk:dk + 1]) for dk in range(16) for c in (1, 2, 0)]
                jobs += [wjob(d_wglu, s_wglu, ft, 0, 1024, None) for ft in range(8)]
                jobs += [wjob(d_wout, s_wout, ft, c * 1024, 1024, outnw[:, ft:ft + 1]) for ft in range(16) for c in range(2)]
                nld = 0
                for i in range(len(jobs)):
                    while nld < len(jobs) and nld < i + NSTG:
                        P.dma(stg[nld % NSTG][:, 0:jobs[nld][3]], jobs[nld][0], q="scalar")
                        nld += 1
                    src, dst, scal, n = jobs[i]
                    if scal is None:
                        P.cp("scalar", stb[i % NSTG][:, 0:n], stg[i % NSTG][:, 0:n])
                    else:
                        P.act(stb[i % NSTG][:, 0:n], stg[i % NSTG][:, 0:n], AF.Copy, scale=scal)
                    P.dma(dst, stb[i % NSTG][:, 0:n], q="scalar")
                bre = load_const(d_bre, [128, 1024])
                bim = load_const(d_bim, [128, 1024])
                cre = load_const(d_cre, [128, 1024])
                cim = load_const(d_cim, [128, 1024])
                cm_tmps = {"vector": [P.sb([128, 32, 32], F32) for _ in range(4)]}

                def cmul(ore, oim, are_, aim_, sre, sim, neg_im=False, eng="vector"):
                    sr = sre.us(2).bc([128, 32, 32])
                    si = sim.us(2).bc([128, 32, 32])
                    u1, u2, u3, u4 = cm_tmps[eng]
                    P.tt(eng, u1[:], are_, sr, ALU.mult)
                    P.tt(eng, u2[:], aim_, si, ALU.mult)
                    P.tt(eng, ore, u1[:], u2[:], ALU.subtract)
                    P.tt(eng, u3[:], aim_, sr, ALU.mult)
                    P.tt(eng, u4[:], are_, si, ALU.mult)
                    if neg_im:
                        P.tt(eng, u3[:], u3[:], u4[:], ALU.add)
                        P.ts(eng, oim, u3[:], -1.0, ALU.mult)
                    else:
                        P.tt(eng, oim, u3[:], u4[:], ALU.add)

                b3 = lambda t: t[:].r("p (a b) -> p a b", b=32)
                Bbre = P.sb([128, 32, 32], F32)
                Bbim = P.sb([128, 32, 32], F32)
                cmul(Bbre[:], Bbim[:], b3(bre), b3(bim), qre[:], qim[:])
                LB = P.sb([128, LB_, 2, 32, 32], BF16, "LB")
                CS = P.sb([128, LB_, 2, 32, 32], BF16, "CS")
                cbf = P.sb([128, 2, 32, 32], BF16, "cbf")
                P.cp("vector", cbf[:, 0], b3(cre))
                P.ts("vector", cbf[:, 1], b3(cim), -1.0, ALU.mult)
                for k in range(LB_):
                    cmul(LB[:, k, 0], LB[:, k, 1], Bbre[:], Bbim[:], Lr[:, k, :], Li[:, k, :])
                for i in range(LB_):
                    cmul(CS[:, i, 0], CS[:, i, 1], b3(cre), b3(cim), Lr[:, i + 1, :], Li[:, i + 1, :], neg_im=True,
                         eng="vector")
                for ft in range(8):
                    for k in range(4):
                        P.dma(s_cs[ft][:, k * 64 * LB_:(k + 1) * 64 * LB_].r("p (i r h) -> p i r h", i=LB_, r=2),
                              CS[:, :, :, 4 * ft + k, :], q="sync")
                fir = P.sb([128, 8, LB_, 128], BF16, "fir")
                P.memset("vector", fir[:].r("p a b c -> p (a b c)"), 0.0)
                bsb = [P.sb([128, 2 * LB_, 128], BF16, "bsb") for _ in range(2)]
                pTs = [pT, pT2]
                for ft in range(8):
                    for hf in range((2 * LB_ + 7) // 8):
                        pt = pTs[hf % 2]
                        pt3 = pt[:].r("p (a b) -> p a b", b=128)
                        nsl = min(8, 2 * LB_ - hf * 8)
                        for k in range(4):
                            gp = 4 * ft + k
                            for il in range(nsl // 2):
                                i = hf * 4 + il
                                for ri in range(2):
                                    P.tr(pt3[32 * k:32 * k + 32, il * 2 + ri, :], LB[:, LB_ - 1 - i, ri, gp, :], idb[:], tp=(0, 32 * k))
                        P.cp("vector" if hf == 0 else "scalar", bsb[ft % 2][:, hf * 8:hf * 8 + nsl, :].r("p a b -> p (a b)"), pt[:, 0:nsl * 128])
                    P.dma(s_bs[ft], bsb[ft % 2][:].r("p a b -> p (a b)"), q="sync")
                    pk3 = pM[:, 0:32 * LB_].r("p (d h) -> p d h", h=32)
                    for k in range(4):
                        gp = 4 * ft + k
                        for d in range(LB_):
                            P.mm(pk3[32 * k:32 * k + 32, d, :], LB[:, d, 0, gp, :], cbf[:, 0, gp, :], start=(d == 0), stop=False, tp=(0, 32 * k))
                            P.mm(pk3[32 * k:32 * k + 32, d, :], LB[:, d, 1, gp, :], cbf[:, 1, gp, :], start=False, stop=True, tp=(0, 32 * k))
                    for k in range(4):
                        P.cp("vector", fir[32 * k:32 * k + 32, ft, :, 32 * k:32 * k + 32], pk3[32 * k:32 * k + 32, :, :])
                    P.dma(s_fir[ft], fir[:, ft].r("p a b -> p (a b)"), q="sync")
                P.es = es
                P.barrier()
            P.es = es
            P.barrier()

            xt = [P.sb([128, D], F32, "xt") for _ in range(1)]
            xn = P.sb([128, D], BF16, "xn")
            xnTs = [P.sb([128, 8, SEG], BF16, "xnT") for _ in range(2)]
            wbuf = [P.sb([128, 16, WCB], BF16, "wbuf") for _ in range(2)]
            uT = P.sb([128, 8, SEG], BF16, "uT")
            szsT = P.sb([128, 8, SEG], BF16, "szsT")
            ygT = P.sb([128, 8, SEG], BF16, "ygT")
            maT = P.sb([128, 8, SEG], BF16, "maT")
            q_sb = [P.sb([128, 1024], BF16, "q_sb") for _ in range(NT)]
            kv_sb = [P.sb([128, 512], BF16, "kv_sb") for _ in range(NT)]
            sz_sb = [P.sb([128, 1024], BF16, "sz_sb") for _ in range(NT)]
            rstd_a = P.sb([128, NT], F32, "rstd_a")
            rstd_s = P.sb([128, NT], F32, "rstd_s")
            kT = [P.sb([128, 4, 128], BF16, "kT") for _ in range(2)]
            vext = [P.sb([128, 4, 65], BF16, "vext") for _ in range(2)]
            for v_ in vext:
                P.memset("vector", v_[:].r("p a b -> p (a b)"), 1.0)
            bs_t = [P.sb([128, 2 * LB_, 128], BF16, "bs_t") for _ in range(2)]
            cs_t = [P.sb([128, 4, LB_, 2, 32], BF16, "cs_t") for _ in range(2)]
            fir_t = [P.sb([128, LB_, 128], BF16, "fir_t") for _ in range(2)]
            NTAB = 3
            tab_t = [P.sb([128, 3, 4, NB + 1], F32, "tab_t") for _ in range(NTAB)]
            wctr = [0]
            sctr = [0]
            xctr = [0]
            pctr = [0]

            def next_wbuf():
                w = wbuf[wctr[0] % 2]
                wctr[0] += 1
                return w

            def mm_bank():
                b = (pA, pB)[pctr[0] % 2]
                pctr[0] += 1
                return b

            def small(shape, dt=F32, name="sm"):
                return P.sb(shape, dt, name)

            ss = small([128, 1]); ms = small([128, 1]); sd = small([128, 1]); rstd = small([128, 1])
            mhalf = small([128, 1])
            P.memset("vector", mhalf[:], -0.5)
            eps_t = small([128, 1])
            P.memset("vector", eps_t[:], 1e-6)

            def norm_stage1(src_rows):
                x_ = xt[0]
                xctr[0] += 1
                P.dma(x_[:], src_rows)
                P.act(xn[:], x_[:], AF.Square, accum=ss[:])
                P.ts("gpsimd", ms[:], ss[:], 1.0 / D, ALU.mult, 1e-6, ALU.add)
                P.tt("gpsimd", rstd[:], ms[:], mhalf[:], ALU.pow)
                P.act(xn[:], x_[:], AF.Copy, scale=rstd[:])

            def norm_transpose(src_rows, tt_slot):
                norm_stage1(src_rows)
                norm_stage2(tt_slot)

            def norm_stage2(tt_slot):
                for hf in range(2):
                    pt = (pT, pT2)[hf]
                    for j in range(8):
                        dk = hf * 8 + j
                        P.tr(pt[:, j * 128:(j + 1) * 128], xn[:, dk * 128:(dk + 1) * 128], idb[:])
                    P.cp("vector" if hf == 0 else "scalar",
                         xnTs[hf][:, :, tt_slot * 128:(tt_slot + 1) * 128],
                         pt[:].r("p (a b) -> p a b", b=128))

            def load_w(scr, col0, nk, ncol=WCB, row0=0):
                w = next_wbuf()
                src = scr[row0:row0 + nk * 128, col0:col0 + ncol].r("(k p) c -> p k c", p=128)
                P.dma(w[:, 0:nk, 0:ncol], src)
                return w

            qn_t = small([128, 1024]); sq_t = small([128, 1024]); ssq = small([128, 16]); rq = small([128, 16])
            ta = small([128, 512]); tb = small([128, 512]); tc_ = small([128, 512]); td = small([128, 512])
            qr = small([128, 1024], BF16); kr = small([128, 256], BF16); kdup = small([128, 4, 128], BF16)
            qT = small([128, 8, 128], BF16)
            rope_sel = [0]
            pe2 = [small([128, 1024], BF16) for _ in range(2)]; pm2 = [small([128, 1024], BF16) for _ in range(2)]
            den4 = small([128, 4]); rden4 = small([128, 4]); o_t = sq_t; oa_t = small([128, 1024], BF16)
            osq = qr; ssa = small([128, 1]);

            def qk_norm_rope(src, nh, w_t, out_bf):
                n = nh * 64
                s3 = lambda v: v.r("p (h d) -> p h d", d=64)
                P.tt("vector", sq_t[:, 0:n], src, src, ALU.mult)
                P.red(ssq[:, 0:nh], s3(sq_t[:, 0:n]))
                P.act(ssq[:, 0:nh], ssq[:, 0:nh], AF.Ln, scale=1.0 / 64, bias=eps_t[:])
                P.act(rq[:, 0:nh], ssq[:, 0:nh], AF.Exp, scale=-0.5)
                P.tt("vector", s3(qn_t[:, 0:n]), s3(src), rq[:, 0:nh].us(2).bc([128, nh, 64]), ALU.mult)
                P.tt("vector", s3(qn_t[:, 0:n]), s3(qn_t[:, 0:n]), w_t[:].us(1).bc([128, nh, 64]), ALU.mult)
                q4 = qn_t[:, 0:n].r("p (h t d) -> p h t d", t=2, d=32)
                o4 = out_bf.r("p (h t d) -> p h t d", t=2, d=32)
                cb_ = cos_all[:, rope_sel[0], :].us(1).bc([128, nh, 32])
                sb_ = sin_all[:, rope_sel[0], :].us(1).bc([128, nh, 32])
                m = nh * 32
                v3 = lambda t: t[:, 0:m].r("p (h d) -> p h d", d=32)
                P.tt("vector", v3(ta), q4[:, :, 0, :], cb_, ALU.mult)
                P.tt("vector", v3(tb), q4[:, :, 1, :], sb_, ALU.mult)
                P.tt("vector", o4[:, :, 0, :], v3(ta), v3(tb), ALU.subtract)
                P.tt("vector", v3(tc_), q4[:, :, 1, :], cb_, ALU.mult)
                P.tt("vector", v3(td), q4[:, :, 0, :], sb_, ALU.mult)
                P.tt("vector", o4[:, :, 1, :], v3(tc_), v3(td), ALU.add)

            if True:

                _ph = {}

                def phasor_small(f, cos_out, sin_out):
                    if "fi" not in _ph:
                        _ph["fi"] = P.sb([128, 32], I32); _ph["ff"] = P.sb([128, 32], F32)
                        _ph["f1"] = P.sb([128, 32], F32); _ph["f2"] = P.sb([128, 32], F32)
                    fi, ff, f1, f2 = _ph["fi"], _ph["ff"], _ph["f1"], _ph["f2"]
                    P.cp("vector", fi[:], f)
                    P.cp("vector", ff[:], fi[:])
                    P.tt("vector", f1[:], f, ff[:], ALU.subtract)
                    P.act(sin_out, f1[:], AF.Sin, scale=S2PI)
                    P.ts("vector", f2[:], f1[:], 0.25, ALU.add)
                    P.cp("vector", fi[:], f2[:])
                    P.cp("vector", ff[:], fi[:])
                    P.tt("vector", f1[:], f2[:], ff[:], ALU.subtract)
                    P.act(cos_out, f1[:], AF.Sin, scale=S2PI)

                def rope_tables2(pos_col):
                    pass

                def prep_kv(kv, gtt, slot):
                    rope_sel[0] = gtt + 1
                    qk_norm_rope(kv[:, 0:256], 4, kw, kr[:])
                    k3 = kr[:].r("p (g d) -> p g d", d=64)
                    P.cp("vector", kdup[:, :, 0:64], k3)
                    P.cp("gpsimd", kdup[:, :, 64:128], k3)
                    for g in range(4):
                        P.tr(pT[:, g * 128:(g + 1) * 128], kdup[:, g, :], idb[:])
                    P.cp("scalar", kT[slot][:].r("p a b -> p (a b)"), pT[:, 0:512])
                    P.cp("vector", vext[slot][:, :, 0:64], kv[:, 256:512].r("p (g d) -> p g d", d=64))

                def attention_tile(tt, slot_cur, slot_prev, first):
                    qk_norm_rope(q_sb[tt][:], 16, qw, qr[:])
                    stage('qrope')
                    for j in range(8):
                        P.tr(pT2[:, j * 128:(j + 1) * 128], qr[:, j * 128:(j + 1) * 128], idb[:])
                    P.cp("scalar", qT[:].r("p a b -> p (a b)"), pT2[:])
                    stage('qT')
                    slots = (slot_prev, slot_cur)
                    mk2 = mask2f if first else mask2
                    sets = ((pS, pM), (pA, pB))

                    def scores(g):
                        bE, bO = sets[g % 2]
                        for kb in range(2):
                            for j in range(4):
                                h = 4 * g + j
                                base = 64 * (h % 2)
                                bank = bE if (h % 2 == 0) else bO
                                c0 = (kb * 2 + j // 2) * 128
                                P.mm(bank[:, c0:c0 + 128], kT[slots[kb]][base:base + 64, g, :], qT[base:base + 64, h // 2, :],
                                     start=True, stop=True)

                    scores(0)
                    for g in range(4):
                        bE, bO = sets[g % 2]
                        pe_ = pe2[g % 2]
                        pm_ = pm2[g % 2]
                        oB = (pO, pY)[g % 2]
                        for bi, bank in enumerate((bE, bO)):
                            P.act(pe_[:, bi * 512:(bi + 1) * 512], bank[:], AF.Exp, scale=0.125)
                        for bi in range(2):
                            P.tt("vector", pm_[:, bi * 512:(bi + 1) * 512], pe_[:, bi * 512:(bi + 1) * 512],
                                 mk2[:].r("p k j q -> p (k j q)"), ALU.mult)
                        if g + 1 < 4:
                            scores(g + 1)
                        n_ = 0
                        for j in range(4):
                            for kb in range(2):
                                c0 = (j % 2) * 512 + (kb * 2 + j // 2) * 128
                                P.mm(oB[:, j * 65:(j + 1) * 65], pm_[:, c0:c0 + 128], vext[slots[kb]][:, g, :],
                                     start=(n_ == 0), stop=(n_ == 7))
                                n_ += 1
                        o3 = oB[:, 0:260].r("p (j d) -> p j d", d=65)
                        P.tt("vector", den4[:], o3[:, :, 64], esink[:, 4 * g:4 * g + 4], ALU.add)
                        P.recip(rden4[:], den4[:])
                        P.tt("vector", o_t[:, g * 256:(g + 1) * 256].r("p (j d) -> p j d", d=64), o3[:, :, 0:64],
                             rden4[:].us(2).bc([128, 4, 64]), ALU.mult)
                    stage('onorm')
                    P.tt("vector", oa_t[:], o_t[:], sz_sb[tt][:], ALU.mult)
                    P.act(osq[:], oa_t[:], AF.Square, accum=ssa[:])
                    P.ts("gpsimd", ssa[:], ssa[:], 1.0 / 1024, ALU.mult, 1e-6, ALU.add)
                    P.tt("gpsimd", rstd_a[:, tt:tt + 1], ssa[:], mhalf[:], ALU.pow)
                    for j in range(8):
                        P.tr(pT[:, j * 128:(j + 1) * 128], oa_t[:, j * 128:(j + 1) * 128], idb[:])
                    P.cp("scalar", maT[:, :, tt * 128:(tt + 1) * 128], pT[:].r("p (a b) -> p a b", b=128))

                NBX = NB + 1
                Sft = [small([128, 4, 2 * NB]) for _ in range(2)]
                Xre = [small([128, 4, NBX])] * 2
                Xim = [small([128, 4, NBX])] * 2
                Gre = [small([128, 4, NBX])] * 2
                Gim = [small([128, 4, NBX])] * 2
                hre = [small([128, 4, NB], BF16) for _ in range(2)]
                him = [small([128, 4, NB], BF16) for _ in range(2)]
                m1 = small([128, 4, NB]); m2 = small([128, 4, NB]); m3 = small([128, 4, NB]); m4 = small([128, 4, NB])
                ysb = small([128, SEG]); y2 = small([128, SEG]); y3 = y2; sg = small([128, SEG]); sg2 = small([128, SEG])
                cn1 = small([128, 4]); cn2 = small([128, 4]); cn3 = small([128, 4]); cn4 = small([128, 4])
                fctr = [0]

                def load_blob(ft, full):
                    s = sctr[0] % 2
                    sctr[0] += 1
                    P.dma(bs_t[s][:].r("p a b -> p (a b)"), s_bs[ft])
                    P.dma(tab_t[(sctr[0] - 1) % NTAB][:].r("p c a b -> p (c a b)"), s_tab[ft])
                    if full:
                        P.dma(cs_t[s][:].r("p k i r h -> p (k i r h)"), s_cs[ft])
                        P.dma(fir_t[s][:].r("p a b -> p (a b)"), s_fir[ft])
                    return s

                def ssm_ft(ft, full):
                    s = load_blob(ft, full)
                    z = fctr[0] % 2
                    fctr[0] += 1
                    u3 = uT[:, ft, :].r("p (j i) -> p j i", i=LB_)
                    y3v = pY[:].r("p (j i) -> p j i", i=LB_)
                    g4 = slice(4 * ft, 4 * ft + 4)
                    if full:
                        for d in range(LB_):
                            for i in range(d, LB_):
                                P.mm(y3v[:, :, i], fir_t[s][:, d, :], u3[:, :, i - d], start=(d == 0 and i == 0), stop=False)
                    sbanks = (pO, pS, pA, pB)
                    for ri in range(2):
                        for i in range(LB_):
                            for k in range(4):
                                base = 32 * k
                                P.mm(sbanks[k][:, 256 + ri * NB:256 + (ri + 1) * NB], bs_t[s][base:base + 32, i * 2 + ri, :],
                                     u3[base:base + 32, :, i], start=(i == 0), stop=(i == LB_ - 1), tp=(base, 0))
                    for k in range(4):
                        P.cp("scalar", Sft[z][:, k, :], sbanks[k][:, 256:256 + 2 * NB])
                    sre_p = Sft[z][:, :, 0:NB]
                    sim_p = Sft[z][:, :, NB:2 * NB]
                    tb_ = tab_t[(sctr[0] - 1) % NTAB]
                    cM = tb_[:, 0, :, 1:NB + 1]
                    sM = tb_[:, 1, :, 1:NB + 1]
                    cD = tb_[:, 0, :, 0:NB]
                    sD = tb_[:, 1, :, 0:NB]
                    P.tt("vector", m1[:], sre_p, cM, ALU.mult)
                    P.tt("vector", m2[:], sim_p, sM, ALU.mult)
                    P.tt("vector", m3[:], sim_p, cM, ALU.mult)
                    P.tt("vector", m4[:], sre_p, sM, ALU.mult)
                    P.cp("vector", Xre[z][:, :, 0], g0re[:, g4])
                    P.cp("vector", Xim[z][:, :, 0], g0im[:, g4])
                    P.tt("vector", Xre[z][:, :, 1:NBX], m1[:], m2[:], ALU.add)
                    P.tt("vector", Xim[z][:, :, 1:NBX], m3[:], m4[:], ALU.subtract)
                    fl = lambda v: v.r("p a b -> p (a b)")
                    cf = fl(tb_[:, 2, :, :])
                    P.scan(fl(Gre[z][:]), cf, fl(Xre[z][:]), 0.0)
                    P.scan(fl(Gim[z][:]), cf, fl(Xim[z][:]), 0.0)
                    P.tt("vector", cn1[:], Gre[z][:, :, NB], c512[:, g4], ALU.mult)
                    P.tt("vector", cn2[:], Gim[z][:, :, NB], s512[:, g4], ALU.mult)
                    P.tt("vector", cn3[:], Gre[z][:, :, NB], s512[:, g4], ALU.mult)
                    P.tt("vector", cn4[:], Gim[z][:, :, NB], c512[:, g4], ALU.mult)
                    P.tt("vector", g0re[:, g4], cn1[:], cn2[:], ALU.subtract)
                    P.tt("vector", g0im[:, g4], cn3[:], cn4[:], ALU.add)
                    if not full:
                        return
                    P.tt("vector", m1[:], Gre[z][:, :, 0:NB], cD, ALU.mult)
                    P.tt("vector", m2[:], Gim[z][:, :, 0:NB], sD, ALU.mult)
                    P.tt("vector", m3[:], Gre[z][:, :, 0:NB], sD, ALU.mult)
                    P.tt("vector", m4[:], Gim[z][:, :, 0:NB], cD, ALU.mult)
                    P.tt("vector", hre[z][:], m1[:], m2[:], ALU.subtract)
                    P.tt("vector", him[z][:], m3[:], m4[:], ALU.add)
                    for i in range(LB_):
                        for k in range(4):
                            base = 32 * k
                            P.mm(y3v[base:base + 32, :, i], cs_t[s][:, k, i, 0, :], hre[z][:, k, :], start=False, stop=False, tp=(0, base))
                        for k in range(4):
                            base = 32 * k
                            P.mm(y3v[base:base + 32, :, i], cs_t[s][:, k, i, 1, :], him[z][:, k, :], start=False, stop=(k == 3 and i == LB_ - 1), tp=(0, base))
                    P.stt(ysb[:], uT[:, ft, :], dskip[:, ft:ft + 1], pY[:], ALU.mult, ALU.add)
                    P.tt("vector", y2[:], ysb[:], ysb[:], ALU.mult)
                    P.ts("vector", y2[:], y2[:], 0.044715, ALU.mult, 1.0, ALU.add)
                    P.tt("vector", y3[:], y2[:], ysb[:], ALU.mult)
                    P.act(sg[:], y3[:], AF.Sigmoid, scale=2.0 * math.sqrt(2.0 / math.pi))
                    P.tt("vector", ygT[:, ft, :], ysb[:], sg[:], ALU.mult)

                def glu_and_gate():
                    for cb in range(4):
                        w = load_w(s_wglu, cb * WCB, 8)
                        for f2 in range(2):
                            fo = cb * 2 + f2
                            bank = mm_bank()
                            for fi_ in range(8):
                                P.mm(bank[:], w[:, fi_, f2 * 128:(f2 + 1) * 128], ygT[:, fi_, :], start=(fi_ == 0), stop=(fi_ == 7))
                            gi_ = (fo % 2) if 'b' in KF else 0
                            sg_ = (sg, sg2)[gi_]
                            ys_ = (ysb, y2)[gi_]
                            oq_ = (pm2[0], pm2[1])[gi_]
                            P.act(sg_[:], bank[:], AF.Sigmoid, bias=bglu[:, fo:fo + 1])
                            P.tt("vector", ys_[:], sg_[:], ygT[:, fo, :], ALU.mult)
                            P.tt("vector", szsT[:, fo, :], ys_[:], szsT[:, fo, :], ALU.mult)
                            P.act(oq_[:, 0:SEG], szsT[:, fo, :], AF.Square)
                            for tt in range(NT):
                                P.mm(pM[:, 256 + tt:256 + tt + 1], oq_[:, tt * 128:(tt + 1) * 128], ones_b[:], start=(fo == 0 and tt == 0),
                                     stop=(fo == 7))
                    P.ts("vector", rstd_s[:], pM[:, 256:256 + NT], 1.0 / 1024, ALU.mult, 1e-6, ALU.add)
                    P.tt("gpsimd", rstd_s[:], rstd_s[:], mhalf[:].bc([128, NT]), ALU.pow)

                xres = [small([128, WCB]) for _ in range(4)]
                ot = [small([128, WCB]) for _ in range(4)]
                octr = [0]

                def out_proj(seg):
                    for cb in range(D // WCB):
                        w = load_w(s_wout, cb * WCB, 16)
                        for tt in range(NT):
                            row0 = seg * SEG + tt * 128
                            b = octr[0] % 4
                            octr[0] += 1
                            P.dma(xres[b][:], d_xmain[row0:row0 + 128, cb * WCB:(cb + 1) * WCB])
                            obanks = (pA, pB, pS, pO, pY, pM) if 'a' in KF else (pA, pB, pA, pB, pA, pB)
                            b1 = obanks[(2 * octr[0]) % 6]
                            for ft in range(8):
                                P.mm(b1[:, 0:WCB], maT[:, ft, tt * 128:(tt + 1) * 128], w[:, ft, :], start=(ft == 0), stop=(ft == 7))
                            b2 = obanks[(2 * octr[0] + 1) % 6]
                            for ft in range(8):
                                P.mm(b2[:, 0:WCB], szsT[:, ft, tt * 128:(tt + 1) * 128], w[:, 8 + ft, :], start=(ft == 0), stop=(ft == 7))
                            P.stt(ot[b][:], b1[:, 0:WCB], rstd_a[:, tt:tt + 1], xres[b][:], ALU.mult, ALU.add)
                            P.stt(ot[b][:], b2[:, 0:WCB], rstd_s[:, tt:tt + 1], ot[b][:], ALU.mult, ALU.add)
                            P.dma(d_out[row0:row0 + 128, cb * WCB:(cb + 1) * WCB], ot[b][:], q="scalar")

                def proj_feature_major(col0, dst, silu):
                    w = load_w(s_win, col0, 16)
                    for f2 in range(2):
                        bank = mm_bank()
                        for dk in range(16):
                            P.mm(bank[:], w[:, dk, f2 * 128:(f2 + 1) * 128], xnTs[dk // 8][:, dk % 8, :], start=(dk == 0), stop=(dk == 15))
                        if silu:
                            P.act(dst[f2], bank[:], AF.Silu)
                        else:
                            P.cp("vector", dst[f2], bank[:])

                def proj_token_major(col0, dst_fn, silu, tts):
                    w = load_w(s_win, col0, 16)
                    for tt in tts:
                        bank = mm_bank()
                        for dk in range(16):
                            P.mm(bank[:, 0:WCB], xnTs[dk // 8][:, dk % 8, tt * 128:(tt + 1) * 128], w[:, dk, :], start=(dk == 0), stop=(dk == 15))
                        if silu:
                            P.act(dst_fn(tt), bank[:, 0:WCB], AF.Silu)
                        else:
                            P.cp("scalar" if tt % 2 else "vector", dst_fn(tt), bank[:, 0:WCB])

                stage('ssmpre')
                def nt_pre(sgi):
                    for tt in range(NT):
                        r0 = sgi * SEG + tt * 128
                        norm_transpose(d_xpre[r0:r0 + 128, :], tt)

                def nt_main(sgi):
                    for tt in range(NT):
                        r0 = sgi * SEG + tt * 128
                        norm_transpose(d_xmain[r0:r0 + 128, :], tt)

                nt_pre(0)
                for sgi in range(NPRE):
                    for c in range(4):
                        proj_feature_major(2560 + c * WCB, [uT[:, 2 * c, :], uT[:, 2 * c + 1, :]], False)
                    if sgi + 1 == NPRE:
                        norm_transpose(d_xhalo[:, :], 0)
                    elif 'd' not in KF:
                        nt_pre(sgi + 1)
                    for ft in range(8):
                        ssm_ft(ft, False)
                        if sgi + 1 < NPRE and ft % 2 == 0 and 'd' in KF:
                            r0n = (sgi + 1) * SEG + (ft // 2) * 128
                            norm_stage1(d_xpre[r0n:r0n + 128, :])
                        if sgi + 1 < NPRE and ft % 2 == 1 and 'd' in KF:
                            norm_stage2(ft // 2)

                stage('prepass')
                proj_token_major(1024, lambda tt: kv_sb[0][:, 0:256], False, [0])
                proj_token_major(1280, lambda tt: kv_sb[0][:, 256:512], False, [0])
                nt_main(0)
                prep_kv(kv_sb[0], -1, 1)
                cur_slot = 1

                stage('halo')
                for sgi in range(NSEG):
                    allt = list(range(NT))
                    for c in range(4):
                        proj_token_major(c * WCB, lambda tt, c=c: q_sb[tt][:, c * WCB:(c + 1) * WCB], False, allt)
                    proj_token_major(1024, lambda tt: kv_sb[tt][:, 0:256], False, allt)
                    proj_token_major(1280, lambda tt: kv_sb[tt][:, 256:512], False, allt)
                    for c in range(4):
                        proj_token_major(1536 + c * WCB, lambda tt, c=c: sz_sb[tt][:, c * WCB:(c + 1) * WCB], True, allt)
                    for c in range(4):
                        proj_feature_major(2560 + c * WCB, [uT[:, 2 * c, :], uT[:, 2 * c + 1, :]], False)
                    for c in range(4):
                        proj_feature_major(3584 + c * WCB, [szsT[:, 2 * c, :], szsT[:, 2 * c + 1, :]], True)
                    if dbg and sgi == 0:
                        dq = qn_t
                        P.cp("vector", dq[:], q_sb[0][:])
                        P.dma(dbg_t["proj_q"][:], dq[:], q="sync")
                    if sgi + 1 < NSEG and 'c' not in KF:
                        nt_main(sgi + 1)
                    if sgi == 0:
                        stage('proj')
                    for tt in range(NT):
                        gtt = sgi * NT + tt
                        if sgi + 1 < NSEG and 'c' in KF:
                            r0n = (sgi + 1) * SEG + tt * 128
                            norm_stage1(d_xmain[r0n:r0n + 128, :])
                        prev_slot = cur_slot
                        cur_slot = 1 - cur_slot
                        prep_kv(kv_sb[tt], gtt, cur_slot)
                        stage('kv0')
                        attention_tile(tt, cur_slot, prev_slot, first=(gtt == 0))
                        if sgi + 1 < NSEG and 'c' in KF:
                            norm_stage2(tt)
                    if sgi == 0:
                        stage('attn')
                    for ft in range(8):
                        ssm_ft(ft, True)
                    if sgi == 0:
                        stage('ssm')
                    glu_and_gate()
                    if sgi == 0:
                        stage('glu')
                    out_proj(sgi)
                    if sgi == 0:
                        stage('seg0')

        except _Stop:
            P.es = es

        finals = [d_out.lw] + list(d_out.rd)
        for t_ in dbg_t.values():
            finals.append(t_.lw)
        P.barrier()
        with nc.Block() as block:
            P.emit(block, [f for f in finals if f is not None])
    return nc


def _host_inputs(x, positions, norm_w, w_in, q_norm_w, k_norm_w, sinks, a_re, a_im, log_step,
                 b_re, b_im, c_re, c_im, d_skip, w_glu, b_glu, attn_out_norm_w, ssm_out_norm_w, w_out):
    f32 = np.float32
    rep = lambda v: np.ascontiguousarray(np.broadcast_to(np.asarray(v, f32)[None, :], (128, len(v))))
    pl = lambda v, n: np.ascontiguousarray(np.asarray(v, f32).reshape(n, 128).T)

    def p_layout(a):
        a = np.asarray(a, f32).reshape(32, 2, 64)
        return np.ascontiguousarray(a.transpose(1, 2, 0).reshape(128, 32))

    def bc_layout(m):
        m = np.asarray(m, f32).reshape(32, 2, 64, 16)
        o = np.zeros((2, 64, 32, 2, 16), f32)
        for g2 in range(2):
            o[g2, :, :, g2, :] = m[:, g2].transpose(1, 0, 2)
        return np.ascontiguousarray(o.reshape(128, 32 * 32))

    common = {
        "w_in": np.ascontiguousarray(w_in, f32), "w_out": np.ascontiguousarray(w_out, f32), "w_glu": np.ascontiguousarray(w_glu, f32),
        "normw": pl(norm_w, 16), "outnw": pl(np.concatenate([np.asarray(attn_out_norm_w), np.asarray(ssm_out_norm_w)]), 16),
        "bglu": pl(b_glu, 8), "dskip": pl(d_skip, 8),
        "qw": rep(q_norm_w), "kw": rep(k_norm_w), "sinks": rep(sinks),
        "invf": rep((10000.0 ** (-np.arange(0, 64, 2, dtype=np.float64) / 64) / (2 * np.pi)).astype(f32)),
        "are": p_layout(a_re), "aim": p_layout(a_im),
        "lst": p_layout(np.broadcast_to(np.asarray(log_step, f32)[:, None], (64, 64))),
        "bre": bc_layout(b_re), "bim": bc_layout(b_im),
        "cre": bc_layout(np.asarray(c_re).transpose(0, 2, 1)), "cim": bc_layout(np.asarray(c_im).transpose(0, 2, 1)),
        "ident": np.eye(128, dtype=f32),
        "maskc": np.triu(np.ones((128, 128), f32)),
        "maskp": np.tril(np.ones((128, 128), f32), -1),
        "kvec": rep(np.array(KS, f32)), "jrow": rep(np.arange(-1, NB).astype(f32)),
    }
    x = np.asarray(x, f32)
    positions = np.asarray(positions, np.int32)
    maps = []
    for c in range(8):
        b, half = c // 2, c % 2
        t0 = half * TC
        m = dict(common)
        m["x_main"] = np.ascontiguousarray(x[b, t0:t0 + TC])
        pos = positions[b, t0:t0 + TC]
        m["pos_main"] = np.ascontiguousarray(pos.reshape(TC // 128, 128).T)
        if half == 1:
            m["x_pre"] = np.ascontiguousarray(x[b, 0:TC])
            m["x_halo"] = np.ascontiguousarray(x[b, t0 - 128:t0])
            m["pos_halo"] = np.ascontiguousarray(positions[b, t0 - 128:t0].reshape(128, 1))
            m["maskf"] = common["maskp"]
        else:
            m["x_pre"] = np.zeros((TC, D), f32)
            m["x_halo"] = np.zeros((128, D), f32)
            m["pos_halo"] = np.zeros((128, 1), np.int32)
            m["maskf"] = np.zeros((128, 128), f32)
        maps.append(m)
    return maps


_NC = {}


def kernel(**inputs):
    maps = _host_inputs(**inputs)
    if "nc" not in _NC:
        _NC["nc"] = build_program()
    res = run_bass_kernel_spmd(_NC["nc"], maps, core_ids=list(range(8)))
    out = np.empty((4, 4096, D), np.float32)
    for c in range(8):
        b, half = c // 2, c % 2
        out[b, half * TC:(half + 1) * TC] = res.results[c]["out"]
    return out
```

```python
import math
import os
KF = os.environ.get('KFLAGS', 'abcd')
from contextlib import ExitStack
import numpy as np
import concourse.bass as bass
import concourse.mybir as mybir
from concourse.bass_utils import run_bass_kernel_spmd

F32 = mybir.dt.float32
BF16 = mybir.dt.bfloat16
I32 = mybir.dt.int32
AF = mybir.ActivationFunctionType
ALU = mybir.AluOpType
AX = mybir.AxisListType

NDSEM = 24
S2PI = 6.28318
D = 2048
TC = 2048
SEG = 512
NT = SEG // 128
NSEG = TC // SEG
NPRE = 4
LB_ = int(os.environ.get('KBLK', '8'))
NB = SEG // LB_
IN_W = 4608
WCB = 256
KS = list(range(LB_ + 1)) + [SEG]
NK = len(KS)


class Vw:
    def __init__(s, t, ap):
        s.t = t
        s.ap = ap

    def __getitem__(s, k):
        return Vw(s.t, s.ap[k])

    def r(s, pat, **kw):
        return Vw(s.t, s.ap.rearrange(pat, **kw))

    def bc(s, shape):
        return Vw(s.t, s.ap.broadcast_to(list(shape)))

    def us(s, ax):
        return Vw(s.t, s.ap.unsqueeze(ax))


class T:
    def __init__(self, h, name):
        self.h = h
        self.name = name
        self.lw = None
        self.rd = []

    def __getitem__(self, k):
        return Vw(self, self.h[k])


def _tl(*xs):
    return [x.t for x in xs if isinstance(x, Vw)]


def _ap(x):
    return x.ap if isinstance(x, Vw) else x


class Prog:
    def __init__(self, nc, es):
        self.nc = nc
        self.es = es
        self.ops = {e: [] for e in ("tensor", "vector", "scalar", "gpsimd", "sync")}
        self.cnt = {e: 0 for e in self.ops}
        self.sem = {e: es.enter_context(nc.semaphore("s_" + e)) for e in self.ops}
        self.dsem = [es.enter_context(nc.semaphore("d%d" % i)) for i in range(NDSEM)]
        self.duse = [0] * NDSEM
        self.dpool = {"sync": list(range(0, 16)), "scalar": list(range(16, NDSEM))}
        self.dnext = {"sync": 0, "scalar": 0}
        self.seen = {e: {} for e in self.ops}
        self.nt = 0
        self.pending = []

    def sb(self, shape, dt, name=None):
        self.nt += 1
        name = (name or "t") + "_%d" % self.nt
        return T(self.es.enter_context(self.nc.sbuf_tensor(name, list(shape), dt)), name)

    def ps(self, shape, dt, name=None):
        self.nt += 1
        name = (name or "p") + "_%d" % self.nt
        return T(self.es.enter_context(self.nc.psum_tensor(name, list(shape), dt)), name)

    def dram(self, name, shape, dt, kind):
        return T(self.nc.dram_tensor(name, list(shape), dt, kind=kind).ap(), name)

    def _deps(self, eng, reads, writes):
        toks = []
        for t in reads:
            if t.lw is not None:
                toks.append(t.lw)
        for t in writes:
            if t.lw is not None:
                toks.append(t.lw)
            toks.extend(t.rd)
        out = []
        for tk in toks:
            (semkey, val, teng) = tk
            if teng == "tensor" and eng == "tensor":
                continue
            if self.seen[eng].get(semkey, 0) >= val:
                continue
            self.seen[eng][semkey] = val
            out.append(tk)
        return out

    def _commit(self, tok, reads, writes):
        for t in writes:
            t.lw = tok
            t.rd = []
        for t in reads:
            if t not in writes:
                t.rd.append(tok)
                if len(t.rd) > 12:
                    best = {}
                    for k in t.rd:
                        if k[0] not in best or best[k[0]][1] < k[1]:
                            best[k[0]] = k
                    t.rd = list(best.values())

    def op(self, eng, fn, reads=(), writes=()):
        reads = list(dict.fromkeys(reads))
        writes = list(dict.fromkeys(writes))
        waits = self._deps(eng, reads, writes)
        self.cnt[eng] += 1
        tok = (("e", eng), self.cnt[eng], eng)
        self.ops[eng].append((fn, waits, ("e", eng), 1))
        self._commit(tok, reads, writes)
        return tok

    def flush(self):
        pend, self.pending = self.pending, []
        for (o, i) in pend:
            self.dma(o, i, q="sync", _flush=False)

    def dma(self, out, in_, q="sync", _flush=True):
        if q == "store":
            self.pending.append((out, in_))
            return None
        pool = self.dpool[q]
        k = pool[self.dnext[q] % len(pool)]
        self.dnext[q] += 1
        waits = self._deps(q, [in_.t], [out.t])
        prev = self.duse[k] * 16
        if prev > 0 and self.seen[q].get(("d", k), 0) < prev:
            self.seen[q][("d", k)] = prev
            waits.append((("d", k), prev, "dma"))
        self.duse[k] += 1
        tok = (("d", k), self.duse[k] * 16, "dma")
        oa, ia = out.ap, in_.ap
        self.ops[q].append((lambda e: e.dma_start(out=oa, in_=ia), waits, ("d", k), 16))
        self._commit(tok, [in_.t], [out.t])
        if _flush:
            self.flush()
        return tok

    def barrier(self):
        self.flush()
        latest = []
        for e in self.ops:
            if self.cnt[e] > 0:
                latest.append((("e", e), self.cnt[e], e))
        for k in range(NDSEM):
            if self.duse[k] > 0:
                latest.append((("d", k), self.duse[k] * 16, "dma"))
        for e in self.ops:
            waits = []
            for tk in latest:
                if self.seen[e].get(tk[0], 0) >= tk[1]:
                    continue
                self.seen[e][tk[0]] = tk[1]
                waits.append(tk)
            self.ops[e].append((None, waits, None, 0))

    def _semh(self, key):
        return self.sem[key[1]] if key[0] == "e" else self.dsem[key[1]]

    def emit(self, block, final_tokens):
        P = self
        waited = {e: set() for e in P.ops}
        for eng in P.ops:
            for fn, waits, semkey, inc in P.ops[eng]:
                for (sk, val, _) in waits:
                    if sk[0] == "e":
                        waited[sk[1]].add(val)
        for (sk, val, _) in final_tokens:
            if sk[0] == "e":
                waited[sk[1]].add(val)
        sigmap = {}
        for eng in P.ops:
            n = 0
            sig = 0
            m = {}
            for fn, waits, semkey, inc in P.ops[eng]:
                if fn is None or semkey[0] != "e":
                    continue
                n += 1
                if n in waited[eng]:
                    sig += 1
                    m[n] = sig
            sigmap[eng] = m

        def wait(e, sk, val):
            if sk[0] == "e":
                e.wait_ge(P.sem[sk[1]], sigmap[sk[1]][val])
            else:
                e.wait_ge(P.dsem[sk[1]], val)

        def run(engname, e):
            n = 0
            for fn, waits, semkey, inc in P.ops[engname]:
                for (sk, val, _) in waits:
                    wait(e, sk, val)
                if fn is None:
                    continue
                ins = fn(e)
                if semkey[0] == "e":
                    n += 1
                    if n in sigmap[engname]:
                        ins.then_inc(P.sem[engname], 1)
                else:
                    ins.then_inc(P.dsem[semkey[1]], inc)
            if engname == "sync":
                for (sk, val, _) in final_tokens:
                    wait(e, sk, val)

        @block.sync
        def _(e):
            run("sync", e)

        @block.tensor
        def _(e):
            run("tensor", e)

        @block.vector
        def _(e):
            run("vector", e)

        @block.scalar
        def _(e):
            run("scalar", e)

        @block.gpsimd
        def _(e):
            run("gpsimd", e)

    def tt(self, eng, out, a, b, op):
        self.op(eng, lambda e: e.tensor_tensor(out=out.ap, in0=a.ap, in1=b.ap, op=op), _tl(a, b), _tl(out))

    def ts(self, eng, out, a, s1, op0, s2=None, op1=None):
        if op1 is None:
            if eng == "gpsimd":
                fn = lambda e: e.tensor_scalar(out=out.ap, in0=a.ap, scalar1=_ap(s1), scalar2=(0.0 if op0 == ALU.mult else 1.0),
                                               op0=op0, op1=(ALU.add if op0 == ALU.mult else ALU.mult))
            else:
                fn = lambda e: e.tensor_scalar(out=out.ap, in0=a.ap, scalar1=_ap(s1), scalar2=None, op0=op0)
        else:
            fn = lambda e: e.tensor_scalar(out=out.ap, in0=a.ap, scalar1=_ap(s1), scalar2=_ap(s2), op0=op0, op1=op1)
        self.op(eng, fn, _tl(a, s1, s2), _tl(out))

    def stt(self, out, a, s, b, op0, op1):
        self.op("vector", lambda e: e.scalar_tensor_tensor(out=out.ap, in0=a.ap, scalar=_ap(s), in1=b.ap, op0=op0, op1=op1),
                _tl(a, s, b), _tl(out))

    def act(self, out, a, func, scale=None, bias=None, accum=None):
        kw = {}
        if scale is not None:
            kw["scale"] = _ap(scale)
        if bias is not None:
            kw["bias"] = _ap(bias)
        if accum is not None:
            kw["accum_out"] = accum.ap
        self.op("scalar", lambda e: e.activation(out=out.ap, in_=a.ap, func=func, **kw), _tl(a, scale, bias), _tl(out, accum))

    def cp(self, eng, out, a):
        if eng == "scalar":
            self.op(eng, lambda e: e.copy(out=out.ap, in_=a.ap), _tl(a), _tl(out))
        else:
            self.op(eng, lambda e: e.tensor_copy(out=out.ap, in_=a.ap), _tl(a), _tl(out))

    def mm(self, out, l, r, start, stop, tp=None):
        kw = {} if tp is None else {"tile_position": tp}
        self.op("tensor", lambda e: e.matmul(out=out.ap, lhsT=l.ap, rhs=r.ap, start=start, stop=stop, skip_group_check=True, **kw),
                _tl(l, r), _tl(out))

    def tr(self, out, a, ident, tp=None):
        kw = {} if tp is None else {"tile_position": tp}
        self.op("tensor", lambda e: e.transpose(out=out.ap, in_=a.ap, identity=ident.ap, **kw), _tl(a, ident), _tl(out))

    def scan(self, out, d0, d1, init):
        self.op("vector", lambda e: e.tensor_tensor_scan(out=out.ap, data0=d0.ap, data1=d1.ap, initial=_ap(init), op0=ALU.mult, op1=ALU.add),
                _tl(d0, d1, init), _tl(out))

    def recip(self, out, a):
        self.op("vector", lambda e: e.reciprocal(out=out.ap, in_=a.ap), _tl(a), _tl(out))

    def red(self, out, a):
        self.op("vector", lambda e: e.tensor_reduce(out=out.ap, in_=a.ap, axis=AX.X, op=ALU.add), _tl(a), _tl(out))

    def memset(self, eng, out, val):
        self.op(eng, lambda e: e.memset(out.ap, val), [], _tl(out))


def build_program(dbg=False, stop_at=None):
    nc = bass.Bass("TRN2", target_bir_lowering=False)
    es = ExitStack()
    with es:
        P = Prog(nc, es)
        EI, EO, IN = "ExternalInput", "ExternalOutput", "Internal"
        d_xmain = P.dram("x_main", [TC, D], F32, EI)
        d_xpre = P.dram("x_pre", [NPRE * SEG, D], F32, EI)
        d_xhalo = P.dram("x_halo", [128, D], F32, EI)
        d_pos = P.dram("pos_main", [128, TC // 128], I32, EI)
        d_posh = P.dram("pos_halo", [128, 1], I32, EI)
        d_win = P.dram("w_in", [D, IN_W], F32, EI)
        d_wout = P.dram("w_out", [D, D], F32, EI)
        d_wglu = P.dram("w_glu", [1024, 1024], F32, EI)
        d_normw = P.dram("normw", [128, 16], F32, EI)
        d_outnw = P.dram("outnw", [128, 16], F32, EI)
        d_bglu = P.dram("bglu", [128, 8], F32, EI)
        d_dskip = P.dram("dskip", [128, 8], F32, EI)
        d_qw = P.dram("qw", [128, 64], F32, EI)
        d_kw = P.dram("kw", [128, 64], F32, EI)
        d_sinks = P.dram("sinks", [128, 16], F32, EI)
        d_invf = P.dram("invf", [128, 32], F32, EI)
        d_are = P.dram("are", [128, 32], F32, EI)
        d_aim = P.dram("aim", [128, 32], F32, EI)
        d_lst = P.dram("lst", [128, 32], F32, EI)
        d_bre = P.dram("bre", [128, 1024], F32, EI)
        d_bim = P.dram("bim", [128, 1024], F32, EI)
        d_cre = P.dram("cre", [128, 1024], F32, EI)
        d_cim = P.dram("cim", [128, 1024], F32, EI)
        d_ident = P.dram("ident", [128, 128], F32, EI)
        d_maskc = P.dram("maskc", [128, 128], F32, EI)
        d_maskp = P.dram("maskp", [128, 128], F32, EI)
        d_maskf = P.dram("maskf", [128, 128], F32, EI)
        d_kvec = P.dram("kvec", [128, NK], F32, EI)
        d_jrow = P.dram("jrow", [128, NB + 1], F32, EI)
        d_out = P.dram("out", [TC, D], F32, EO)
        s_win = P.dram("s_win", [D, IN_W], BF16, IN)
        s_wout = P.dram("s_wout", [D, D], BF16, IN)
        s_wglu = P.dram("s_wglu", [1024, 1024], BF16, IN)
        s_bs = P.dram("s_bs", [8, 128, 2 * LB_ * 128], BF16, IN)
        s_cs = P.dram("s_cs", [8, 128, 4 * 2 * LB_ * 32], BF16, IN)
        s_fir = P.dram("s_fir", [8, 128, LB_ * 128], BF16, IN)
        s_tab = P.dram("s_tab", [8, 128, 4 * 3 * (NB + 1)], F32, IN)
        dbg_t = {}
        if dbg:
            dbg_t["proj_q"] = P.dram("dbg_q", [128, 1024], F32, EO)

        def load_const(dram_t, shape, dt=F32, name=None):
            t = P.sb(shape, dt, name)
            P.dma(t[:], dram_t[:])
            return t

        identf = load_const(d_ident, [128, 128], name="identf")
        idb = P.sb([128, 128], BF16, "idb")
        P.cp("vector", idb[:], identf[:])
        normw = load_const(d_normw, [128, 16], name="normw")
        outnw = load_const(d_outnw, [128, 16], name="outnw")
        bglu = load_const(d_bglu, [128, 8], name="bglu")
        dskip = load_const(d_dskip, [128, 8], name="dskip")
        qw = load_const(d_qw, [128, 64], name="qw")
        kw = load_const(d_kw, [128, 64], name="kw")
        invf = load_const(d_invf, [128, 32], name="invf")
        sinks = load_const(d_sinks, [128, 16], name="sinks")
        esink = P.sb([128, 16], F32, "esink")
        P.act(esink[:], sinks[:], AF.Exp)
        mk_f = load_const(d_maskc, [128, 128])
        maskc = P.sb([128, 128], BF16, "maskc")
        P.cp("vector", maskc[:], mk_f[:])
        mk_f2 = load_const(d_maskp, [128, 128])
        maskp = P.sb([128, 128], BF16, "maskp")
        P.cp("vector", maskp[:], mk_f2[:])
        mk_f3 = load_const(d_maskf, [128, 128])
        maskf = P.sb([128, 128], BF16, "maskf")
        P.cp("vector", maskf[:], mk_f3[:])
        mask2 = P.sb([128, 2, 2, 128], BF16, "mask2")
        mask2f = P.sb([128, 2, 2, 128], BF16, "mask2f")
        for jj_ in range(2):
            P.cp("vector", mask2[:, 0, jj_, :], mk_f2[:])
            P.cp("vector", mask2[:, 1, jj_, :], mk_f[:])
            P.cp("vector", mask2f[:, 0, jj_, :], mk_f3[:])
            P.cp("vector", mask2f[:, 1, jj_, :], mk_f[:])
        posi = load_const(d_pos, [128, TC // 128], I32)
        posf = P.sb([128, TC // 128], F32, "posf")
        P.cp("vector", posf[:], posi[:])
        poshi = load_const(d_posh, [128, 1], I32)
        poshf = P.sb([128, 1], F32, "poshf")
        P.cp("vector", poshf[:], poshi[:])
        ones_b = P.sb([128, 1], BF16, "ones_b")
        P.memset("vector", ones_b[:], 1.0)
        NTL = TC // 128 + 1
        cos_all = P.sb([128, NTL, 32], F32, "cos_all")
        sin_all = P.sb([128, NTL, 32], F32, "sin_all")
        r8 = P.sb([128, 32], F32, "r8")
        c512 = P.sb([128, 32], F32, "c512")
        s512 = P.sb([128, 32], F32, "s512")
        g0re = P.sb([128, 32], F32, "g0re")
        g0im = P.sb([128, 32], F32, "g0im")
        P.memset("vector", g0re[:], 0.0)
        P.memset("vector", g0im[:], 0.0)

        pA = P.ps([128, 512], F32, "pA")
        pB = P.ps([128, 512], F32, "pB")
        pT = P.ps([128, 1024], BF16, "pT")
        pT2 = P.ps([128, 1024], BF16, "pT2")
        pS = P.ps([128, 512], F32, "pS")
        pO = P.ps([128, 512], F32, "pO")
        pY = P.ps([128, 512], F32, "pY")
        pM = P.ps([128, 512], F32, "pM")

        def phasor(f, cos_out, sin_out, n, scope_es):
            old = P.es
            P.es = scope_es
            fi = P.sb([128, n], I32)
            ff = P.sb([128, n], F32)
            f1 = P.sb([128, n], F32)
            f2 = P.sb([128, n], F32)
            P.es = old
            P.cp("vector", fi[:], f)
            P.cp("vector", ff[:], fi[:])
            P.tt("vector", f1[:], f, ff[:], ALU.subtract)
            P.act(sin_out, f1[:], AF.Sin, scale=S2PI)
            P.ts("vector", f2[:], f1[:], 0.25, ALU.add)
            P.cp("vector", fi[:], f2[:])
            P.cp("vector", ff[:], fi[:])
            P.tt("vector", f1[:], f2[:], ff[:], ALU.subtract)
            P.act(cos_out, f1[:], AF.Sin, scale=S2PI)

        class _Stop(Exception):
            pass

        def stage(name):
            if stop_at == name:
                raise _Stop()

        try:
            stage('wconv')
            with ExitStack() as es2:
                P.es = es2
                are = load_const(d_are, [128, 32])
                aim = load_const(d_aim, [128, 32])
                lst = load_const(d_lst, [128, 32])
                kvec = load_const(d_kvec, [128, NK])
                jrow = load_const(d_jrow, [128, NB + 1])
                dl = P.sb([128, 32], F32)
                P.act(dl[:], lst[:], AF.Exp)
                ard = P.sb([128, 32], F32)
                P.tt("vector", ard[:], are[:], dl[:], ALU.mult)
                th = P.sb([128, 32], F32)
                P.stt(th[:], aim[:], 1.0 / (2.0 * math.pi), dl[:], ALU.mult, ALU.mult)
                fk = P.sb([128, NK, 32], F32)
                ek = P.sb([128, NK, 32], F32)
                kb = kvec[:].us(2).bc([128, NK, 32])
                P.tt("vector", fk[:], th[:].us(1).bc([128, NK, 32]), kb, ALU.mult)
                P.tt("vector", ek[:], ard[:].us(1).bc([128, NK, 32]), kb, ALU.mult)
                rk = P.sb([128, NK, 32], F32)
                fl = lambda t: t[:].r("p a b -> p (a b)")
                P.act(fl(rk), fl(ek), AF.Exp)
                ck = P.sb([128, NK, 32], F32)
                sk = P.sb([128, NK, 32], F32)
                phasor(fl(fk), fl(ck), fl(sk), NK * 32, es2)
                Lr = P.sb([128, NK, 32], F32)
                Li = P.sb([128, NK, 32], F32)
                P.tt("vector", Lr[:], rk[:], ck[:], ALU.mult)
                P.tt("vector", Li[:], rk[:], sk[:], ALU.mult)
                P.cp("vector", r8[:], rk[:, LB_, :])
                P.cp("vector", c512[:], ck[:, LB_ + 1, :])
                P.cp("vector", s512[:], sk[:, LB_ + 1, :])
                nre = P.sb([128, 32], F32)
                P.ts("vector", nre[:], Lr[:, 1, :], -1.0, ALU.add)
                nim = Li[:, 1, :]
                t1 = P.sb([128, 32], F32)
                t2 = P.sb([128, 32], F32)
                den = P.sb([128, 32], F32)
                P.tt("vector", t1[:], are[:], are[:], ALU.mult)
                P.tt("vector", t2[:], aim[:], aim[:], ALU.mult)
                P.tt("vector", den[:], t1[:], t2[:], ALU.add)
                rden = P.sb([128, 32], F32)
                P.recip(rden[:], den[:])
                qre = P.sb([128, 32], F32)
                qim = P.sb([128, 32], F32)
                P.tt("vector", t1[:], nre[:], are[:], ALU.mult)
                P.tt("vector", t2[:], nim, aim[:], ALU.mult)
                P.tt("vector", t1[:], t1[:], t2[:], ALU.add)
                P.tt("vector", qre[:], t1[:], rden[:], ALU.mult)
                P.tt("vector", t1[:], nim, are[:], ALU.mult)
                P.tt("vector", t2[:], nre[:], aim[:], ALU.mult)
                P.tt("vector", t1[:], t1[:], t2[:], ALU.subtract)
                P.tt("vector", qim[:], t1[:], rden[:], ALU.mult)

                with ExitStack() as es3:
                    P.es = es3
                    posall = P.sb([128, NTL], F32)
                    P.cp("vector", posall[:, 0:1], poshf[:])
                    P.cp("vector", posall[:, 1:NTL], posf[:])
                    fro = P.sb([128, NTL, 32], F32)
                    P.tt("vector", fro[:], posall[:].us(2).bc([128, NTL, 32]), invf[:].us(1).bc([128, NTL, 32]), ALU.mult)
                    phasor(fro[:].r("p a b -> p (a b)"), cos_all[:].r("p a b -> p (a b)"), sin_all[:].r("p a b -> p (a b)"), NTL * 32, es3)
                    P.es = es2
                    P.barrier()
                P.es = es2
                with ExitStack() as es3:
                    P.es = es3
                    NJ = NB + 1
                    fj = P.sb([128, 32, NJ], F32)
                    P.tt("vector", fj[:], fk[:, LB_, :].us(2).bc([128, 32, NJ]), jrow[:].us(1).bc([128, 32, NJ]), ALU.mult)
                    tab = P.sb([128, 3, 32, NJ], F32, "tab")
                    phasor(fj[:].r("p a b -> p (a b)"), tab[:, 0].r("p a b -> p (a b)"), tab[:, 1].r("p a b -> p (a b)"), 32 * NJ, es3)
                    P.memset("vector", tab[:, 2].r("p a b -> p (a b)"), 0.0)
                    P.cp("vector", tab[:, 2, :, 1:NJ], r8[:].us(2).bc([128, 32, NB]))
                    for ft in range(8):
                        P.dma(s_tab[ft].r("p (c a b) -> p c a b", c=3, a=4), tab[:, :, 4 * ft:4 * ft + 4, :], q="sync")
                    P.es = es2
                    P.barrier()
                P.es = es2
                P.barrier()

                NSTG = 4
                stg = [P.sb([128, 1536], F32, "stg") for _ in range(NSTG)]
                stb = [P.sb([128, 1536], BF16, "stb") for _ in range(NSTG)]

                def wjob(src_t, dst_t, r, c0, n, scal):
                    return (src_t[r * 128:(r + 1) * 128, c0:c0 + n], dst_t[r * 128:(r + 1) * 128, c0:c0 + n], scal, n)

                jobs = [wjob(d_win, s_win, dk, c * 1536, 1536, normw[:, dk:dk + 1]) for dk in range(16) for c in (1, 2, 0)]
                jobs += [wjob(d_wglu, s_wglu, ft, 0, 1024, None) for ft in range(8)]
                jobs += [wjob(d_wout, s_wout, ft, c * 1024, 1024, outnw[:, ft:ft + 1]) for ft in range(16) for c in range(2)]
                nld = 0
                for i in range(len(jobs)):
                    while nld < len(jobs) and nld < i + NSTG:
                        P.dma(stg[nld % NSTG][:, 0:jobs[nld][3]], jobs[nld][0], q="scalar")
                        nld += 1
                    src, dst, scal, n = jobs[i]
                    if scal is None:
                        P.cp("scalar", stb[i % NSTG][:, 0:n], stg[i % NSTG][:, 0:n])
                    else:
                        P.act(stb[i % NSTG][:, 0:n], stg[i % NSTG][:, 0:n], AF.Copy, scale=scal)
                    P.dma(dst, stb[i % NSTG][:, 0:n], q="scalar")
                bre = load_const(d_bre, [128, 1024])
                bim = load_const(d_bim, [128, 1024])
                cre = load_const(d_cre, [128, 1024])
                cim = load_const(d_cim, [128, 1024])
                cm_tmps = {"vector": [P.sb([128, 32, 32], F32) for _ in range(4)]}

                def cmul(ore, oim, are_, aim_, sre, sim, neg_im=False, eng="vector"):
                    sr = sre.us(2).bc([128, 32, 32])
                    si = sim.us(2).bc([128, 32, 32])
                    u1, u2, u3, u4 = cm_tmps[eng]
                    P.tt(eng, u1[:], are_, sr, ALU.mult)
                    P.tt(eng, u2[:], aim_, si, ALU.mult)
                    P.tt(eng, ore, u1[:], u2[:], ALU.subtract)
                    P.tt(eng, u3[:], aim_, sr, ALU.mult)
                    P.tt(eng, u4[:], are_, si, ALU.mult)
                    if neg_im:
                        P.tt(eng, u3[:], u3[:], u4[:], ALU.add)
                        P.ts(eng, oim, u3[:], -1.0, ALU.mult)
                    else:
                        P.tt(eng, oim, u3[:], u4[:], ALU.add)

                b3 = lambda t: t[:].r("p (a b) -> p a b", b=32)
                Bbre = P.sb([128, 32, 32], F32)
                Bbim = P.sb([128, 32, 32], F32)
                cmul(Bbre[:], Bbim[:], b3(bre), b3(bim), qre[:], qim[:])
                LB = P.sb([128, LB_, 2, 32, 32], BF16, "LB")
                CS = P.sb([128, LB_, 2, 32, 32], BF16, "CS")
                cbf = P.sb([128, 2, 32, 32], BF16, "cbf")
                P.cp("vector", cbf[:, 0], b3(cre))
                P.ts("vector", cbf[:, 1], b3(cim), -1.0, ALU.mult)
                for k in range(LB_):
                    cmul(LB[:, k, 0], LB[:, k, 1], Bbre[:], Bbim[:], Lr[:, k, :], Li[:, k, :])
                for i in range(LB_):
                    cmul(CS[:, i, 0], CS[:, i, 1], b3(cre), b3(cim), Lr[:, i + 1, :], Li[:, i + 1, :], neg_im=True,
                         eng="vector")
                for ft in range(8):
                    for k in range(4):
                        P.dma(s_cs[ft][:, k * 64 * LB_:(k + 1) * 64 * LB_].r("p (i r h) -> p i r h", i=LB_, r=2),
                              CS[:, :, :, 4 * ft + k, :], q="sync")
                fir = P.sb([128, 8, LB_, 128], BF16, "fir")
                P.memset("vector", fir[:].r("p a b c -> p (a b c)"), 0.0)
                bsb = [P.sb([128, 2 * LB_, 128], BF16, "bsb") for _ in range(2)]
                pTs = [pT, pT2]
                for ft in range(8):
                    for hf in range((2 * LB_ + 7) // 8):
                        pt = pTs[hf % 2]
                        pt3 = pt[:].r("p (a b) -> p a b", b=128)
                        nsl = min(8, 2 * LB_ - hf * 8)
                        for k in range(4):
                            gp = 4 * ft + k
                            for il in range(nsl // 2):
                                i = hf * 4 + il
                                for ri in range(2):
                                    P.tr(pt3[32 * k:32 * k + 32, il * 2 + ri, :], LB[:, LB_ - 1 - i, ri, gp, :], idb[:], tp=(0, 32 * k))
                        P.cp("vector" if hf == 0 else "scalar", bsb[ft % 2][:, hf * 8:hf * 8 + nsl, :].r("p a b -> p (a b)"), pt[:, 0:nsl * 128])
                    P.dma(s_bs[ft], bsb[ft % 2][:].r("p a b -> p (a b)"), q="sync")
                    pk3 = pM[:, 0:32 * LB_].r("p (d h) -> p d h", h=32)
                    for k in range(4):
                        gp = 4 * ft + k
                        for d in range(LB_):
                            P.mm(pk3[32 * k:32 * k + 32, d, :], LB[:, d, 0, gp, :], cbf[:, 0, gp, :], start=(d == 0), stop=False, tp=(0, 32 * k))
                            P.mm(pk3[32 * k:32 * k + 32, d, :], LB[:, d, 1, gp, :], cbf[:, 1, gp, :], start=False, stop=True, tp=(0, 32 * k))
                    for k in range(4):
                        P.cp("vector", fir[32 * k:32 * k + 32, ft, :, 32 * k:32 * k + 32], pk3[32 * k:32 * k + 32, :, :])
                    P.dma(s_fir[ft], fir[:, ft].r("p a b -> p (a b)"), q="sync")
                P.es = es
                P.barrier()
            P.es = es
            P.barrier()

            xt = [P.sb([128, D], F32, "xt") for _ in range(1)]
            xn = P.sb([128, D], BF16, "xn")
            xnTs = [P.sb([128, 8, SEG], BF16, "xnT") for _ in range(2)]
            wbuf = [P.sb([128, 16, WCB], BF16, "wbuf") for _ in range(2)]
            uT = P.sb([128, 8, SEG], BF16, "uT")
            szsT = P.sb([128, 8, SEG], BF16, "szsT")
            ygT = P.sb([128, 8, SEG], BF16, "ygT")
            maT = P.sb([128, 8, SEG], BF16, "maT")
            q_sb = [P.sb([128, 1024], BF16, "q_sb") for _ in range(NT)]
            kv_sb = [P.sb([128, 512], BF16, "kv_sb") for _ in range(NT)]
            sz_sb = [P.sb([128, 1024], BF16, "sz_sb") for _ in range(NT)]
            rstd_a = P.sb([128, NT], F32, "rstd_a")
            rstd_s = P.sb([128, NT], F32, "rstd_s")
            kT = [P.sb([128, 4, 128], BF16, "kT") for _ in range(2)]
            vext = [P.sb([128, 4, 65], BF16, "vext") for _ in range(2)]
            for v_ in vext:
                P.memset("vector", v_[:].r("p a b -> p (a b)"), 1.0)
            bs_t = [P.sb([128, 2 * LB_, 128], BF16, "bs_t") for _ in range(2)]
            cs_t = [P.sb([128, 4, LB_, 2, 32], BF16, "cs_t") for _ in range(2)]
            fir_t = [P.sb([128, LB_, 128], BF16, "fir_t") for _ in range(2)]
            NTAB = 3
            tab_t = [P.sb([128, 3, 4, NB + 1], F32, "tab_t") for _ in range(NTAB)]
            wctr = [0]
            sctr = [0]
            xctr = [0]
            pctr = [0]

            def next_wbuf():
                w = wbuf[wctr[0] % 2]
                wctr[0] += 1
                return w

            def mm_bank():
                b = (pA, pB)[pctr[0] % 2]
                pctr[0] += 1
                return b

            def small(shape, dt=F32, name="sm"):
                return P.sb(shape, dt, name)

            ss = small([128, 1]); ms = small([128, 1]); sd = small([128, 1]); rstd = small([128, 1])
            mhalf = small([128, 1])
            P.memset("vector", mhalf[:], -0.5)

            def norm_stage1(src_rows):
                x_ = xt[0]
                xctr[0] += 1
                P.dma(x_[:], src_rows)
                P.act(xn[:], x_[:], AF.Square, accum=ss[:])
                P.ts("gpsimd", ms[:], ss[:], 1.0 / D, ALU.mult, 1e-6, ALU.add)
                P.tt("gpsimd", rstd[:], ms[:], mhalf[:], ALU.pow)
                P.act(xn[:], x_[:], AF.Copy, scale=rstd[:])

            def norm_transpose(src_rows, tt_slot):
                norm_stage1(src_rows)
                norm_stage2(tt_slot)

            def norm_stage2(tt_slot):
                for hf in range(2):
                    pt = (pT, pT2)[hf]
                    for j in range(8):
                        dk = hf * 8 + j
                        P.tr(pt[:, j * 128:(j + 1) * 128], xn[:, dk * 128:(dk + 1) * 128], idb[:])
                    P.cp("vector" if hf == 0 else "scalar",
                         xnTs[hf][:, :, tt_slot * 128:(tt_slot + 1) * 128],
                         pt[:].r("p (a b) -> p a b", b=128))

            def load_w(scr, col0, nk, ncol=WCB, row0=0):
                w = next_wbuf()
                src = scr[row0:row0 + nk * 128, col0:col0 + ncol].r("(k p) c -> p k c", p=128)
                P.dma(w[:, 0:nk, 0:ncol], src)
                return w

            qn_t = small([128, 1024]); sq_t = small([128, 1024]); ssq = small([128, 16]); rq = small([128, 16])
            ta = small([128, 512]); tb = small([128, 512]); tc_ = small([128, 512]); td = small([128, 512])
            qr = small([128, 1024], BF16); kr = small([128, 256], BF16); kdup = small([128, 4, 128], BF16)
            qT = small([128, 8, 128], BF16)
            rope_sel = [0]
            pe2 = [small([128, 1024], BF16) for _ in range(2)]; pm2 = [small([128, 1024], BF16) for _ in range(2)]
            den4 = small([128, 4]); rden4 = small([128, 4]); o_t = sq_t; oa_t = small([128, 1024], BF16)
            osq = qr; ssa = small([128, 1]);

            def qk_norm_rope(src, nh, w_t, out_bf):
                n = nh * 64
                s3 = lambda v: v.r("p (h d) -> p h d", d=64)
                P.tt("vector", sq_t[:, 0:n], src, src, ALU.mult)
                P.red(ssq[:, 0:nh], s3(sq_t[:, 0:n]))
                P.ts("vector", ssq[:, 0:nh], ssq[:, 0:nh], 1.0 / 64, ALU.mult, 1e-6, ALU.add)
                P.tt("gpsimd", rq[:, 0:nh], ssq[:, 0:nh], mhalf[:].bc([128, nh]), ALU.pow)
                P.tt("vector", s3(qn_t[:, 0:n]), s3(src), rq[:, 0:nh].us(2).bc([128, nh, 64]), ALU.mult)
                P.tt("vector", s3(qn_t[:, 0:n]), s3(qn_t[:, 0:n]), w_t[:].us(1).bc([128, nh, 64]), ALU.mult)
                q4 = qn_t[:, 0:n].r("p (h t d) -> p h t d", t=2, d=32)
                o4 = out_bf.r("p (h t d) -> p h t d", t=2, d=32)
                cb_ = cos_all[:, rope_sel[0], :].us(1).bc([128, nh, 32])
                sb_ = sin_all[:, rope_sel[0], :].us(1).bc([128, nh, 32])
                m = nh * 32
                v3 = lambda t: t[:, 0:m].r("p (h d) -> p h d", d=32)
                P.tt("vector", v3(ta), q4[:, :, 0, :], cb_, ALU.mult)
                P.tt("vector", v3(tb), q4[:, :, 1, :], sb_, ALU.mult)
                P.tt("vector", o4[:, :, 0, :], v3(ta), v3(tb), ALU.subtract)
                P.tt("vector", v3(tc_), q4[:, :, 1, :], cb_, ALU.mult)
                P.tt("vector", v3(td), q4[:, :, 0, :], sb_, ALU.mult)
                P.tt("vector", o4[:, :, 1, :], v3(tc_), v3(td), ALU.add)

            if True:

                _ph = {}

                def phasor_small(f, cos_out, sin_out):
                    if "fi" not in _ph:
                        _ph["fi"] = P.sb([128, 32], I32); _ph["ff"] = P.sb([128, 32], F32)
                        _ph["f1"] = P.sb([128, 32], F32); _ph["f2"] = P.sb([128, 32], F32)
                    fi, ff, f1, f2 = _ph["fi"], _ph["ff"], _ph["f1"], _ph["f2"]
                    P.cp("vector", fi[:], f)
                    P.cp("vector", ff[:], fi[:])
                    P.tt("vector", f1[:], f, ff[:], ALU.subtract)
                    P.act(sin_out, f1[:], AF.Sin, scale=S2PI)
                    P.ts("vector", f2[:], f1[:], 0.25, ALU.add)
                    P.cp("vector", fi[:], f2[:])
                    P.cp("vector", ff[:], fi[:])
                    P.tt("vector", f1[:], f2[:], ff[:], ALU.subtract)
                    P.act(cos_out, f1[:], AF.Sin, scale=S2PI)

                def rope_tables2(pos_col):
                    pass

                def prep_kv(kv, gtt, slot):
                    rope_sel[0] = gtt + 1
                    qk_norm_rope(kv[:, 0:256], 4, kw, kr[:])
                    k3 = kr[:].r("p (g d) -> p g d", d=64)
                    P.cp("vector", kdup[:, :, 0:64], k3)
                    P.cp("gpsimd", kdup[:, :, 64:128], k3)
                    for g in range(4):
                        P.tr(pT[:, g * 128:(g + 1) * 128], kdup[:, g, :], idb[:])
                    P.cp("scalar", kT[slot][:].r("p a b -> p (a b)"), pT[:, 0:512])
                    P.cp("vector", vext[slot][:, :, 0:64], kv[:, 256:512].r("p (g d) -> p g d", d=64))

                def attention_tile(tt, slot_cur, slot_prev, first):
                    qk_norm_rope(q_sb[tt][:], 16, qw, qr[:])
                    stage('qrope')
                    for j in range(8):
                        P.tr(pT2[:, j * 128:(j + 1) * 128], qr[:, j * 128:(j + 1) * 128], idb[:])
                    P.cp("scalar", qT[:].r("p a b -> p (a b)"), pT2[:])
                    stage('qT')
                    slots = (slot_prev, slot_cur)
                    mk2 = mask2f if first else mask2
                    sets = ((pS, pM), (pA, pB))

                    def scores(g):
                        bE, bO = sets[g % 2]
                        for kb in range(2):
                            for j in range(4):
                                h = 4 * g + j
                                base = 64 * (h % 2)
                                bank = bE if (h % 2 == 0) else bO
                                c0 = (kb * 2 + j // 2) * 128
                                P.mm(bank[:, c0:c0 + 128], kT[slots[kb]][base:base + 64, g, :], qT[base:base + 64, h // 2, :],
                                     start=True, stop=True)

                    scores(0)
                    for g in range(4):
                        bE, bO = sets[g % 2]
                        pe_ = pe2[g % 2]
                        pm_ = pm2[g % 2]
                        oB = (pO, pY)[g % 2]
                        for bi, bank in enumerate((bE, bO)):
                            P.act(pe_[:, bi * 512:(bi + 1) * 512], bank[:], AF.Exp, scale=0.125)
                        for bi in range(2):
                            P.tt("vector", pm_[:, bi * 512:(bi + 1) * 512], pe_[:, bi * 512:(bi + 1) * 512],
                                 mk2[:].r("p k j q -> p (k j q)"), ALU.mult)
                        if g + 1 < 4:
                            scores(g + 1)
                        n_ = 0
                        for j in range(4):
                            for kb in range(2):
                                c0 = (j % 2) * 512 + (kb * 2 + j // 2) * 128
                                P.mm(oB[:, j * 65:(j + 1) * 65], pm_[:, c0:c0 + 128], vext[slots[kb]][:, g, :],
                                     start=(n_ == 0), stop=(n_ == 7))
                                n_ += 1
                        o3 = oB[:, 0:260].r("p (j d) -> p j d", d=65)
                        P.tt("vector", den4[:], o3[:, :, 64], esink[:, 4 * g:4 * g + 4], ALU.add)
                        P.recip(rden4[:], den4[:])
                        P.tt("vector", o_t[:, g * 256:(g + 1) * 256].r("p (j d) -> p j d", d=64), o3[:, :, 0:64],
                             rden4[:].us(2).bc([128, 4, 64]), ALU.mult)
                    stage('onorm')
                    P.tt("vector", oa_t[:], o_t[:], sz_sb[tt][:], ALU.mult)
                    P.act(osq[:], oa_t[:], AF.Square, accum=ssa[:])
                    P.ts("gpsimd", ssa[:], ssa[:], 1.0 / 1024, ALU.mult, 1e-6, ALU.add)
                    P.tt("gpsimd", rstd_a[:, tt:tt + 1], ssa[:], mhalf[:], ALU.pow)
                    for j in range(8):
                        P.tr(pT[:, j * 128:(j + 1) * 128], oa_t[:, j * 128:(j + 1) * 128], idb[:])
                    P.cp("scalar", maT[:, :, tt * 128:(tt + 1) * 128], pT[:].r("p (a b) -> p a b", b=128))

                NBX = NB + 1
                Sft = [small([128, 4, 2 * NB]) for _ in range(2)]
                Xre = [small([128, 4, NBX])] * 2
                Xim = [small([128, 4, NBX])] * 2
                Gre = [small([128, 4, NBX])] * 2
                Gim = [small([128, 4, NBX])] * 2
                hre = [small([128, 4, NB], BF16) for _ in range(2)]
                him = [small([128, 4, NB], BF16) for _ in range(2)]
                m1 = small([128, 4, NB]); m2 = small([128, 4, NB]); m3 = small([128, 4, NB]); m4 = small([128, 4, NB])
                ysb = small([128, SEG]); y2 = small([128, SEG]); y3 = y2; sg = small([128, SEG]); sg2 = small([128, SEG])
                cn1 = small([128, 4]); cn2 = small([128, 4]); cn3 = small([128, 4]); cn4 = small([128, 4])
                fctr = [0]

                def load_blob(ft, full):
                    s = sctr[0] % 2
                    sctr[0] += 1
                    P.dma(bs_t[s][:].r("p a b -> p (a b)"), s_bs[ft])
                    P.dma(tab_t[(sctr[0] - 1) % NTAB][:].r("p c a b -> p (c a b)"), s_tab[ft])
                    if full:
                        P.dma(cs_t[s][:].r("p k i r h -> p (k i r h)"), s_cs[ft])
                        P.dma(fir_t[s][:].r("p a b -> p (a b)"), s_fir[ft])
                    return s

                def ssm_ft(ft, full):
                    ssm_back(ssm_front(ft, full))

                def ssm_front(ft, full):
                    s = load_blob(ft, full)
                    z = fctr[0] % 2
                    fctr[0] += 1
                    yb = (pY, pM)[z]
                    u3 = uT[:, ft, :].r("p (j i) -> p j i", i=LB_)
                    y3v = yb[:].r("p (j i) -> p j i", i=LB_)
                    if full:
                        for d in range(LB_):
                            for i in range(d, LB_):
                                P.mm(y3v[:, :, i], fir_t[s][:, d, :], u3[:, :, i - d], start=(d == 0 and i == 0), stop=False)
                    sbanks = (pO, pS, pA, pB)
                    for ri in range(2):
                        for i in range(LB_):
                            for k in range(4):
                                base = 32 * k
                                P.mm(sbanks[k][:, 256 + ri * NB:256 + (ri + 1) * NB], bs_t[s][base:base + 32, i * 2 + ri, :],
                                     u3[base:base + 32, :, i], start=(i == 0), stop=(i == LB_ - 1), tp=(base, 0))
                    for k in range(4):
                        P.cp("scalar", Sft[z][:, k, :], sbanks[k][:, 256:256 + 2 * NB])
                    return (ft, full, s, z, tab_t[(sctr[0] - 1) % NTAB])

                def ssm_back(st_):
                    ft, full, s, z, tb_ = st_
                    yb = (pY, pM)[z]
                    y3v = yb[:].r("p (j i) -> p j i", i=LB_)
                    g4 = slice(4 * ft, 4 * ft + 4)
                    sre_p = Sft[z][:, :, 0:NB]
                    sim_p = Sft[z][:, :, NB:2 * NB]
                    cM = tb_[:, 0, :, 1:NB + 1]
                    sM = tb_[:, 1, :, 1:NB + 1]
                    cD = tb_[:, 0, :, 0:NB]
                    sD = tb_[:, 1, :, 0:NB]
                    P.tt("vector", m1[:], sre_p, cM, ALU.mult)
                    P.tt("vector", m2[:], sim_p, sM, ALU.mult)
                    P.tt("vector", m3[:], sim_p, cM, ALU.mult)
                    P.tt("vector", m4[:], sre_p, sM, ALU.mult)
                    P.cp("vector", Xre[z][:, :, 0], g0re[:, g4])
                    P.cp("vector", Xim[z][:, :, 0], g0im[:, g4])
                    P.tt("vector", Xre[z][:, :, 1:NBX], m1[:], m2[:], ALU.add)
                    P.tt("vector", Xim[z][:, :, 1:NBX], m3[:], m4[:], ALU.subtract)
                    fl = lambda v: v.r("p a b -> p (a b)")
                    cf = fl(tb_[:, 2, :, :])
                    P.scan(fl(Gre[z][:]), cf, fl(Xre[z][:]), 0.0)
                    P.scan(fl(Gim[z][:]), cf, fl(Xim[z][:]), 0.0)
                    P.tt("vector", cn1[:], Gre[z][:, :, NB], c512[:, g4], ALU.mult)
                    P.tt("vector", cn2[:], Gim[z][:, :, NB], s512[:, g4], ALU.mult)
                    P.tt("vector", cn3[:], Gre[z][:, :, NB], s512[:, g4], ALU.mult)
                    P.tt("vector", cn4[:], Gim[z][:, :, NB], c512[:, g4], ALU.mult)
                    P.tt("vector", g0re[:, g4], cn1[:], cn2[:], ALU.subtract)
                    P.tt("vector", g0im[:, g4], cn3[:], cn4[:], ALU.add)
                    if not full:
                        return
                    P.tt("vector", m1[:], Gre[z][:, :, 0:NB], cD, ALU.mult)
                    P.tt("vector", m2[:], Gim[z][:, :, 0:NB], sD, ALU.mult)
                    P.tt("vector", m3[:], Gre[z][:, :, 0:NB], sD, ALU.mult)
                    P.tt("vector", m4[:], Gim[z][:, :, 0:NB], cD, ALU.mult)
                    P.tt("vector", hre[z][:], m1[:], m2[:], ALU.subtract)
                    P.tt("vector", him[z][:], m3[:], m4[:], ALU.add)
                    for i in range(LB_):
                        for k in range(4):
                            base = 32 * k
                            P.mm(y3v[base:base + 32, :, i], cs_t[s][:, k, i, 0, :], hre[z][:, k, :], start=False, stop=False, tp=(0, base))
                        for k in range(4):
                            base = 32 * k
                            P.mm(y3v[base:base + 32, :, i], cs_t[s][:, k, i, 1, :], him[z][:, k, :], start=False, stop=(k == 3 and i == LB_ - 1), tp=(0, base))
                    P.stt(ysb[:], uT[:, ft, :], dskip[:, ft:ft + 1], yb[:], ALU.mult, ALU.add)
                    P.tt("vector", y2[:], ysb[:], ysb[:], ALU.mult)
                    P.ts("vector", y2[:], y2[:], 0.044715, ALU.mult, 1.0, ALU.add)
                    P.tt("vector", y3[:], y2[:], ysb[:], ALU.mult)
                    P.act(sg[:], y3[:], AF.Sigmoid, scale=2.0 * math.sqrt(2.0 / math.pi))
                    P.tt("vector", ygT[:, ft, :], ysb[:], sg[:], ALU.mult)

                def glu_and_gate():
                    for cb in range(4):
                        w = load_w(s_wglu, cb * WCB, 8)
                        for f2 in range(2):
                            fo = cb * 2 + f2
                            bank = mm_bank()
                            for fi_ in range(8):
                                P.mm(bank[:], w[:, fi_, f2 * 128:(f2 + 1) * 128], ygT[:, fi_, :], start=(fi_ == 0), stop=(fi_ == 7))
                            gi_ = (fo % 2) if 'b' in KF else 0
                            sg_ = (sg, sg2)[gi_]
                            ys_ = (ysb, y2)[gi_]
                            oq_ = (pm2[0], pm2[1])[gi_]
                            P.act(sg_[:], bank[:], AF.Sigmoid, bias=bglu[:, fo:fo + 1])
                            P.tt("vector", ys_[:], sg_[:], ygT[:, fo, :], ALU.mult)
                            P.tt("vector", szsT[:, fo, :], ys_[:], szsT[:, fo, :], ALU.mult)
                            P.act(oq_[:, 0:SEG], szsT[:, fo, :], AF.Square)
                            for tt in range(NT):
                                P.mm(pM[:, 256 + tt:256 + tt + 1], oq_[:, tt * 128:(tt + 1) * 128], ones_b[:], start=(fo == 0 and tt == 0),
                                     stop=(fo == 7))
                    P.ts("vector", rstd_s[:], pM[:, 256:256 + NT], 1.0 / 1024, ALU.mult, 1e-6, ALU.add)
                    P.tt("gpsimd", rstd_s[:], rstd_s[:], mhalf[:].bc([128, NT]), ALU.pow)

                xres = [small([128, WCB]) for _ in range(4)]
                ot = [small([128, WCB]) for _ in range(4)]
                octr = [0]

                def out_proj(seg):
                    for cb in range(D // WCB):
                        w = load_w(s_wout, cb * WCB, 16)
                        for tt in range(NT):
                            row0 = seg * SEG + tt * 128
                            b = octr[0] % 4
                            octr[0] += 1
                            P.dma(xres[b][:], d_xmain[row0:row0 + 128, cb * WCB:(cb + 1) * WCB])
                            obanks = (pA, pB, pS, pO, pY, pM) if 'a' in KF else (pA, pB, pA, pB, pA, pB)
                            b1 = obanks[(2 * octr[0]) % 6]
                            for ft in range(8):
                                P.mm(b1[:, 0:WCB], maT[:, ft, tt * 128:(tt + 1) * 128], w[:, ft, :], start=(ft == 0), stop=(ft == 7))
                            b2 = obanks[(2 * octr[0] + 1) % 6]
                            for ft in range(8):
                                P.mm(b2[:, 0:WCB], szsT[:, ft, tt * 128:(tt + 1) * 128], w[:, 8 + ft, :], start=(ft == 0), stop=(ft == 7))
                            P.stt(ot[b][:], b1[:, 0:WCB], rstd_a[:, tt:tt + 1], xres[b][:], ALU.mult, ALU.add)
                            P.stt(ot[b][:], b2[:, 0:WCB], rstd_s[:, tt:tt + 1], ot[b][:], ALU.mult, ALU.add)
                            P.dma(d_out[row0:row0 + 128, cb * WCB:(cb + 1) * WCB], ot[b][:], q="scalar")

                def proj_feature_major(col0, dst, silu):
                    w = load_w(s_win, col0, 16)
                    for f2 in range(2):
                        bank = mm_bank()
                        for dk in range(16):
                            P.mm(bank[:], w[:, dk, f2 * 128:(f2 + 1) * 128], xnTs[dk // 8][:, dk % 8, :], start=(dk == 0), stop=(dk == 15))
                        if silu:
                            P.act(dst[f2], bank[:], AF.Silu)
                        else:
                            P.cp("vector", dst[f2], bank[:])

                def proj_token_major(col0, dst_fn, silu, tts):
                    w = load_w(s_win, col0, 16)
                    for tt in tts:
                        bank = mm_bank()
                        for dk in range(16):
                            P.mm(bank[:, 0:WCB], xnTs[dk // 8][:, dk % 8, tt * 128:(tt + 1) * 128], w[:, dk, :], start=(dk == 0), stop=(dk == 15))
                        if silu:
                            P.act(dst_fn(tt), bank[:, 0:WCB], AF.Silu)
                        else:
                            P.cp("scalar" if tt % 2 else "vector", dst_fn(tt), bank[:, 0:WCB])

                stage('ssmpre')
                def nt_pre(sgi):
                    for tt in range(NT):
                        r0 = sgi * SEG + tt * 128
                        norm_transpose(d_xpre[r0:r0 + 128, :], tt)

                def nt_main(sgi):
                    for tt in range(NT):
                        r0 = sgi * SEG + tt * 128
                        norm_transpose(d_xmain[r0:r0 + 128, :], tt)

                nt_pre(0)
                for sgi in range(NPRE):
                    for c in range(4):
                        proj_feature_major(2560 + c * WCB, [uT[:, 2 * c, :], uT[:, 2 * c + 1, :]], False)
                    if sgi + 1 == NPRE:
                        norm_transpose(d_xhalo[:, :], 0)
                    elif 'd' not in KF:
                        nt_pre(sgi + 1)
                    for ft in range(8):
                        if ft == 0:
                            nxt_ = ssm_front(0, False)
                        cur_ = nxt_
                        if ft + 1 < 8:
                            nxt_ = ssm_front(ft + 1, False)
                        ssm_back(cur_)
                        if sgi + 1 < NPRE and ft % 2 == 0 and 'd' in KF:
                            r0n = (sgi + 1) * SEG + (ft // 2) * 128
                            norm_stage1(d_xpre[r0n:r0n + 128, :])
                        if sgi + 1 < NPRE and ft % 2 == 1 and 'd' in KF:
                            norm_stage2(ft // 2)

                stage('prepass')
                proj_token_major(1024, lambda tt: kv_sb[0][:, 0:256], False, [0])
                proj_token_major(1280, lambda tt: kv_sb[0][:, 256:512], False, [0])
                nt_main(0)
                prep_kv(kv_sb[0], -1, 1)
                cur_slot = 1

                stage('halo')
                for sgi in range(NSEG):
                    allt = list(range(NT))
                    for c in range(4):
                        proj_token_major(c * WCB, lambda tt, c=c: q_sb[tt][:, c * WCB:(c + 1) * WCB], False, allt)
                    proj_token_major(1024, lambda tt: kv_sb[tt][:, 0:256], False, allt)
                    proj_token_major(1280, lambda tt: kv_sb[tt][:, 256:512], False, allt)
                    for c in range(4):
                        proj_token_major(1536 + c * WCB, lambda tt, c=c: sz_sb[tt][:, c * WCB:(c + 1) * WCB], True, allt)
                    for c in range(4):
                        proj_feature_major(2560 + c * WCB, [uT[:, 2 * c, :], uT[:, 2 * c + 1, :]], False)
                    for c in range(4):
                        proj_feature_major(3584 + c * WCB, [szsT[:, 2 * c, :], szsT[:, 2 * c + 1, :]], True)
                    if dbg and sgi == 0:
                        dq = qn_t
                        P.cp("vector", dq[:], q_sb[0][:])
                        P.dma(dbg_t["proj_q"][:], dq[:], q="sync")
                    if sgi + 1 < NSEG and 'c' not in KF:
                        nt_main(sgi + 1)
                    if sgi == 0:
                        stage('proj')
                    for tt in range(NT):
                        gtt = sgi * NT + tt
                        if sgi + 1 < NSEG and 'c' in KF:
                            r0n = (sgi + 1) * SEG + tt * 128
                            norm_stage1(d_xmain[r0n:r0n + 128, :])
                        prev_slot = cur_slot
                        cur_slot = 1 - cur_slot
                        prep_kv(kv_sb[tt], gtt, cur_slot)
                        stage('kv0')
                        attention_tile(tt, cur_slot, prev_slot, first=(gtt == 0))
                        if sgi + 1 < NSEG and 'c' in KF:
                            norm_stage2(tt)
                    if sgi == 0:
                        stage('attn')
                    nxt_ = ssm_front(0, True)
                    for ft in range(8):
                        cur_ = nxt_
                        if ft + 1 < 8:
                            nxt_ = ssm_front(ft + 1, True)
                        ssm_back(cur_)
                    if sgi == 0:
                        stage('ssm')
                    glu_and_gate()
                    if sgi == 0:
                        stage('glu')
                    out_proj(sgi)
                    if sgi == 0:
                        stage('seg0')

        except _Stop:
            P.es = es

        finals = [d_out.lw] + list(d_out.rd)
        for t_ in dbg_t.values():
            finals.append(t_.lw)
        P.barrier()
        with nc.Block() as block:
            P.emit(block, [f for f in finals if f is not None])
    return nc


def _host_inputs(x, positions, norm_w, w_in, q_norm_w, k_norm_w, sinks, a_re, a_im, log_step,
                 b_re, b_im, c_re, c_im, d_skip, w_glu, b_glu, attn_out_norm_w, ssm_out_norm_w, w_out):
    f32 = np.float32
    rep = lambda v: np.ascontiguousarray(np.broadcast_to(np.asarray(v, f32)[None, :], (128, len(v))))
    pl = lambda v, n: np.ascontiguousarray(np.asarray(v, f32).reshape(n, 128).T)

    def p_layout(a):
        a = np.asarray(a, f32).reshape(32, 2, 64)
        return np.ascontiguousarray(a.transpose(1, 2, 0).reshape(128, 32))

    def bc_layout(m):
        m = np.asarray(m, f32).reshape(32, 2, 64, 16)
        o = np.zeros((2, 64, 32, 2, 16), f32)
        for g2 in range(2):
            o[g2, :, :, g2, :] = m[:, g2].transpose(1, 0, 2)
        return np.ascontiguousarray(o.reshape(128, 32 * 32))

    common = {
        "w_in": np.ascontiguousarray(w_in, f32), "w_out": np.ascontiguousarray(w_out, f32), "w_glu": np.ascontiguousarray(w_glu, f32),
        "normw": pl(norm_w, 16), "outnw": pl(np.concatenate([np.asarray(attn_out_norm_w), np.asarray(ssm_out_norm_w)]), 16),
        "bglu": pl(b_glu, 8), "dskip": pl(d_skip, 8),
        "qw": rep(q_norm_w), "kw": rep(k_norm_w), "sinks": rep(sinks),
        "invf": rep((10000.0 ** (-np.arange(0, 64, 2, dtype=np.float64) / 64) / (2 * np.pi)).astype(f32)),
        "are": p_layout(a_re), "aim": p_layout(a_im),
        "lst": p_layout(np.broadcast_to(np.asarray(log_step, f32)[:, None], (64, 64))),
        "bre": bc_layout(b_re), "bim": bc_layout(b_im),
        "cre": bc_layout(np.asarray(c_re).transpose(0, 2, 1)), "cim": bc_layout(np.asarray(c_im).transpose(0, 2, 1)),
        "ident": np.eye(128, dtype=f32),
        "maskc": np.triu(np.ones((128, 128), f32)),
        "maskp": np.tril(np.ones((128, 128), f32), -1),
        "kvec": rep(np.array(KS, f32)), "jrow": rep(np.arange(-1, NB).astype(f32)),
    }
    x = np.asarray(x, f32)
    positions = np.asarray(positions, np.int32)
    maps = []
    for c in range(8):
        b, half = c // 2, c % 2
        t0 = half * TC
        m = dict(common)
        m["x_main"] = np.ascontiguousarray(x[b, t0:t0 + TC])
        pos = positions[b, t0:t0 + TC]
        m["pos_main"] = np.ascontiguousarray(pos.reshape(TC // 128, 128).T)
        if half == 1:
            m["x_pre"] = np.ascontiguousarray(x[b, 0:TC])
            m["x_halo"] = np.ascontiguousarray(x[b, t0 - 128:t0])
            m["pos_halo"] = np.ascontiguousarray(positions[b, t0 - 128:t0].reshape(128, 1))
            m["maskf"] = common["maskp"]
        else:
            m["x_pre"] = np.zeros((TC, D), f32)
            m["x_halo"] = np.zeros((128, D), f32)
            m["pos_halo"] = np.zeros((128, 1), np.int32)
            m["maskf"] = np.zeros((128, 128), f32)
        maps.append(m)
    return maps


_NC = {}


def kernel(**inputs):
    maps = _host_inputs(**inputs)
    if "nc" not in _NC:
        _NC["nc"] = build_program()
    res = run_bass_kernel_spmd(_NC["nc"], maps, core_ids=list(range(8)))
    out = np.empty((4, 4096, D), np.float32)
    for c in range(8):
        b, half = c // 2, c % 2
        out[b, half * TC:(half + 1) * TC] = res.results[c]["out"]
    return out
```

```python
import math
import os
KF = os.environ.get('KFLAGS', 'abcd')
from contextlib import ExitStack
import numpy as np
import concourse.bass as bass
import concourse.mybir as mybir
from concourse.bass_utils import run_bass_kernel_spmd

F32 = mybir.dt.float32
BF16 = mybir.dt.bfloat16
I32 = mybir.dt.int32
AF = mybir.ActivationFunctionType
ALU = mybir.AluOpType
AX = mybir.AxisListType

NDSEM = 24
S2PI = 6.28318
D = 2048
TC = 2048
SEG = 512
NT = SEG // 128
NSEG = TC // SEG
NPRE = 4
LB_ = int(os.environ.get('KBLK', '8'))
NB = SEG // LB_
IN_W = 4608
WCB = 256
KS = list(range(LB_ + 1)) + [SEG]
NK = len(KS)


class Vw:
    def __init__(s, t, ap):
        s.t = t
        s.ap = ap

    def __getitem__(s, k):
        return Vw(s.t, s.ap[k])

    def r(s, pat, **kw):
        return Vw(s.t, s.ap.rearrange(pat, **kw))

    def bc(s, shape):
        return Vw(s.t, s.ap.broadcast_to(list(shape)))

    def us(s, ax):
        return Vw(s.t, s.ap.unsqueeze(ax))


class T:
    def __init__(self, h, name):
        self.h = h
        self.name = name
        self.lw = None
        self.rd = []

    def __getitem__(self, k):
        return Vw(self, self.h[k])


def _tl(*xs):
    return [x.t for x in xs if isinstance(x, Vw)]


def _ap(x):
    return x.ap if isinstance(x, Vw) else x


class Prog:
    def __init__(self, nc, es):
        self.nc = nc
        self.es = es
        self.ops = {e: [] for e in ("tensor", "vector", "scalar", "gpsimd", "sync")}
        self.cnt = {e: 0 for e in self.ops}
        self.sem = {e: es.enter_context(nc.semaphore("s_" + e)) for e in self.ops}
        self.dsem = [es.enter_context(nc.semaphore("d%d" % i)) for i in range(NDSEM)]
        self.duse = [0] * NDSEM
        self.dpool = {"sync": list(range(0, 16)), "scalar": list(range(16, NDSEM))}
        self.dnext = {"sync": 0, "scalar": 0}
        self.seen = {e: {} for e in self.ops}
        self.nt = 0
        self.pending = []

    def sb(self, shape, dt, name=None):
        self.nt += 1
        name = (name or "t") + "_%d" % self.nt
        return T(self.es.enter_context(self.nc.sbuf_tensor(name, list(shape), dt)), name)

    def ps(self, shape, dt, name=None):
        self.nt += 1
        name = (name or "p") + "_%d" % self.nt
        return T(self.es.enter_context(self.nc.psum_tensor(name, list(shape), dt)), name)

    def dram(self, name, shape, dt, kind):
        return T(self.nc.dram_tensor(name, list(shape), dt, kind=kind).ap(), name)

    def _deps(self, eng, reads, writes):
        toks = []
        for t in reads:
            if t.lw is not None:
                toks.append(t.lw)
        for t in writes:
            if t.lw is not None:
                toks.append(t.lw)
            toks.extend(t.rd)
        out = []
        for tk in toks:
            (semkey, val, teng) = tk
            if teng == "tensor" and eng == "tensor":
                continue
            if self.seen[eng].get(semkey, 0) >= val:
                continue
            self.seen[eng][semkey] = val
            out.append(tk)
        return out

    def _commit(self, tok, reads, writes):
        for t in writes:
            t.lw = tok
            t.rd = []
        for t in reads:
            if t not in writes:
                t.rd.append(tok)
                if len(t.rd) > 12:
                    best = {}
                    for k in t.rd:
                        if k[0] not in best or best[k[0]][1] < k[1]:
                            best[k[0]] = k
                    t.rd = list(best.values())

    def op(self, eng, fn, reads=(), writes=()):
        reads = list(dict.fromkeys(reads))
        writes = list(dict.fromkeys(writes))
        waits = self._deps(eng, reads, writes)
        self.cnt[eng] += 1
        tok = (("e", eng), self.cnt[eng], eng)
        self.ops[eng].append((fn, waits, ("e", eng), 1))
        self._commit(tok, reads, writes)
        return tok

    def flush(self):
        pend, self.pending = self.pending, []
        for (o, i) in pend:
            self.dma(o, i, q="sync", _flush=False)

    def dma(self, out, in_, q="sync", _flush=True):
        if q == "store":
            self.pending.append((out, in_))
            return None
        pool = self.dpool[q]
        k = pool[self.dnext[q] % len(pool)]
        self.dnext[q] += 1
        waits = self._deps(q, [in_.t], [out.t])
        prev = self.duse[k] * 16
        if prev > 0 and self.seen[q].get(("d", k), 0) < prev:
            self.seen[q][("d", k)] = prev
            waits.append((("d", k), prev, "dma"))
        self.duse[k] += 1
        tok = (("d", k), self.duse[k] * 16, "dma")
        oa, ia = out.ap, in_.ap
        self.ops[q].append((lambda e: e.dma_start(out=oa, in_=ia), waits, ("d", k), 16))
        self._commit(tok, [in_.t], [out.t])
        if _flush:
            self.flush()
        return tok

    def barrier(self):
        self.flush()
        latest = []
        for e in self.ops:
            if self.cnt[e] > 0:
                latest.append((("e", e), self.cnt[e], e))
        for k in range(NDSEM):
            if self.duse[k] > 0:
                latest.append((("d", k), self.duse[k] * 16, "dma"))
        for e in self.ops:
            waits = []
            for tk in latest:
                if self.seen[e].get(tk[0], 0) >= tk[1]:
                    continue
                self.seen[e][tk[0]] = tk[1]
                waits.append(tk)
            self.ops[e].append((None, waits, None, 0))

    def _semh(self, key):
        return self.sem[key[1]] if key[0] == "e" else self.dsem[key[1]]

    def emit(self, block, final_tokens):
        P = self
        waited = {e: set() for e in P.ops}
        for eng in P.ops:
            for fn, waits, semkey, inc in P.ops[eng]:
                for (sk, val, _) in waits:
                    if sk[0] == "e":
                        waited[sk[1]].add(val)
        for (sk, val, _) in final_tokens:
            if sk[0] == "e":
                waited[sk[1]].add(val)
        sigmap = {}
        for eng in P.ops:
            n = 0
            sig = 0
            m = {}
            for fn, waits, semkey, inc in P.ops[eng]:
                if fn is None or semkey[0] != "e":
                    continue
                n += 1
                if n in waited[eng]:
                    sig += 1
                    m[n] = sig
            sigmap[eng] = m

        def wait(e, sk, val):
            if sk[0] == "e":
                e.wait_ge(P.sem[sk[1]], sigmap[sk[1]][val])
            else:
                e.wait_ge(P.dsem[sk[1]], val)

        def run(engname, e):
            n = 0
            for fn, waits, semkey, inc in P.ops[engname]:
                for (sk, val, _) in waits:
                    wait(e, sk, val)
                if fn is None:
                    continue
                ins = fn(e)
                if semkey[0] == "e":
                    n += 1
                    if n in sigmap[engname]:
                        ins.then_inc(P.sem[engname], 1)
                else:
                    ins.then_inc(P.dsem[semkey[1]], inc)
            if engname == "sync":
                for (sk, val, _) in final_tokens:
                    wait(e, sk, val)

        @block.sync
        def _(e):
            run("sync", e)

        @block.tensor
        def _(e):
            run("tensor", e)

        @block.vector
        def _(e):
            run("vector", e)

        @block.scalar
        def _(e):
            run("scalar", e)

        @block.gpsimd
        def _(e):
            run("gpsimd", e)

    def tt(self, eng, out, a, b, op):
        self.op(eng, lambda e: e.tensor_tensor(out=out.ap, in0=a.ap, in1=b.ap, op=op), _tl(a, b), _tl(out))

    def ts(self, eng, out, a, s1, op0, s2=None, op1=None):
        if op1 is None:
            if eng == "gpsimd":
                fn = lambda e: e.tensor_scalar(out=out.ap, in0=a.ap, scalar1=_ap(s1), scalar2=(0.0 if op0 == ALU.mult else 1.0),
                                               op0=op0, op1=(ALU.add if op0 == ALU.mult else ALU.mult))
            else:
                fn = lambda e: e.tensor_scalar(out=out.ap, in0=a.ap, scalar1=_ap(s1), scalar2=None, op0=op0)
        else:
            fn = lambda e: e.tensor_scalar(out=out.ap, in0=a.ap, scalar1=_ap(s1), scalar2=_ap(s2), op0=op0, op1=op1)
        self.op(eng, fn, _tl(a, s1, s2), _tl(out))

    def stt(self, out, a, s, b, op0, op1):
        self.op("vector", lambda e: e.scalar_tensor_tensor(out=out.ap, in0=a.ap, scalar=_ap(s), in1=b.ap, op0=op0, op1=op1),
                _tl(a, s, b), _tl(out))

    def act(self, out, a, func, scale=None, bias=None, accum=None):
        kw = {}
        if scale is not None:
            kw["scale"] = _ap(scale)
        if bias is not None:
            kw["bias"] = _ap(bias)
        if accum is not None:
            kw["accum_out"] = accum.ap
        self.op("scalar", lambda e: e.activation(out=out.ap, in_=a.ap, func=func, **kw), _tl(a, scale, bias), _tl(out, accum))

    def cp(self, eng, out, a):
        if eng == "scalar":
            self.op(eng, lambda e: e.copy(out=out.ap, in_=a.ap), _tl(a), _tl(out))
        else:
            self.op(eng, lambda e: e.tensor_copy(out=out.ap, in_=a.ap), _tl(a), _tl(out))

    def mm(self, out, l, r, start, stop, tp=None):
        kw = {} if tp is None else {"tile_position": tp}
        self.op("tensor", lambda e: e.matmul(out=out.ap, lhsT=l.ap, rhs=r.ap, start=start, stop=stop, skip_group_check=True, **kw),
                _tl(l, r), _tl(out))

    def tr(self, out, a, ident, tp=None):
        kw = {} if tp is None else {"tile_position": tp}
        self.op("tensor", lambda e: e.transpose(out=out.ap, in_=a.ap, identity=ident.ap, **kw), _tl(a, ident), _tl(out))

    def scan(self, out, d0, d1, init):
        self.op("vector", lambda e: e.tensor_tensor_scan(out=out.ap, data0=d0.ap, data1=d1.ap, initial=_ap(init), op0=ALU.mult, op1=ALU.add),
                _tl(d0, d1, init), _tl(out))

    def recip(self, out, a):
        self.op("vector", lambda e: e.reciprocal(out=out.ap, in_=a.ap), _tl(a), _tl(out))

    def red(self, out, a):
        self.op("vector", lambda e: e.tensor_reduce(out=out.ap, in_=a.ap, axis=AX.X, op=ALU.add), _tl(a), _tl(out))

    def memset(self, eng, out, val):
        self.op(eng, lambda e: e.memset(out.ap, val), [], _tl(out))


def build_program(dbg=False, stop_at=None):
    nc = bass.Bass("TRN2", target_bir_lowering=False)
    es = ExitStack()
    with es:
        P = Prog(nc, es)
        EI, EO, IN = "ExternalInput", "ExternalOutput", "Internal"
        d_xmain = P.dram("x_main", [TC, D], F32, EI)
        d_xpre = P.dram("x_pre", [NPRE * SEG, D], F32, EI)
        d_xhalo = P.dram("x_halo", [128, D], F32, EI)
        d_pos = P.dram("pos_main", [128, TC // 128], I32, EI)
        d_posh = P.dram("pos_halo", [128, 1], I32, EI)
        d_win = P.dram("w_in", [D, IN_W], F32, EI)
        d_wout = P.dram("w_out", [D, D], F32, EI)
        d_wglu = P.dram("w_glu", [1024, 1024], F32, EI)
        d_normw = P.dram("normw", [128, 16], F32, EI)
        d_outnw = P.dram("outnw", [128, 16], F32, EI)
        d_bglu = P.dram("bglu", [128, 8], F32, EI)
        d_dskip = P.dram("dskip", [128, 8], F32, EI)
        d_qw = P.dram("qw", [128, 64], F32, EI)
        d_kw = P.dram("kw", [128, 64], F32, EI)
        d_sinks = P.dram("sinks", [128, 16], F32, EI)
        d_invf = P.dram("invf", [128, 32], F32, EI)
        d_are = P.dram("are", [128, 32], F32, EI)
        d_aim = P.dram("aim", [128, 32], F32, EI)
        d_lst = P.dram("lst", [128, 32], F32, EI)
        d_bre = P.dram("bre", [128, 1024], F32, EI)
        d_bim = P.dram("bim", [128, 1024], F32, EI)
        d_cre = P.dram("cre", [128, 1024], F32, EI)
        d_cim = P.dram("cim", [128, 1024], F32, EI)
        d_ident = P.dram("ident", [128, 128], F32, EI)
        d_maskc = P.dram("maskc", [128, 128], F32, EI)
        d_maskp = P.dram("maskp", [128, 128], F32, EI)
        d_maskf = P.dram("maskf", [128, 128], F32, EI)
        d_kvec = P.dram("kvec", [128, NK], F32, EI)
        d_jrow = P.dram("jrow", [128, NB + 1], F32, EI)
        d_out = P.dram("out", [TC, D], F32, EO)
        s_win = P.dram("s_win", [D, IN_W], BF16, IN)
        s_wout = P.dram("s_wout", [D, D], BF16, IN)
        s_wglu = P.dram("s_wglu", [1024, 1024], BF16, IN)
        s_bs = P.dram("s_bs", [8, 128, 2 * LB_ * 128], BF16, IN)
        s_cs = P.dram("s_cs", [8, 128, 4 * 2 * LB_ * 32], BF16, IN)
        s_fir = P.dram("s_fir", [8, 128, LB_ * 128], BF16, IN)
        s_tab = P.dram("s_tab", [8, 128, 4 * 3 * (NB + 1)], F32, IN)
        dbg_t = {}
        if dbg:
            dbg_t["proj_q"] = P.dram("dbg_q", [128, 1024], F32, EO)

        def load_const(dram_t, shape, dt=F32, name=None):
            t = P.sb(shape, dt, name)
            P.dma(t[:], dram_t[:])
            return t

        identf = load_const(d_ident, [128, 128], name="identf")
        idb = P.sb([128, 128], BF16, "idb")
        P.cp("vector", idb[:], identf[:])
        normw = load_const(d_normw, [128, 16], name="normw")
        outnw = load_const(d_outnw, [128, 16], name="outnw")
        bglu = load_const(d_bglu, [128, 8], name="bglu")
        dskip = load_const(d_dskip, [128, 8], name="dskip")
        qw = load_const(d_qw, [128, 64], name="qw")
        kw = load_const(d_kw, [128, 64], name="kw")
        invf = load_const(d_invf, [128, 32], name="invf")
        sinks = load_const(d_sinks, [128, 16], name="sinks")
        esink = P.sb([128, 16], F32, "esink")
        P.act(esink[:], sinks[:], AF.Exp)
        mk_f = load_const(d_maskc, [128, 128])
        maskc = P.sb([128, 128], BF16, "maskc")
        P.cp("vector", maskc[:], mk_f[:])
        mk_f2 = load_const(d_maskp, [128, 128])
        maskp = P.sb([128, 128], BF16, "maskp")
        P.cp("vector", maskp[:], mk_f2[:])
        mk_f3 = load_const(d_maskf, [128, 128])
        maskf = P.sb([128, 128], BF16, "maskf")
        P.cp("vector", maskf[:], mk_f3[:])
        mask2 = P.sb([128, 2, 2, 128], BF16, "mask2")
        mask2f = P.sb([128, 2, 2, 128], BF16, "mask2f")
        for jj_ in range(2):
            P.cp("vector", mask2[:, 0, jj_, :], mk_f2[:])
            P.cp("vector", mask2[:, 1, jj_, :], mk_f[:])
            P.cp("vector", mask2f[:, 0, jj_, :], mk_f3[:])
            P.cp("vector", mask2f[:, 1, jj_, :], mk_f[:])
        posi = load_const(d_pos, [128, TC // 128], I32)
        posf = P.sb([128, TC // 128], F32, "posf")
        P.cp("vector", posf[:], posi[:])
        poshi = load_const(d_posh, [128, 1], I32)
        poshf = P.sb([128, 1], F32, "poshf")
        P.cp("vector", poshf[:], poshi[:])
        ones_b = P.sb([128, 1], BF16, "ones_b")
        P.memset("vector", ones_b[:], 1.0)
        NTL = TC // 128 + 1
        cos_all = P.sb([128, NTL, 32], F32, "cos_all")
        sin_all = P.sb([128, NTL, 32], F32, "sin_all")
        r8 = P.sb([128, 32], F32, "r8")
        c512 = P.sb([128, 32], F32, "c512")
        s512 = P.sb([128, 32], F32, "s512")
        g0re_t = [P.sb([128, 4], F32, "g0re") for _ in range(8)]
        g0im_t = [P.sb([128, 4], F32, "g0im") for _ in range(8)]
        for g_ in g0re_t + g0im_t:
            P.memset("vector", g_[:], 0.0)

        pA = P.ps([128, 512], F32, "pA")
        pB = P.ps([128, 512], F32, "pB")
        pT = P.ps([128, 1024], BF16, "pT")
        pT2 = P.ps([128, 1024], BF16, "pT2")
        pS = P.ps([128, 512], F32, "pS")
        pO = P.ps([128, 512], F32, "pO")
        pY = P.ps([128, 512], F32, "pY")
        pM = P.ps([128, 512], F32, "pM")

        def phasor(f, cos_out, sin_out, n, scope_es):
            old = P.es
            P.es = scope_es
            fi = P.sb([128, n], I32)
            ff = P.sb([128, n], F32)
            f1 = P.sb([128, n], F32)
            f2 = P.sb([128, n], F32)
            P.es = old
            P.cp("vector", fi[:], f)
            P.cp("vector", ff[:], fi[:])
            P.tt("vector", f1[:], f, ff[:], ALU.subtract)
            P.act(sin_out, f1[:], AF.Sin, scale=S2PI)
            P.ts("vector", f2[:], f1[:], 0.25, ALU.add)
            P.cp("vector", fi[:], f2[:])
            P.cp("vector", ff[:], fi[:])
            P.tt("vector", f1[:], f2[:], ff[:], ALU.subtract)
            P.act(cos_out, f1[:], AF.Sin, scale=S2PI)

        class _Stop(Exception):
            pass

        def stage(name):
            if stop_at == name:
                raise _Stop()

        try:
            stage('wconv')
            with ExitStack() as es2:
                P.es = es2
                are = load_const(d_are, [128, 32])
                aim = load_const(d_aim, [128, 32])
                lst = load_const(d_lst, [128, 32])
                kvec = load_const(d_kvec, [128, NK])
                jrow = load_const(d_jrow, [128, NB + 1])
                dl = P.sb([128, 32], F32)
                P.act(dl[:], lst[:], AF.Exp)
                ard = P.sb([128, 32], F32)
                P.tt("vector", ard[:], are[:], dl[:], ALU.mult)
                th = P.sb([128, 32], F32)
                P.stt(th[:], aim[:], 1.0 / (2.0 * math.pi), dl[:], ALU.mult, ALU.mult)
                fk = P.sb([128, NK, 32], F32)
                ek = P.sb([128, NK, 32], F32)
                kb = kvec[:].us(2).bc([128, NK, 32])
                P.tt("vector", fk[:], th[:].us(1).bc([128, NK, 32]), kb, ALU.mult)
                P.tt("vector", ek[:], ard[:].us(1).bc([128, NK, 32]), kb, ALU.mult)
                rk = P.sb([128, NK, 32], F32)
                fl = lambda t: t[:].r("p a b -> p (a b)")
                P.act(fl(rk), fl(ek), AF.Exp)
                ck = P.sb([128, NK, 32], F32)
                sk = P.sb([128, NK, 32], F32)
                phasor(fl(fk), fl(ck), fl(sk), NK * 32, es2)
                Lr = P.sb([128, NK, 32], F32)
                Li = P.sb([128, NK, 32], F32)
                P.tt("vector", Lr[:], rk[:], ck[:], ALU.mult)
                P.tt("vector", Li[:], rk[:], sk[:], ALU.mult)
                P.cp("vector", r8[:], rk[:, LB_, :])
                P.cp("vector", c512[:], ck[:, LB_ + 1, :])
                P.cp("vector", s512[:], sk[:, LB_ + 1, :])
                nre = P.sb([128, 32], F32)
                P.ts("vector", nre[:], Lr[:, 1, :], -1.0, ALU.add)
                nim = Li[:, 1, :]
                t1 = P.sb([128, 32], F32)
                t2 = P.sb([128, 32], F32)
                den = P.sb([128, 32], F32)
                P.tt("vector", t1[:], are[:], are[:], ALU.mult)
                P.tt("vector", t2[:], aim[:], aim[:], ALU.mult)
                P.tt("vector", den[:], t1[:], t2[:], ALU.add)
                rden = P.sb([128, 32], F32)
                P.recip(rden[:], den[:])
                qre = P.sb([128, 32], F32)
                qim = P.sb([128, 32], F32)
                P.tt("vector", t1[:], nre[:], are[:], ALU.mult)
                P.tt("vector", t2[:], nim, aim[:], ALU.mult)
                P.tt("vector", t1[:], t1[:], t2[:], ALU.add)
                P.tt("vector", qre[:], t1[:], rden[:], ALU.mult)
                P.tt("vector", t1[:], nim, are[:], ALU.mult)
                P.tt("vector", t2[:], nre[:], aim[:], ALU.mult)
                P.tt("vector", t1[:], t1[:], t2[:], ALU.subtract)
                P.tt("vector", qim[:], t1[:], rden[:], ALU.mult)

                with ExitStack() as es3:
                    P.es = es3
                    posall = P.sb([128, NTL], F32)
                    P.cp("vector", posall[:, 0:1], poshf[:])
                    P.cp("vector", posall[:, 1:NTL], posf[:])
                    fro = P.sb([128, NTL, 32], F32)
                    P.tt("vector", fro[:], posall[:].us(2).bc([128, NTL, 32]), invf[:].us(1).bc([128, NTL, 32]), ALU.mult)
                    phasor(fro[:].r("p a b -> p (a b)"), cos_all[:].r("p a b -> p (a b)"), sin_all[:].r("p a b -> p (a b)"), NTL * 32, es3)
                    P.es = es2
                    P.barrier()
                P.es = es2
                with ExitStack() as es3:
                    P.es = es3
                    NJ = NB + 1
                    fj = P.sb([128, 32, NJ], F32)
                    P.tt("vector", fj[:], fk[:, LB_, :].us(2).bc([128, 32, NJ]), jrow[:].us(1).bc([128, 32, NJ]), ALU.mult)
                    tab = P.sb([128, 3, 32, NJ], F32, "tab")
                    phasor(fj[:].r("p a b -> p (a b)"), tab[:, 0].r("p a b -> p (a b)"), tab[:, 1].r("p a b -> p (a b)"), 32 * NJ, es3)
                    P.memset("vector", tab[:, 2].r("p a b -> p (a b)"), 0.0)
                    P.cp("vector", tab[:, 2, :, 1:NJ], r8[:].us(2).bc([128, 32, NB]))
                    for ft in range(8):
                        P.dma(s_tab[ft].r("p (c a b) -> p c a b", c=3, a=4), tab[:, :, 4 * ft:4 * ft + 4, :], q="sync")
                    P.es = es2
                    P.barrier()
                P.es = es2
                P.barrier()

                NSTG = 4
                stg = [P.sb([128, 1536], F32, "stg") for _ in range(NSTG)]
                stb = [P.sb([128, 1536], BF16, "stb") for _ in range(NSTG)]

                def wjob(src_t, dst_t, r, c0, n, scal):
                    return (src_t[r * 128:(r + 1) * 128, c0:c0 + n], dst_t[r * 128:(r + 1) * 128, c0:c0 + n], scal, n)

                jobs = [wjob(d_win, s_win, dk, c * 1536, 1536, normw[:, dk:dk + 1]) for dk in range(16) for c in (1, 2, 0)]
                jobs += [wjob(d_wglu, s_wglu, ft, 0, 1024, None) for ft in range(8)]
                jobs += [wjob(d_wout, s_wout, ft, c * 1024, 1024, outnw[:, ft:ft + 1]) for ft in range(16) for c in range(2)]
                nld = 0
                for i in range(len(jobs)):
                    while nld < len(jobs) and nld < i + NSTG:
                        P.dma(stg[nld % NSTG][:, 0:jobs[nld][3]], jobs[nld][0], q="scalar")
                        nld += 1
                    src, dst, scal, n = jobs[i]
                    if scal is None:
                        P.cp("scalar", stb[i % NSTG][:, 0:n], stg[i % NSTG][:, 0:n])
                    else:
                        P.act(stb[i % NSTG][:, 0:n], stg[i % NSTG][:, 0:n], AF.Copy, scale=scal)
                    P.dma(dst, stb[i % NSTG][:, 0:n], q="scalar")
                bre = load_const(d_bre, [128, 1024])
                bim = load_const(d_bim, [128, 1024])
                cre = load_const(d_cre, [128, 1024])
                cim = load_const(d_cim, [128, 1024])
                cm_tmps = {"vector": [P.sb([128, 32, 32], F32) for _ in range(4)]}

                def cmul(ore, oim, are_, aim_, sre, sim, neg_im=False, eng="vector"):
                    sr = sre.us(2).bc([128, 32, 32])
                    si = sim.us(2).bc([128, 32, 32])
                    u1, u2, u3, u4 = cm_tmps[eng]
                    P.tt(eng, u1[:], are_, sr, ALU.mult)
                    P.tt(eng, u2[:], aim_, si, ALU.mult)
                    P.tt(eng, ore, u1[:], u2[:], ALU.subtract)
                    P.tt(eng, u3[:], aim_, sr, ALU.mult)
                    P.tt(eng, u4[:], are_, si, ALU.mult)
                    if neg_im:
                        P.tt(eng, u3[:], u3[:], u4[:], ALU.add)
                        P.ts(eng, oim, u3[:], -1.0, ALU.mult)
                    else:
                        P.tt(eng, oim, u3[:], u4[:], ALU.add)

                b3 = lambda t: t[:].r("p (a b) -> p a b", b=32)
                Bbre = P.sb([128, 32, 32], F32)
                Bbim = P.sb([128, 32, 32], F32)
                cmul(Bbre[:], Bbim[:], b3(bre), b3(bim), qre[:], qim[:])
                LB = P.sb([128, LB_, 2, 32, 32], BF16, "LB")
                CS = P.sb([128, LB_, 2, 32, 32], BF16, "CS")
                cbf = P.sb([128, 2, 32, 32], BF16, "cbf")
                P.cp("vector", cbf[:, 0], b3(cre))
                P.ts("vector", cbf[:, 1], b3(cim), -1.0, ALU.mult)
                for k in range(LB_):
                    cmul(LB[:, k, 0], LB[:, k, 1], Bbre[:], Bbim[:], Lr[:, k, :], Li[:, k, :])
                for i in range(LB_):
                    cmul(CS[:, i, 0], CS[:, i, 1], b3(cre), b3(cim), Lr[:, i + 1, :], Li[:, i + 1, :], neg_im=True,
                         eng="vector")
                for ft in range(8):
                    for k in range(4):
                        P.dma(s_cs[ft][:, k * 64 * LB_:(k + 1) * 64 * LB_].r("p (i r h) -> p i r h", i=LB_, r=2),
                              CS[:, :, :, 4 * ft + k, :], q="sync")
                fir = P.sb([128, 8, LB_, 128], BF16, "fir")
                P.memset("vector", fir[:].r("p a b c -> p (a b c)"), 0.0)
                bsb = [P.sb([128, 2 * LB_, 128], BF16, "bsb") for _ in range(2)]
                pTs = [pT, pT2]
                for ft in range(8):
                    for hf in range((2 * LB_ + 7) // 8):
                        pt = pTs[hf % 2]
                        pt3 = pt[:].r("p (a b) -> p a b", b=128)
                        nsl = min(8, 2 * LB_ - hf * 8)
                        for k in range(4):
                            gp = 4 * ft + k
                            for il in range(nsl // 2):
                                i = hf * 4 + il
                                for ri in range(2):
                                    P.tr(pt3[32 * k:32 * k + 32, il * 2 + ri, :], LB[:, LB_ - 1 - i, ri, gp, :], idb[:], tp=(0, 32 * k))
                        P.cp("vector" if hf == 0 else "scalar", bsb[ft % 2][:, hf * 8:hf * 8 + nsl, :].r("p a b -> p (a b)"), pt[:, 0:nsl * 128])
                    P.dma(s_bs[ft], bsb[ft % 2][:].r("p a b -> p (a b)"), q="sync")
                    pk3 = pM[:, 0:32 * LB_].r("p (d h) -> p d h", h=32)
                    for k in range(4):
                        gp = 4 * ft + k
                        for d in range(LB_):
                            P.mm(pk3[32 * k:32 * k + 32, d, :], LB[:, d, 0, gp, :], cbf[:, 0, gp, :], start=(d == 0), stop=False, tp=(0, 32 * k))
                            P.mm(pk3[32 * k:32 * k + 32, d, :], LB[:, d, 1, gp, :], cbf[:, 1, gp, :], start=False, stop=True, tp=(0, 32 * k))
                    for k in range(4):
                        P.cp("vector", fir[32 * k:32 * k + 32, ft, :, 32 * k:32 * k + 32], pk3[32 * k:32 * k + 32, :, :])
                    P.dma(s_fir[ft], fir[:, ft].r("p a b -> p (a b)"), q="sync")
                P.es = es
                P.barrier()
            P.es = es
            P.barrier()

            xt = [P.sb([128, D], F32, "xt") for _ in range(1)]
            xn = P.sb([128, D], BF16, "xn")
            xnTs = [P.sb([128, 8, SEG], BF16, "xnT") for _ in range(2)]
            wbuf = [P.sb([128, 16, WCB], BF16, "wbuf") for _ in range(2)]
            uT = P.sb([128, 8, SEG], BF16, "uT")
            szsT = P.sb([128, 8, SEG], BF16, "szsT")
            ygT = P.sb([128, 8, SEG], BF16, "ygT")
            maT = P.sb([128, 8, SEG], BF16, "maT")
            q_sb = [P.sb([128, 1024], BF16, "q_sb") for _ in range(NT)]
            kv_sb = [P.sb([128, 512], BF16, "kv_sb") for _ in range(NT)]
            sz_sb = [P.sb([128, 1024], BF16, "sz_sb") for _ in range(NT)]
            rstd_a = P.sb([128, NT], F32, "rstd_a")
            rstd_s = P.sb([128, NT], F32, "rstd_s")
            kT = [P.sb([128, 4, 128], BF16, "kT") for _ in range(2)]
            vext = [P.sb([128, 4, 65], BF16, "vext") for _ in range(2)]
            for v_ in vext:
                P.memset("vector", v_[:].r("p a b -> p (a b)"), 1.0)
            bs_t = [P.sb([128, 2 * LB_, 128], BF16, "bs_t") for _ in range(2)]
            cs_t = [P.sb([128, 4, LB_, 2, 32], BF16, "cs_t") for _ in range(2)]
            fir_t = [P.sb([128, LB_, 128], BF16, "fir_t") for _ in range(2)]
            NTAB = 3
            tab_t = [P.sb([128, 3, 4, NB + 1], F32, "tab_t") for _ in range(NTAB)]
            wctr = [0]
            sctr = [0]
            xctr = [0]
            pctr = [0]

            def next_wbuf():
                w = wbuf[wctr[0] % 2]
                wctr[0] += 1
                return w

            def mm_bank():
                b = (pA, pB)[pctr[0] % 2]
                pctr[0] += 1
                return b

            def small(shape, dt=F32, name="sm"):
                return P.sb(shape, dt, name)

            ss = small([128, 1]); ms = small([128, 1]); sd = small([128, 1]); rstd = small([128, 1])
            mhalf = small([128, 1])
            P.memset("vector", mhalf[:], -0.5)

            def norm_stage1(src_rows):
                x_ = xt[0]
                xctr[0] += 1
                P.dma(x_[:], src_rows)
                P.act(xn[:], x_[:], AF.Square, accum=ss[:])
                P.ts("gpsimd", ms[:], ss[:], 1.0 / D, ALU.mult, 1e-6, ALU.add)
                P.tt("gpsimd", rstd[:], ms[:], mhalf[:], ALU.pow)
                P.act(xn[:], x_[:], AF.Copy, scale=rstd[:])

            def norm_transpose(src_rows, tt_slot):
                norm_stage1(src_rows)
                norm_stage2(tt_slot)

            def norm_stage2(tt_slot):
                for hf in range(2):
                    pt = (pT, pT2)[hf]
                    for j in range(8):
                        dk = hf * 8 + j
                        P.tr(pt[:, j * 128:(j + 1) * 128], xn[:, dk * 128:(dk + 1) * 128], idb[:])
                    P.cp("vector" if hf == 0 else "scalar",
                         xnTs[hf][:, :, tt_slot * 128:(tt_slot + 1) * 128],
                         pt[:].r("p (a b) -> p a b", b=128))

            def load_w(scr, col0, nk, ncol=WCB, row0=0):
                w = next_wbuf()
                src = scr[row0:row0 + nk * 128, col0:col0 + ncol].r("(k p) c -> p k c", p=128)
                P.dma(w[:, 0:nk, 0:ncol], src)
                return w

            qn_t = small([128, 1024]); sq_t = small([128, 1024]); ssq = small([128, 16]); rq = small([128, 16])
            ta = small([128, 512]); tb = small([128, 512]); tc_ = small([128, 512]); td = small([128, 512])
            qr = small([128, 1024], BF16); kr = small([128, 256], BF16); kdup = small([128, 4, 128], BF16)
            qT = small([128, 8, 128], BF16)
            rope_sel = [0]
            pe2 = [small([128, 1024], BF16) for _ in range(2)]; pm2 = [small([128, 1024], BF16) for _ in range(2)]
            den4 = small([128, 4]); rden4 = small([128, 4]); o_t = sq_t; oa_t = small([128, 1024], BF16)
            osq = qr; ssa = small([128, 1]);

            def qk_norm_rope(src, nh, w_t, out_bf):
                n = nh * 64
                s3 = lambda v: v.r("p (h d) -> p h d", d=64)
                P.tt("vector", sq_t[:, 0:n], src, src, ALU.mult)
                P.red(ssq[:, 0:nh], s3(sq_t[:, 0:n]))
                P.ts("vector", ssq[:, 0:nh], ssq[:, 0:nh], 1.0 / 64, ALU.mult, 1e-6, ALU.add)
                P.tt("gpsimd", rq[:, 0:nh], ssq[:, 0:nh], mhalf[:].bc([128, nh]), ALU.pow)
                P.tt("vector", s3(qn_t[:, 0:n]), s3(src), rq[:, 0:nh].us(2).bc([128, nh, 64]), ALU.mult)
                P.tt("vector", s3(qn_t[:, 0:n]), s3(qn_t[:, 0:n]), w_t[:].us(1).bc([128, nh, 64]), ALU.mult)
                q4 = qn_t[:, 0:n].r("p (h t d) -> p h t d", t=2, d=32)
                o4 = out_bf.r("p (h t d) -> p h t d", t=2, d=32)
                cb_ = cos_all[:, rope_sel[0], :].us(1).bc([128, nh, 32])
                sb_ = sin_all[:, rope_sel[0], :].us(1).bc([128, nh, 32])
                m = nh * 32
                v3 = lambda t: t[:, 0:m].r("p (h d) -> p h d", d=32)
                P.tt("vector", v3(ta), q4[:, :, 0, :], cb_, ALU.mult)
                P.tt("vector", v3(tb), q4[:, :, 1, :], sb_, ALU.mult)
                P.tt("vector", o4[:, :, 0, :], v3(ta), v3(tb), ALU.subtract)
                P.tt("vector", v3(tc_), q4[:, :, 1, :], cb_, ALU.mult)
                P.tt("vector", v3(td), q4[:, :, 0, :], sb_, ALU.mult)
                P.tt("vector", o4[:, :, 1, :], v3(tc_), v3(td), ALU.add)

            if True:

                _ph = {}

                def phasor_small(f, cos_out, sin_out):
                    if "fi" not in _ph:
                        _ph["fi"] = P.sb([128, 32], I32); _ph["ff"] = P.sb([128, 32], F32)
                        _ph["f1"] = P.sb([128, 32], F32); _ph["f2"] = P.sb([128, 32], F32)
                    fi, ff, f1, f2 = _ph["fi"], _ph["ff"], _ph["f1"], _ph["f2"]
                    P.cp("vector", fi[:], f)
                    P.cp("vector", ff[:], fi[:])
                    P.tt("vector", f1[:], f, ff[:], ALU.subtract)
                    P.act(sin_out, f1[:], AF.Sin, scale=S2PI)
                    P.ts("vector", f2[:], f1[:], 0.25, ALU.add)
                    P.cp("vector", fi[:], f2[:])
                    P.cp("vector", ff[:], fi[:])
                    P.tt("vector", f1[:], f2[:], ff[:], ALU.subtract)
                    P.act(cos_out, f1[:], AF.Sin, scale=S2PI)

                def rope_tables2(pos_col):
                    pass

                def prep_kv(kv, gtt, slot):
                    rope_sel[0] = gtt + 1
                    qk_norm_rope(kv[:, 0:256], 4, kw, kr[:])
                    k3 = kr[:].r("p (g d) -> p g d", d=64)
                    P.cp("vector", kdup[:, :, 0:64], k3)
                    P.cp("gpsimd", kdup[:, :, 64:128], k3)
                    for g in range(4):
                        P.tr(pT[:, g * 128:(g + 1) * 128], kdup[:, g, :], idb[:])
                    P.cp("scalar", kT[slot][:].r("p a b -> p (a b)"), pT[:, 0:512])
                    P.cp("vector", vext[slot][:, :, 0:64], kv[:, 256:512].r("p (g d) -> p g d", d=64))

                def attention_tile(tt, slot_cur, slot_prev, first):
                    qk_norm_rope(q_sb[tt][:], 16, qw, qr[:])
                    stage('qrope')
                    for j in range(8):
                        P.tr(pT2[:, j * 128:(j + 1) * 128], qr[:, j * 128:(j + 1) * 128], idb[:])
                    P.cp("scalar", qT[:].r("p a b -> p (a b)"), pT2[:])
                    stage('qT')
                    slots = (slot_prev, slot_cur)
                    mk2 = mask2f if first else mask2
                    sets = ((pS, pM), (pA, pB))

                    def scores(g):
                        bE, bO = sets[g % 2]
                        for kb in range(2):
                            for j in range(4):
                                h = 4 * g + j
                                base = 64 * (h % 2)
                                bank = bE if (h % 2 == 0) else bO
                                c0 = (kb * 2 + j // 2) * 128
                                P.mm(bank[:, c0:c0 + 128], kT[slots[kb]][base:base + 64, g, :], qT[base:base + 64, h // 2, :],
                                     start=True, stop=True)

                    scores(0)
                    for g in range(4):
                        bE, bO = sets[g % 2]
                        pe_ = pe2[g % 2]
                        pm_ = pm2[g % 2]
                        oB = (pO, pY)[g % 2]
                        for bi, bank in enumerate((bE, bO)):
                            P.act(pe_[:, bi * 512:(bi + 1) * 512], bank[:], AF.Exp, scale=0.125)
                        for bi in range(2):
                            P.tt("vector", pm_[:, bi * 512:(bi + 1) * 512], pe_[:, bi * 512:(bi + 1) * 512],
                                 mk2[:].r("p k j q -> p (k j q)"), ALU.mult)
                        if g + 1 < 4:
                            scores(g + 1)
                        n_ = 0
                        for j in range(4):
                            for kb in range(2):
                                c0 = (j % 2) * 512 + (kb * 2 + j // 2) * 128
                                P.mm(oB[:, j * 65:(j + 1) * 65], pm_[:, c0:c0 + 128], vext[slots[kb]][:, g, :],
                                     start=(n_ == 0), stop=(n_ == 7))
                                n_ += 1
                        o3 = oB[:, 0:260].r("p (j d) -> p j d", d=65)
                        P.tt("vector", den4[:], o3[:, :, 64], esink[:, 4 * g:4 * g + 4], ALU.add)
                        P.recip(rden4[:], den4[:])
                        P.tt("vector", o_t[:, g * 256:(g + 1) * 256].r("p (j d) -> p j d", d=64), o3[:, :, 0:64],
                             rden4[:].us(2).bc([128, 4, 64]), ALU.mult)
                    stage('onorm')
                    P.tt("vector", oa_t[:], o_t[:], sz_sb[tt][:], ALU.mult)
                    P.act(osq[:], oa_t[:], AF.Square, accum=ssa[:])
                    P.ts("gpsimd", ssa[:], ssa[:], 1.0 / 1024, ALU.mult, 1e-6, ALU.add)
                    P.tt("gpsimd", rstd_a[:, tt:tt + 1], ssa[:], mhalf[:], ALU.pow)
                    for j in range(8):
                        P.tr(pT[:, j * 128:(j + 1) * 128], oa_t[:, j * 128:(j + 1) * 128], idb[:])
                    P.cp("scalar", maT[:, :, tt * 128:(tt + 1) * 128], pT[:].r("p (a b) -> p a b", b=128))

                NBX = NB + 1
                Sft = [small([128, 4, 2 * NB]) for _ in range(2)]
                Xre = [small([128, 4, NBX])] * 2
                Xim = [small([128, 4, NBX])] * 2
                Gre = [small([128, 4, NBX])] * 2
                Gim = [small([128, 4, NBX])] * 2
                hre = [small([128, 4, NB], BF16) for _ in range(2)]
                him = [small([128, 4, NB], BF16) for _ in range(2)]
                m1 = small([128, 4, NB]); m2 = small([128, 4, NB]); m3 = small([128, 4, NB]); m4 = small([128, 4, NB])
                ysb = small([128, SEG]); y2 = small([128, SEG]); y3 = y2; sg = small([128, SEG]); sg2 = small([128, SEG])
                cn1 = small([128, 4]); cn2 = small([128, 4]); cn3 = small([128, 4]); cn4 = small([128, 4])
                fctr = [0]

                def load_blob(ft, full):
                    s = sctr[0] % 2
                    sctr[0] += 1
                    P.dma(bs_t[s][:].r("p a b -> p (a b)"), s_bs[ft])
                    P.dma(tab_t[(sctr[0] - 1) % NTAB][:].r("p c a b -> p (c a b)"), s_tab[ft])
                    if full:
                        P.dma(cs_t[s][:].r("p k i r h -> p (k i r h)"), s_cs[ft])
                        P.dma(fir_t[s][:].r("p a b -> p (a b)"), s_fir[ft])
                    return s

                def ssm_ft(ft, full):
                    ssm_back(ssm_front(ft, full))

                def ssm_front(ft, full):
                    s = load_blob(ft, full)
                    z = fctr[0] % 2
                    fctr[0] += 1
                    yb = (pY, pM)[z]
                    u3 = uT[:, ft, :].r("p (j i) -> p j i", i=LB_)
                    y3v = yb[:].r("p (j i) -> p j i", i=LB_)
                    if full:
                        for d in range(LB_):
                            for i in range(d, LB_):
                                P.mm(y3v[:, :, i], fir_t[s][:, d, :], u3[:, :, i - d], start=(d == 0 and i == 0), stop=False)
                    sbanks = (pO, pS, pA, pB)
                    for ri in range(2):
                        for i in range(LB_):
                            for k in range(4):
                                base = 32 * k
                                P.mm(sbanks[k][:, 256 + ri * NB:256 + (ri + 1) * NB], bs_t[s][base:base + 32, i * 2 + ri, :],
                                     u3[base:base + 32, :, i], start=(i == 0), stop=(i == LB_ - 1), tp=(base, 0))
                    for k in range(4):
                        P.cp("scalar", Sft[z][:, k, :], sbanks[k][:, 256:256 + 2 * NB])
                    return (ft, full, s, z, tab_t[(sctr[0] - 1) % NTAB])

                def ssm_back(st_):
                    ft, full, s, z, tb_ = st_
                    yb = (pY, pM)[z]
                    y3v = yb[:].r("p (j i) -> p j i", i=LB_)
                    g4 = slice(4 * ft, 4 * ft + 4)
                    sre_p = Sft[z][:, :, 0:NB]
                    sim_p = Sft[z][:, :, NB:2 * NB]
                    cM = tb_[:, 0, :, 1:NB + 1]
                    sM = tb_[:, 1, :, 1:NB + 1]
                    cD = tb_[:, 0, :, 0:NB]
                    sD = tb_[:, 1, :, 0:NB]
                    P.tt("vector", m1[:], sre_p, cM, ALU.mult)
                    P.tt("vector", m2[:], sim_p, sM, ALU.mult)
                    P.tt("vector", m3[:], sim_p, cM, ALU.mult)
                    P.tt("vector", m4[:], sre_p, sM, ALU.mult)
                    P.cp("vector", Xre[z][:, :, 0], g0re_t[ft][:])
                    P.cp("vector", Xim[z][:, :, 0], g0im_t[ft][:])
                    P.tt("vector", Xre[z][:, :, 1:NBX], m1[:], m2[:], ALU.add)
                    P.tt("vector", Xim[z][:, :, 1:NBX], m3[:], m4[:], ALU.subtract)
                    fl = lambda v: v.r("p a b -> p (a b)")
                    cf = fl(tb_[:, 2, :, :])
                    P.scan(fl(Gre[z][:]), cf, fl(Xre[z][:]), 0.0)
                    P.scan(fl(Gim[z][:]), cf, fl(Xim[z][:]), 0.0)
                    P.tt("gpsimd", cn1[:], Gre[z][:, :, NB], c512[:, g4], ALU.mult)
                    P.tt("gpsimd", cn2[:], Gim[z][:, :, NB], s512[:, g4], ALU.mult)
                    P.tt("gpsimd", cn3[:], Gre[z][:, :, NB], s512[:, g4], ALU.mult)
                    P.tt("gpsimd", cn4[:], Gim[z][:, :, NB], c512[:, g4], ALU.mult)
                    P.tt("gpsimd", g0re_t[ft][:], cn1[:], cn2[:], ALU.subtract)
                    P.tt("gpsimd", g0im_t[ft][:], cn3[:], cn4[:], ALU.add)
                    if not full:
                        return
                    P.tt("vector", m1[:], Gre[z][:, :, 0:NB], cD, ALU.mult)
                    P.tt("vector", m2[:], Gim[z][:, :, 0:NB], sD, ALU.mult)
                    P.tt("vector", m3[:], Gre[z][:, :, 0:NB], sD, ALU.mult)
                    P.tt("vector", m4[:], Gim[z][:, :, 0:NB], cD, ALU.mult)
                    P.tt("vector", hre[z][:], m1[:], m2[:], ALU.subtract)
                    P.tt("vector", him[z][:], m3[:], m4[:], ALU.add)
                    for i in range(LB_):
                        for k in range(4):
                            base = 32 * k
                            P.mm(y3v[base:base + 32, :, i], cs_t[s][:, k, i, 0, :], hre[z][:, k, :], start=False, stop=False, tp=(0, base))
                        for k in range(4):
                            base = 32 * k
                            P.mm(y3v[base:base + 32, :, i], cs_t[s][:, k, i, 1, :], him[z][:, k, :], start=False, stop=(k == 3 and i == LB_ - 1), tp=(0, base))
                    P.stt(ysb[:], uT[:, ft, :], dskip[:, ft:ft + 1], yb[:], ALU.mult, ALU.add)
                    P.tt("vector", y2[:], ysb[:], ysb[:], ALU.mult)
                    P.ts("vector", y2[:], y2[:], 0.044715, ALU.mult, 1.0, ALU.add)
                    P.tt("vector", y3[:], y2[:], ysb[:], ALU.mult)
                    P.act(sg[:], y3[:], AF.Sigmoid, scale=2.0 * math.sqrt(2.0 / math.pi))
                    P.tt("vector", ygT[:, ft, :], ysb[:], sg[:], ALU.mult)

                def glu_and_gate():
                    for cb in range(4):
                        w = load_w(s_wglu, cb * WCB, 8)
                        for f2 in range(2):
                            fo = cb * 2 + f2
                            bank = mm_bank()
                            for fi_ in range(8):
                                P.mm(bank[:], w[:, fi_, f2 * 128:(f2 + 1) * 128], ygT[:, fi_, :], start=(fi_ == 0), stop=(fi_ == 7))
                            gi_ = (fo % 2) if 'b' in KF else 0
                            sg_ = (sg, sg2)[gi_]
                            ys_ = (ysb, y2)[gi_]
                            oq_ = (pm2[0], pm2[1])[gi_]
                            P.act(sg_[:], bank[:], AF.Sigmoid, bias=bglu[:, fo:fo + 1])
                            P.tt("vector", ys_[:], sg_[:], ygT[:, fo, :], ALU.mult)
                            P.tt("vector", szsT[:, fo, :], ys_[:], szsT[:, fo, :], ALU.mult)
                            P.act(oq_[:, 0:SEG], szsT[:, fo, :], AF.Square)
                            for tt in range(NT):
                                P.mm(pM[:, 256 + tt:256 + tt + 1], oq_[:, tt * 128:(tt + 1) * 128], ones_b[:], start=(fo == 0 and tt == 0),
                                     stop=(fo == 7))
                    P.ts("vector", rstd_s[:], pM[:, 256:256 + NT], 1.0 / 1024, ALU.mult, 1e-6, ALU.add)
                    P.tt("gpsimd", rstd_s[:], rstd_s[:], mhalf[:].bc([128, NT]), ALU.pow)

                xres = [small([128, WCB]) for _ in range(4)]
                ot = [small([128, WCB]) for _ in range(4)]
                octr = [0]

                def out_proj(seg):
                    for cb in range(D // WCB):
                        w = load_w(s_wout, cb * WCB, 16)
                        for tt in range(NT):
                            row0 = seg * SEG + tt * 128
                            b = octr[0] % 4
                            octr[0] += 1
                            P.dma(xres[b][:], d_xmain[row0:row0 + 128, cb * WCB:(cb + 1) * WCB])
                            obanks = (pA, pB, pS, pO, pY, pM) if 'a' in KF else (pA, pB, pA, pB, pA, pB)
                            b1 = obanks[(2 * octr[0]) % 6]
                            for ft in range(8):
                                P.mm(b1[:, 0:WCB], maT[:, ft, tt * 128:(tt + 1) * 128], w[:, ft, :], start=(ft == 0), stop=(ft == 7))
                            b2 = obanks[(2 * octr[0] + 1) % 6]
                            for ft in range(8):
                                P.mm(b2[:, 0:WCB], szsT[:, ft, tt * 128:(tt + 1) * 128], w[:, 8 + ft, :], start=(ft == 0), stop=(ft == 7))
                            P.stt(ot[b][:], b1[:, 0:WCB], rstd_a[:, tt:tt + 1], xres[b][:], ALU.mult, ALU.add)
                            P.stt(ot[b][:], b2[:, 0:WCB], rstd_s[:, tt:tt + 1], ot[b][:], ALU.mult, ALU.add)
                            P.dma(d_out[row0:row0 + 128, cb * WCB:(cb + 1) * WCB], ot[b][:], q="scalar")

                def proj_feature_major(col0, dst, silu):
                    w = load_w(s_win, col0, 16)
                    for f2 in range(2):
                        bank = mm_bank()
                        for dk in range(16):
                            P.mm(bank[:], w[:, dk, f2 * 128:(f2 + 1) * 128], xnTs[dk // 8][:, dk % 8, :], start=(dk == 0), stop=(dk == 15))
                        if silu:
                            P.act(dst[f2], bank[:], AF.Silu)
                        else:
                            P.cp("vector", dst[f2], bank[:])

                def proj_token_major(col0, dst_fn, silu, tts):
                    w = load_w(s_win, col0, 16)
                    for tt in tts:
                        bank = mm_bank()
                        for dk in range(16):
                            P.mm(bank[:, 0:WCB], xnTs[dk // 8][:, dk % 8, tt * 128:(tt + 1) * 128], w[:, dk, :], start=(dk == 0), stop=(dk == 15))
                        if silu:
                            P.act(dst_fn(tt), bank[:, 0:WCB], AF.Silu)
                        else:
                            P.cp("scalar" if tt % 2 else "vector", dst_fn(tt), bank[:, 0:WCB])

                stage('ssmpre')
                def nt_pre(sgi):
                    for tt in range(NT):
                        r0 = sgi * SEG + tt * 128
                        norm_transpose(d_xpre[r0:r0 + 128, :], tt)

                def nt_main(sgi):
                    for tt in range(NT):
                        r0 = sgi * SEG + tt * 128
                        norm_transpose(d_xmain[r0:r0 + 128, :], tt)

                nt_pre(0)
                for sgi in range(NPRE):
                    for c in range(4):
                        proj_feature_major(2560 + c * WCB, [uT[:, 2 * c, :], uT[:, 2 * c + 1, :]], False)
                    if sgi + 1 == NPRE:
                        norm_transpose(d_xhalo[:, :], 0)
                    elif 'd' not in KF:
                        nt_pre(sgi + 1)
                    for ft in range(8):
                        if ft == 0:
                            nxt_ = ssm_front(0, False)
                        cur_ = nxt_
                        if ft + 1 < 8:
                            nxt_ = ssm_front(ft + 1, False)
                        ssm_back(cur_)
                        if sgi + 1 < NPRE and ft % 2 == 0 and 'd' in KF:
                            r0n = (sgi + 1) * SEG + (ft // 2) * 128
                            norm_stage1(d_xpre[r0n:r0n + 128, :])
                        if sgi + 1 < NPRE and ft % 2 == 1 and 'd' in KF:
                            norm_stage2(ft // 2)

                stage('prepass')
                proj_token_major(1024, lambda tt: kv_sb[0][:, 0:256], False, [0])
                proj_token_major(1280, lambda tt: kv_sb[0][:, 256:512], False, [0])
                nt_main(0)
                prep_kv(kv_sb[0], -1, 1)
                cur_slot = 1

                stage('halo')
                for sgi in range(NSEG):
                    allt = list(range(NT))
                    for c in range(4):
                        proj_token_major(c * WCB, lambda tt, c=c: q_sb[tt][:, c * WCB:(c + 1) * WCB], False, allt)
                    proj_token_major(1024, lambda tt: kv_sb[tt][:, 0:256], False, allt)
                    proj_token_major(1280, lambda tt: kv_sb[tt][:, 256:512], False, allt)
                    for c in range(4):
                        proj_token_major(1536 + c * WCB, lambda tt, c=c: sz_sb[tt][:, c * WCB:(c + 1) * WCB], True, allt)
                    for c in range(4):
                        proj_feature_major(2560 + c * WCB, [uT[:, 2 * c, :], uT[:, 2 * c + 1, :]], False)
                    for c in range(4):
                        proj_feature_major(3584 + c * WCB, [szsT[:, 2 * c, :], szsT[:, 2 * c + 1, :]], True)
                    if dbg and sgi == 0:
                        dq = qn_t
                        P.cp("vector", dq[:], q_sb[0][:])
                        P.dma(dbg_t["proj_q"][:], dq[:], q="sync")
                    if sgi + 1 < NSEG and 'c' not in KF:
                        nt_main(sgi + 1)
                    if sgi == 0:
                        stage('proj')
                    for tt in range(NT):
                        gtt = sgi * NT + tt
                        if sgi + 1 < NSEG and 'c' in KF:
                            r0n = (sgi + 1) * SEG + tt * 128
                            norm_stage1(d_xmain[r0n:r0n + 128, :])
                        prev_slot = cur_slot
                        cur_slot = 1 - cur_slot
                        prep_kv(kv_sb[tt], gtt, cur_slot)
                        stage('kv0')
                        attention_tile(tt, cur_slot, prev_slot, first=(gtt == 0))
                        if sgi + 1 < NSEG and 'c' in KF:
                            norm_stage2(tt)
                    if sgi == 0:
                        stage('attn')
                    nxt_ = ssm_front(0, True)
                    for ft in range(8):
                        cur_ = nxt_
                        if ft + 1 < 8:
                            nxt_ = ssm_front(ft + 1, True)
                        ssm_back(cur_)
                    if sgi == 0:
                        stage('ssm')
                    glu_and_gate()
                    if sgi == 0:
                        stage('glu')
                    out_proj(sgi)
                    if sgi == 0:
                        stage('seg0')

        except _Stop:
            P.es = es

        finals = [d_out.lw] + list(d_out.rd)
        for t_ in dbg_t.values():
            finals.append(t_.lw)
        P.barrier()
        with nc.Block() as block:
            P.emit(block, [f for f in finals if f is not None])
    return nc


def _host_inputs(x, positions, norm_w, w_in, q_norm_w, k_norm_w, sinks, a_re, a_im, log_step,
                 b_re, b_im, c_re, c_im, d_skip, w_glu, b_glu, attn_out_norm_w, ssm_out_norm_w, w_out):
    f32 = np.float32
    rep = lambda v: np.ascontiguousarray(np.broadcast_to(np.asarray(v, f32)[None, :], (128, len(v))))
    pl = lambda v, n: np.ascontiguousarray(np.asarray(v, f32).reshape(n, 128).T)

    def p_layout(a):
        a = np.asarray(a, f32).reshape(32, 2, 64)
        return np.ascontiguousarray(a.transpose(1, 2, 0).reshape(128, 32))

    def bc_layout(m):
        m = np.asarray(m, f32).reshape(32, 2, 64, 16)
        o = np.zeros((2, 64, 32, 2, 16), f32)
        for g2 in range(2):
            o[g2, :, :, g2, :] = m[:, g2].transpose(1, 0, 2)
        return np.ascontiguousarray(o.reshape(128, 32 * 32))

    common = {
        "w_in": np.ascontiguousarray(w_in, f32), "w_out": np.ascontiguousarray(w_out, f32), "w_glu": np.ascontiguousarray(w_glu, f32),
        "normw": pl(norm_w, 16), "outnw": pl(np.concatenate([np.asarray(attn_out_norm_w), np.asarray(ssm_out_norm_w)]), 16),
        "bglu": pl(b_glu, 8), "dskip": pl(d_skip, 8),
        "qw": rep(q_norm_w), "kw": rep(k_norm_w), "sinks": rep(sinks),
        "invf": rep((10000.0 ** (-np.arange(0, 64, 2, dtype=np.float64) / 64) / (2 * np.pi)).astype(f32)),
        "are": p_layout(a_re), "aim": p_layout(a_im),
        "lst": p_layout(np.broadcast_to(np.asarray(log_step, f32)[:, None], (64, 64))),
        "bre": bc_layout(b_re), "bim": bc_layout(b_im),
        "cre": bc_layout(np.asarray(c_re).transpose(0, 2, 1)), "cim": bc_layout(np.asarray(c_im).transpose(0, 2, 1)),
        "ident": np.eye(128, dtype=f32),
        "maskc": np.triu(np.ones((128, 128), f32)),
        "maskp": np.tril(np.ones((128, 128), f32), -1),
        "kvec": rep(np.array(KS, f32)), "jrow": rep(np.arange(-1, NB).astype(f32)),
    }
    x = np.asarray(x, f32)
    positions = np.asarray(positions, np.int32)
    maps = []
    for c in range(8):
        b, half = c // 2, c % 2
        t0 = half * TC
        m = dict(common)
        m["x_main"] = np.ascontiguousarray(x[b, t0:t0 + TC])
        pos = positions[b, t0:t0 + TC]
        m["pos_main"] = np.ascontiguousarray(pos.reshape(TC // 128, 128).T)
        if half == 1:
            m["x_pre"] = np.ascontiguousarray(x[b, 0:TC])
            m["x_halo"] = np.ascontiguousarray(x[b, t0 - 128:t0])
            m["pos_halo"] = np.ascontiguousarray(positions[b, t0 - 128:t0].reshape(128, 1))
            m["maskf"] = common["maskp"]
        else:
            m["x_pre"] = np.zeros((TC, D), f32)
            m["x_halo"] = np.zeros((128, D), f32)
            m["pos_halo"] = np.zeros((128, 1), np.int32)
            m["maskf"] = np.zeros((128, 128), f32)
        maps.append(m)
    return maps


_NC = {}


def kernel(**inputs):
    maps = _host_inputs(**inputs)
    if "nc" not in _NC:
        _NC["nc"] = build_program()
    res = run_bass_kernel_spmd(_NC["nc"], maps, core_ids=list(range(8)))
    out = np.empty((4, 4096, D), np.float32)
    for c in range(8):
        b, half = c // 2, c % 2
        out[b, half * TC:(half + 1) * TC] = res.results[c]["out"]
    return out
```

```python
import math
import os
KF = os.environ.get('KFLAGS', 'abcd')
from contextlib import ExitStack
import numpy as np
import concourse.bass as bass
import concourse.mybir as mybir
from concourse.bass_utils import run_bass_kernel_spmd

F32 = mybir.dt.float32
BF16 = mybir.dt.bfloat16
I32 = mybir.dt.int32
AF = mybir.ActivationFunctionType
ALU = mybir.AluOpType
AX = mybir.AxisListType

NDSEM = 24
S2PI = 6.28318
D = 2048
TC = 2048
SEG = 512
NT = SEG // 128
NSEG = TC // SEG
NPRE = 4
LB_ = int(os.environ.get('KBLK', '8'))
NB = SEG // LB_
IN_W = 4608
WCB = 256
KS = list(range(LB_ + 1)) + [SEG]
NK = len(KS)


class Vw:
    def __init__(s, t, ap):
        s.t = t
        s.ap = ap

    def __getitem__(s, k):
        return Vw(s.t, s.ap[k])

    def r(s, pat, **kw):
        return Vw(s.t, s.ap.rearrange(pat, **kw))

    def bc(s, shape):
        return Vw(s.t, s.ap.broadcast_to(list(shape)))

    def us(s, ax):
        return Vw(s.t, s.ap.unsqueeze(ax))


class T:
    def __init__(self, h, name):
        self.h = h
        self.name = name
        self.lw = None
        self.rd = []

    def __getitem__(self, k):
        return Vw(self, self.h[k])


def _tl(*xs):
    return [x.t for x in xs if isinstance(x, Vw)]


def _ap(x):
    return x.ap if isinstance(x, Vw) else x


class Prog:
    def __init__(self, nc, es):
        self.nc = nc
        self.es = es
        self.ops = {e: [] for e in ("tensor", "vector", "scalar", "gpsimd", "sync")}
        self.cnt = {e: 0 for e in self.ops}
        self.sem = {e: es.enter_context(nc.semaphore("s_" + e)) for e in self.ops}
        self.dsem = [es.enter_context(nc.semaphore("d%d" % i)) for i in range(NDSEM)]
        self.duse = [0] * NDSEM
        self.dpool = {"sync": list(range(0, 16)), "scalar": list(range(16, NDSEM))}
        self.dnext = {"sync": 0, "scalar": 0}
        self.seen = {e: {} for e in self.ops}
        self.nt = 0
        self.pending = []

    def sb(self, shape, dt, name=None):
        self.nt += 1
        name = (name or "t") + "_%d" % self.nt
        return T(self.es.enter_context(self.nc.sbuf_tensor(name, list(shape), dt)), name)

    def ps(self, shape, dt, name=None):
        self.nt += 1
        name = (name or "p") + "_%d" % self.nt
        return T(self.es.enter_context(self.nc.psum_tensor(name, list(shape), dt)), name)

    def dram(self, name, shape, dt, kind):
        return T(self.nc.dram_tensor(name, list(shape), dt, kind=kind).ap(), name)

    def _deps(self, eng, reads, writes):
        toks = []
        for t in reads:
            if t.lw is not None:
                toks.append(t.lw)
        for t in writes:
            if t.lw is not None:
                toks.append(t.lw)
            toks.extend(t.rd)
        out = []
        for tk in toks:
            (semkey, val, teng) = tk
            if teng == "tensor" and eng == "tensor":
                continue
            if self.seen[eng].get(semkey, 0) >= val:
                continue
            self.seen[eng][semkey] = val
            out.append(tk)
        return out

    def _commit(self, tok, reads, writes):
        for t in writes:
            t.lw = tok
            t.rd = []
        for t in reads:
            if t not in writes:
                t.rd.append(tok)
                if len(t.rd) > 12:
                    best = {}
                    for k in t.rd:
                        if k[0] not in best or best[k[0]][1] < k[1]:
                            best[k[0]] = k
                    t.rd = list(best.values())

    def op(self, eng, fn, reads=(), writes=()):
        reads = list(dict.fromkeys(reads))
        writes = list(dict.fromkeys(writes))
        waits = self._deps(eng, reads, writes)
        self.cnt[eng] += 1
        tok = (("e", eng), self.cnt[eng], eng)
        self.ops[eng].append((fn, waits, ("e", eng), 1))
        self._commit(tok, reads, writes)
        return tok

    def flush(self):
        pend, self.pending = self.pending, []
        for (o, i) in pend:
            self.dma(o, i, q="sync", _flush=False)

    def dma(self, out, in_, q="sync", _flush=True):
        if q == "store":
            self.pending.append((out, in_))
            return None
        pool = self.dpool[q]
        k = pool[self.dnext[q] % len(pool)]
        self.dnext[q] += 1
        waits = self._deps(q, [in_.t], [out.t])
        prev = self.duse[k] * 16
        if prev > 0 and self.seen[q].get(("d", k), 0) < prev:
            self.seen[q][("d", k)] = prev
            waits.append((("d", k), prev, "dma"))
        self.duse[k] += 1
        tok = (("d", k), self.duse[k] * 16, "dma")
        oa, ia = out.ap, in_.ap
        self.ops[q].append((lambda e: e.dma_start(out=oa, in_=ia), waits, ("d", k), 16))
        self._commit(tok, [in_.t], [out.t])
        if _flush:
            self.flush()
        return tok

    def barrier(self):
        self.flush()
        latest = []
        for e in self.ops:
            if self.cnt[e] > 0:
                latest.append((("e", e), self.cnt[e], e))
        for k in range(NDSEM):
            if self.duse[k] > 0:
                latest.append((("d", k), self.duse[k] * 16, "dma"))
        for e in self.ops:
            waits = []
            for tk in latest:
                if self.seen[e].get(tk[0], 0) >= tk[1]:
                    continue
                self.seen[e][tk[0]] = tk[1]
                waits.append(tk)
            self.ops[e].append((None, waits, None, 0))

    def _semh(self, key):
        return self.sem[key[1]] if key[0] == "e" else self.dsem[key[1]]

    def emit(self, block, final_tokens):
        P = self
        waited = {e: set() for e in P.ops}
        for eng in P.ops:
            for fn, waits, semkey, inc in P.ops[eng]:
                for (sk, val, _) in waits:
                    if sk[0] == "e":
                        waited[sk[1]].add(val)
        for (sk, val, _) in final_tokens:
            if sk[0] == "e":
                waited[sk[1]].add(val)
        sigmap = {}
        for eng in P.ops:
            n = 0
            sig = 0
            m = {}
            for fn, waits, semkey, inc in P.ops[eng]:
                if fn is None or semkey[0] != "e":
                    continue
                n += 1
                if n in waited[eng]:
                    sig += 1
                    m[n] = sig
            sigmap[eng] = m

        def wait(e, sk, val):
            if sk[0] == "e":
                e.wait_ge(P.sem[sk[1]], sigmap[sk[1]][val])
            else:
                e.wait_ge(P.dsem[sk[1]], val)

        def run(engname, e):
            n = 0
            for fn, waits, semkey, inc in P.ops[engname]:
                for (sk, val, _) in waits:
                    wait(e, sk, val)
                if fn is None:
                    continue
                ins = fn(e)
                if semkey[0] == "e":
                    n += 1
                    if n in sigmap[engname]:
                        ins.then_inc(P.sem[engname], 1)
                else:
                    ins.then_inc(P.dsem[semkey[1]], inc)
            if engname == "sync":
                for (sk, val, _) in final_tokens:
                    wait(e, sk, val)

        @block.sync
        def _(e):
            run("sync", e)

        @block.tensor
        def _(e):
            run("tensor", e)

        @block.vector
        def _(e):
            run("vector", e)

        @block.scalar
        def _(e):
            run("scalar", e)

        @block.gpsimd
        def _(e):
            run("gpsimd", e)

    def tt(self, eng, out, a, b, op):
        self.op(eng, lambda e: e.tensor_tensor(out=out.ap, in0=a.ap, in1=b.ap, op=op), _tl(a, b), _tl(out))

    def ts(self, eng, out, a, s1, op0, s2=None, op1=None):
        if op1 is None:
            if eng == "gpsimd":
                fn = lambda e: e.tensor_scalar(out=out.ap, in0=a.ap, scalar1=_ap(s1), scalar2=(0.0 if op0 == ALU.mult else 1.0),
                                               op0=op0, op1=(ALU.add if op0 == ALU.mult else ALU.mult))
            else:
                fn = lambda e: e.tensor_scalar(out=out.ap, in0=a.ap, scalar1=_ap(s1), scalar2=None, op0=op0)
        else:
            fn = lambda e: e.tensor_scalar(out=out.ap, in0=a.ap, scalar1=_ap(s1), scalar2=_ap(s2), op0=op0, op1=op1)
        self.op(eng, fn, _tl(a, s1, s2), _tl(out))

    def stt(self, out, a, s, b, op0, op1):
        self.op("vector", lambda e: e.scalar_tensor_tensor(out=out.ap, in0=a.ap, scalar=_ap(s), in1=b.ap, op0=op0, op1=op1),
                _tl(a, s, b), _tl(out))

    def act(self, out, a, func, scale=None, bias=None, accum=None):
        kw = {}
        if scale is not None:
            kw["scale"] = _ap(scale)
        if bias is not None:
            kw["bias"] = _ap(bias)
        if accum is not None:
            kw["accum_out"] = accum.ap
        self.op("scalar", lambda e: e.activation(out=out.ap, in_=a.ap, func=func, **kw), _tl(a, scale, bias), _tl(out, accum))

    def cp(self, eng, out, a):
        if eng == "scalar":
            self.op(eng, lambda e: e.copy(out=out.ap, in_=a.ap), _tl(a), _tl(out))
        else:
            self.op(eng, lambda e: e.tensor_copy(out=out.ap, in_=a.ap), _tl(a), _tl(out))

    def mm(self, out, l, r, start, stop, tp=None):
        kw = {} if tp is None else {"tile_position": tp}
        self.op("tensor", lambda e: e.matmul(out=out.ap, lhsT=l.ap, rhs=r.ap, start=start, stop=stop, skip_group_check=True, **kw),
                _tl(l, r), _tl(out))

    def tr(self, out, a, ident, tp=None):
        kw = {} if tp is None else {"tile_position": tp}
        self.op("tensor", lambda e: e.transpose(out=out.ap, in_=a.ap, identity=ident.ap, **kw), _tl(a, ident), _tl(out))

    def scan(self, out, d0, d1, init):
        self.op("vector", lambda e: e.tensor_tensor_scan(out=out.ap, data0=d0.ap, data1=d1.ap, initial=_ap(init), op0=ALU.mult, op1=ALU.add),
                _tl(d0, d1, init), _tl(out))

    def recip(self, out, a):
        self.op("vector", lambda e: e.reciprocal(out=out.ap, in_=a.ap), _tl(a), _tl(out))

    def red(self, out, a):
        self.op("vector", lambda e: e.tensor_reduce(out=out.ap, in_=a.ap, axis=AX.X, op=ALU.add), _tl(a), _tl(out))

    def memset(self, eng, out, val):
        self.op(eng, lambda e: e.memset(out.ap, val), [], _tl(out))


def build_program(dbg=False, stop_at=None):
    nc = bass.Bass("TRN2", target_bir_lowering=False)
    es = ExitStack()
    with es:
        P = Prog(nc, es)
        EI, EO, IN = "ExternalInput", "ExternalOutput", "Internal"
        d_xmain = P.dram("x_main", [TC, D], F32, EI)
        d_xpre = P.dram("x_pre", [NPRE * SEG, D], F32, EI)
        d_xhalo = P.dram("x_halo", [128, D], F32, EI)
        d_pos = P.dram("pos_main", [128, TC // 128], I32, EI)
        d_posh = P.dram("pos_halo", [128, 1], I32, EI)
        d_win = P.dram("w_in", [D, IN_W], F32, EI)
        d_wout = P.dram("w_out", [D, D], F32, EI)
        d_wglu = P.dram("w_glu", [1024, 1024], F32, EI)
        d_normw = P.dram("normw", [128, 16], F32, EI)
        d_outnw = P.dram("outnw", [128, 16], F32, EI)
        d_bglu = P.dram("bglu", [128, 8], F32, EI)
        d_dskip = P.dram("dskip", [128, 8], F32, EI)
        d_qw = P.dram("qw", [128, 64], F32, EI)
        d_kw = P.dram("kw", [128, 64], F32, EI)
        d_sinks = P.dram("sinks", [128, 16], F32, EI)
        d_invf = P.dram("invf", [128, 32], F32, EI)
        d_are = P.dram("are", [128, 32], F32, EI)
        d_aim = P.dram("aim", [128, 32], F32, EI)
        d_lst = P.dram("lst", [128, 32], F32, EI)
        d_bre = P.dram("bre", [128, 1024], F32, EI)
        d_bim = P.dram("bim", [128, 1024], F32, EI)
        d_cre = P.dram("cre", [128, 1024], F32, EI)
        d_cim = P.dram("cim", [128, 1024], F32, EI)
        d_ident = P.dram("ident", [128, 128], F32, EI)
        d_maskc = P.dram("maskc", [128, 128], F32, EI)
        d_maskp = P.dram("maskp", [128, 128], F32, EI)
        d_maskf = P.dram("maskf", [128, 128], F32, EI)
        d_kvec = P.dram("kvec", [128, NK], F32, EI)
        d_jrow = P.dram("jrow", [128, NB + 1], F32, EI)
        d_out = P.dram("out", [TC, D], F32, EO)
        s_win = P.dram("s_win", [D, IN_W], BF16, IN)
        s_wout = P.dram("s_wout", [D, D], BF16, IN)
        s_wglu = P.dram("s_wglu", [1024, 1024], BF16, IN)
        s_bs = P.dram("s_bs", [8, 128, 2 * LB_ * 128], BF16, IN)
        s_cs = P.dram("s_cs", [8, 128, 4 * 2 * LB_ * 32], BF16, IN)
        s_fir = P.dram("s_fir", [8, 128, LB_ * 128], BF16, IN)
        s_tab = P.dram("s_tab", [8, 128, 4 * 3 * (NB + 1)], F32, IN)
        dbg_t = {}
        if dbg:
            dbg_t["proj_q"] = P.dram("dbg_q", [128, 1024], F32, EO)

        def load_const(dram_t, shape, dt=F32, name=None):
            t = P.sb(shape, dt, name)
            P.dma(t[:], dram_t[:])
            return t

        identf = load_const(d_ident, [128, 128], name="identf")
        idb = P.sb([128, 128], BF16, "idb")
        P.cp("vector", idb[:], identf[:])
        normw = load_const(d_normw, [128, 16], name="normw")
        outnw = load_const(d_outnw, [128, 16], name="outnw")
        bglu = load_const(d_bglu, [128, 8], name="bglu")
        dskip = load_const(d_dskip, [128, 8], name="dskip")
        qw = load_const(d_qw, [128, 64], name="qw")
        kw = load_const(d_kw, [128, 64], name="kw")
        invf = load_const(d_invf, [128, 32], name="invf")
        sinks = load_const(d_sinks, [128, 16], name="sinks")
        esink = P.sb([128, 16], F32, "esink")
        P.act(esink[:], sinks[:], AF.Exp)
        mk_f = load_const(d_maskc, [128, 128])
        maskc = P.sb([128, 128], BF16, "maskc")
        P.cp("vector", maskc[:], mk_f[:])
        mk_f2 = load_const(d_maskp, [128, 128])
        maskp = P.sb([128, 128], BF16, "maskp")
        P.cp("vector", maskp[:], mk_f2[:])
        mk_f3 = load_const(d_maskf, [128, 128])
        maskf = P.sb([128, 128], BF16, "maskf")
        P.cp("vector", maskf[:], mk_f3[:])
        mask2 = P.sb([128, 2, 2, 128], BF16, "mask2")
        mask2f = P.sb([128, 2, 2, 128], BF16, "mask2f")
        for jj_ in range(2):
            P.cp("vector", mask2[:, 0, jj_, :], mk_f2[:])
            P.cp("vector", mask2[:, 1, jj_, :], mk_f[:])
            P.cp("vector", mask2f[:, 0, jj_, :], mk_f3[:])
            P.cp("vector", mask2f[:, 1, jj_, :], mk_f[:])
        posi = load_const(d_pos, [128, TC // 128], I32)
        posf = P.sb([128, TC // 128], F32, "posf")
        P.cp("vector", posf[:], posi[:])
        poshi = load_const(d_posh, [128, 1], I32)
        poshf = P.sb([128, 1], F32, "poshf")
        P.cp("vector", poshf[:], poshi[:])
        ones_b = P.sb([128, 1], BF16, "ones_b")
        P.memset("vector", ones_b[:], 1.0)
        NTL = TC // 128 + 1
        cos_all = P.sb([128, NTL, 32], F32, "cos_all")
        sin_all = P.sb([128, NTL, 32], F32, "sin_all")
        r8 = P.sb([128, 32], F32, "r8")
        c512 = P.sb([128, 32], F32, "c512")
        s512 = P.sb([128, 32], F32, "s512")
        g0re_t = [P.sb([128, 4], F32, "g0re") for _ in range(8)]
        g0im_t = [P.sb([128, 4], F32, "g0im") for _ in range(8)]
        for g_ in g0re_t + g0im_t:
            P.memset("vector", g_[:], 0.0)

        pA = P.ps([128, 512], F32, "pA")
        pB = P.ps([128, 512], F32, "pB")
        pT = P.ps([128, 1024], BF16, "pT")
        pT2 = P.ps([128, 1024], BF16, "pT2")
        pS = P.ps([128, 512], F32, "pS")
        pO = P.ps([128, 512], F32, "pO")
        pY = P.ps([128, 512], F32, "pY")
        pM = P.ps([128, 512], F32, "pM")

        def phasor(f, cos_out, sin_out, n, scope_es):
            old = P.es
            P.es = scope_es
            fi = P.sb([128, n], I32)
            ff = P.sb([128, n], F32)
            f1 = P.sb([128, n], F32)
            f2 = P.sb([128, n], F32)
            P.es = old
            P.cp("vector", fi[:], f)
            P.cp("vector", ff[:], fi[:])
            P.tt("vector", f1[:], f, ff[:], ALU.subtract)
            P.act(sin_out, f1[:], AF.Sin, scale=S2PI)
            P.ts("vector", f2[:], f1[:], 0.25, ALU.add)
            P.cp("vector", fi[:], f2[:])
            P.cp("vector", ff[:], fi[:])
            P.tt("vector", f1[:], f2[:], ff[:], ALU.subtract)
            P.act(cos_out, f1[:], AF.Sin, scale=S2PI)

        class _Stop(Exception):
            pass

        def stage(name):
            if stop_at == name:
                raise _Stop()

        try:
            stage('wconv')
            with ExitStack() as es2:
                P.es = es2
                are = load_const(d_are, [128, 32])
                aim = load_const(d_aim, [128, 32])
                lst = load_const(d_lst, [128, 32])
                kvec = load_const(d_kvec, [128, NK])
                jrow = load_const(d_jrow, [128, NB + 1])
                dl = P.sb([128, 32], F32)
                P.act(dl[:], lst[:], AF.Exp)
                ard = P.sb([128, 32], F32)
                P.tt("vector", ard[:], are[:], dl[:], ALU.mult)
                th = P.sb([128, 32], F32)
                P.stt(th[:], aim[:], 1.0 / (2.0 * math.pi), dl[:], ALU.mult, ALU.mult)
                fk = P.sb([128, NK, 32], F32)
                ek = P.sb([128, NK, 32], F32)
                kb = kvec[:].us(2).bc([128, NK, 32])
                P.tt("vector", fk[:], th[:].us(1).bc([128, NK, 32]), kb, ALU.mult)
                P.tt("vector", ek[:], ard[:].us(1).bc([128, NK, 32]), kb, ALU.mult)
                rk = P.sb([128, NK, 32], F32)
                fl = lambda t: t[:].r("p a b -> p (a b)")
                P.act(fl(rk), fl(ek), AF.Exp)
                ck = P.sb([128, NK, 32], F32)
                sk = P.sb([128, NK, 32], F32)
                phasor(fl(fk), fl(ck), fl(sk), NK * 32, es2)
                Lr = P.sb([128, NK, 32], F32)
                Li = P.sb([128, NK, 32], F32)
                P.tt("vector", Lr[:], rk[:], ck[:], ALU.mult)
                P.tt("vector", Li[:], rk[:], sk[:], ALU.mult)
                P.cp("vector", r8[:], rk[:, LB_, :])
                P.cp("vector", c512[:], ck[:, LB_ + 1, :])
                P.cp("vector", s512[:], sk[:, LB_ + 1, :])
                nre = P.sb([128, 32], F32)
                P.ts("vector", nre[:], Lr[:, 1, :], -1.0, ALU.add)
                nim = Li[:, 1, :]
                t1 = P.sb([128, 32], F32)
                t2 = P.sb([128, 32], F32)
                den = P.sb([128, 32], F32)
                P.tt("vector", t1[:], are[:], are[:], ALU.mult)
                P.tt("vector", t2[:], aim[:], aim[:], ALU.mult)
                P.tt("vector", den[:], t1[:], t2[:], ALU.add)
                rden = P.sb([128, 32], F32)
                P.recip(rden[:], den[:])
                qre = P.sb([128, 32], F32)
                qim = P.sb([128, 32], F32)
                P.tt("vector", t1[:], nre[:], are[:], ALU.mult)
                P.tt("vector", t2[:], nim, aim[:], ALU.mult)
                P.tt("vector", t1[:], t1[:], t2[:], ALU.add)
                P.tt("vector", qre[:], t1[:], rden[:], ALU.mult)
                P.tt("vector", t1[:], nim, are[:], ALU.mult)
                P.tt("vector", t2[:], nre[:], aim[:], ALU.mult)
                P.tt("vector", t1[:], t1[:], t2[:], ALU.subtract)
                P.tt("vector", qim[:], t1[:], rden[:], ALU.mult)

                with ExitStack() as es3:
                    P.es = es3
                    posall = P.sb([128, NTL], F32)
                    P.cp("vector", posall[:, 0:1], poshf[:])
                    P.cp("vector", posall[:, 1:NTL], posf[:])
                    fro = P.sb([128, NTL, 32], F32)
                    P.tt("vector", fro[:], posall[:].us(2).bc([128, NTL, 32]), invf[:].us(1).bc([128, NTL, 32]), ALU.mult)
                    phasor(fro[:].r("p a b -> p (a b)"), cos_all[:].r("p a b -> p (a b)"), sin_all[:].r("p a b -> p (a b)"), NTL * 32, es3)
                    P.es = es2
                    P.barrier()
                P.es = es2
                with ExitStack() as es3:
                    P.es = es3
                    NJ = NB + 1
                    fj = P.sb([128, 32, NJ], F32)
                    P.tt("vector", fj[:], fk[:, LB_, :].us(2).bc([128, 32, NJ]), jrow[:].us(1).bc([128, 32, NJ]), ALU.mult)
                    tab = P.sb([128, 3, 32, NJ], F32, "tab")
                    phasor(fj[:].r("p a b -> p (a b)"), tab[:, 0].r("p a b -> p (a b)"), tab[:, 1].r("p a b -> p (a b)"), 32 * NJ, es3)
                    P.memset("vector", tab[:, 2].r("p a b -> p (a b)"), 0.0)
                    P.cp("vector", tab[:, 2, :, 1:NJ], r8[:].us(2).bc([128, 32, NB]))
                    for ft in range(8):
                        P.dma(s_tab[ft].r("p (c a b) -> p c a b", c=3, a=4), tab[:, :, 4 * ft:4 * ft + 4, :], q="sync")
                    P.es = es2
                    P.barrier()
                P.es = es2
                P.barrier()

                NSTG = 4
                stg = [P.sb([128, 1536], F32, "stg") for _ in range(NSTG)]
                stb = [P.sb([128, 1536], BF16, "stb") for _ in range(NSTG)]

                def wjob(src_t, dst_t, r, c0, n, scal):
                    return (src_t[r * 128:(r + 1) * 128, c0:c0 + n], dst_t[r * 128:(r + 1) * 128, c0:c0 + n], scal, n)

                jobs = [wjob(d_win, s_win, dk, c * 1536, 1536, normw[:, dk:dk + 1]) for dk in range(16) for c in (1, 2, 0)]
                jobs += [wjob(d_wglu, s_wglu, ft, 0, 1024, None) for ft in range(8)]
                jobs += [wjob(d_wout, s_wout, ft, c * 1024, 1024, outnw[:, ft:ft + 1]) for ft in range(16) for c in range(2)]
                nld = 0
                for i in range(len(jobs)):
                    while nld < len(jobs) and nld < i + NSTG:
                        P.dma(stg[nld % NSTG][:, 0:jobs[nld][3]], jobs[nld][0], q="scalar")
                        nld += 1
                    src, dst, scal, n = jobs[i]
                    if scal is None:
                        P.cp("scalar", stb[i % NSTG][:, 0:n], stg[i % NSTG][:, 0:n])
                    else:
                        P.act(stb[i % NSTG][:, 0:n], stg[i % NSTG][:, 0:n], AF.Copy, scale=scal)
                    P.dma(dst, stb[i % NSTG][:, 0:n], q="scalar")
                bre = load_const(d_bre, [128, 1024])
                bim = load_const(d_bim, [128, 1024])
                cre = load_const(d_cre, [128, 1024])
                cim = load_const(d_cim, [128, 1024])
                cm_tmps = {"vector": [P.sb([128, 32, 32], F32) for _ in range(4)]}

                def cmul(ore, oim, are_, aim_, sre, sim, neg_im=False, eng="vector"):
                    sr = sre.us(2).bc([128, 32, 32])
                    si = sim.us(2).bc([128, 32, 32])
                    u1, u2, u3, u4 = cm_tmps[eng]
                    P.tt(eng, u1[:], are_, sr, ALU.mult)
                    P.tt(eng, u2[:], aim_, si, ALU.mult)
                    P.tt(eng, ore, u1[:], u2[:], ALU.subtract)
                    P.tt(eng, u3[:], aim_, sr, ALU.mult)
                    P.tt(eng, u4[:], are_, si, ALU.mult)
                    if neg_im:
                        P.tt(eng, u3[:], u3[:], u4[:], ALU.add)
                        P.ts(eng, oim, u3[:], -1.0, ALU.mult)
                    else:
                        P.tt(eng, oim, u3[:], u4[:], ALU.add)

                b3 = lambda t: t[:].r("p (a b) -> p a b", b=32)
                Bbre = P.sb([128, 32, 32], F32)
                Bbim = P.sb([128, 32, 32], F32)
                cmul(Bbre[:], Bbim[:], b3(bre), b3(bim), qre[:], qim[:])
                LB = P.sb([128, LB_, 2, 32, 32], BF16, "LB")
                CS = P.sb([128, LB_, 2, 32, 32], BF16, "CS")
                cbf = P.sb([128, 2, 32, 32], BF16, "cbf")
                P.cp("vector", cbf[:, 0], b3(cre))
                P.ts("vector", cbf[:, 1], b3(cim), -1.0, ALU.mult)
                for k in range(LB_):
                    cmul(LB[:, k, 0], LB[:, k, 1], Bbre[:], Bbim[:], Lr[:, k, :], Li[:, k, :])
                for i in range(LB_):
                    cmul(CS[:, i, 0], CS[:, i, 1], b3(cre), b3(cim), Lr[:, i + 1, :], Li[:, i + 1, :], neg_im=True,
                         eng="vector")
                for ft in range(8):
                    for k in range(4):
                        P.dma(s_cs[ft][:, k * 64 * LB_:(k + 1) * 64 * LB_].r("p (i r h) -> p i r h", i=LB_, r=2),
                              CS[:, :, :, 4 * ft + k, :], q="sync")
                fir = P.sb([128, 8, LB_, 128], BF16, "fir")
                P.memset("vector", fir[:].r("p a b c -> p (a b c)"), 0.0)
                bsb = [P.sb([128, 2 * LB_, 128], BF16, "bsb") for _ in range(2)]
                pTs = [pT, pT2]
                for ft in range(8):
                    for hf in range((2 * LB_ + 7) // 8):
                        pt = pTs[hf % 2]
                        pt3 = pt[:].r("p (a b) -> p a b", b=128)
                        nsl = min(8, 2 * LB_ - hf * 8)
                        for k in range(4):
                            gp = 4 * ft + k
                            for il in range(nsl // 2):
                                i = hf * 4 + il
                                for ri in range(2):
                                    P.tr(pt3[32 * k:32 * k + 32, il * 2 + ri, :], LB[:, LB_ - 1 - i, ri, gp, :], idb[:], tp=(0, 32 * k))
                        P.cp("vector" if hf == 0 else "scalar", bsb[ft % 2][:, hf * 8:hf * 8 + nsl, :].r("p a b -> p (a b)"), pt[:, 0:nsl * 128])
                    P.dma(s_bs[ft], bsb[ft % 2][:].r("p a b -> p (a b)"), q="sync")
                    pk3 = pM[:, 0:32 * LB_].r("p (d h) -> p d h", h=32)
                    for k in range(4):
                        gp = 4 * ft + k
                        for d in range(LB_):
                            P.mm(pk3[32 * k:32 * k + 32, d, :], LB[:, d, 0, gp, :], cbf[:, 0, gp, :], start=(d == 0), stop=False, tp=(0, 32 * k))
                            P.mm(pk3[32 * k:32 * k + 32, d, :], LB[:, d, 1, gp, :], cbf[:, 1, gp, :], start=False, stop=True, tp=(0, 32 * k))
                    for k in range(4):
                        P.cp("vector", fir[32 * k:32 * k + 32, ft, :, 32 * k:32 * k + 32], pk3[32 * k:32 * k + 32, :, :])
                    P.dma(s_fir[ft], fir[:, ft].r("p a b -> p (a b)"), q="sync")
                P.es = es
                P.barrier()
            P.es = es
            P.barrier()

            xt = [P.sb([128, D], F32, "xt") for _ in range(1)]
            xn = P.sb([128, D], BF16, "xn")
            xnTs = [P.sb([128, 8, SEG], BF16, "xnT") for _ in range(2)]
            wbuf = [P.sb([128, 16, WCB], BF16, "wbuf") for _ in range(2)]
            uT = P.sb([128, 8, SEG], BF16, "uT")
            szsT = P.sb([128, 8, SEG], BF16, "szsT")
            ygT = P.sb([128, 8, SEG], BF16, "ygT")
            maT = P.sb([128, 8, SEG], BF16, "maT")
            q_sb = [P.sb([128, 1024], BF16, "q_sb") for _ in range(NT)]
            kv_sb = [P.sb([128, 512], BF16, "kv_sb") for _ in range(NT)]
            sz_sb = [P.sb([128, 1024], BF16, "sz_sb") for _ in range(NT)]
            rstd_a = P.sb([128, NT], F32, "rstd_a")
            rstd_s = P.sb([128, NT], F32, "rstd_s")
            kT = [P.sb([128, 4, 128], BF16, "kT") for _ in range(2)]
            vext = [P.sb([128, 4, 65], BF16, "vext") for _ in range(2)]
            for v_ in vext:
                P.memset("vector", v_[:].r("p a b -> p (a b)"), 1.0)
            bs_t = [P.sb([128, 2 * LB_, 128], BF16, "bs_t") for _ in range(2)]
            cs_t = [P.sb([128, 4, LB_, 2, 32], BF16, "cs_t") for _ in range(2)]
            fir_t = [P.sb([128, LB_, 128], BF16, "fir_t") for _ in range(2)]
            NTAB = 3
            tab_t = [P.sb([128, 3, 4, NB + 1], F32, "tab_t") for _ in range(NTAB)]
            wctr = [0]
            sctr = [0]
            xctr = [0]
            pctr = [0]

            def next_wbuf():
                w = wbuf[wctr[0] % 2]
                wctr[0] += 1
                return w

            def mm_bank():
                b = (pA, pB)[pctr[0] % 2]
                pctr[0] += 1
                return b

            def small(shape, dt=F32, name="sm"):
                return P.sb(shape, dt, name)

            ss = small([128, 1]); ms = small([128, 1]); sd = small([128, 1]); rstd = small([128, 1])
            mhalf = small([128, 1])
            P.memset("vector", mhalf[:], -0.5)

            def norm_stage1(src_rows):
                x_ = xt[0]
                xctr[0] += 1
                P.dma(x_[:], src_rows)
                P.act(xn[:], x_[:], AF.Square, accum=ss[:])
                P.ts("gpsimd", ms[:], ss[:], 1.0 / D, ALU.mult, 1e-6, ALU.add)
                P.tt("gpsimd", rstd[:], ms[:], mhalf[:], ALU.pow)
                P.act(xn[:], x_[:], AF.Copy, scale=rstd[:])

            def norm_transpose(src_rows, tt_slot):
                norm_stage1(src_rows)
                norm_stage2(tt_slot)

            def norm_stage2(tt_slot):
                for hf in range(2):
                    pt = (pT, pT2)[hf]
                    for j in range(8):
                        dk = hf * 8 + j
                        P.tr(pt[:, j * 128:(j + 1) * 128], xn[:, dk * 128:(dk + 1) * 128], idb[:])
                    P.cp("vector" if hf == 0 else "scalar",
                         xnTs[hf][:, :, tt_slot * 128:(tt_slot + 1) * 128],
                         pt[:].r("p (a b) -> p a b", b=128))

            def load_w(scr, col0, nk, ncol=WCB, row0=0):
                w = next_wbuf()
                src = scr[row0:row0 + nk * 128, col0:col0 + ncol].r("(k p) c -> p k c", p=128)
                P.dma(w[:, 0:nk, 0:ncol], src)
                return w

            qn_t = small([128, 1024]); sq_t = small([128, 1024]); ssq = small([128, 16]); rq = small([128, 16])
            ta = small([128, 512]); tb = small([128, 512]); tc_ = small([128, 512]); td = small([128, 512])
            qr = small([128, 1024], BF16); kr = small([128, 256], BF16); kdup = small([128, 4, 128], BF16)
            qT = small([128, 8, 128], BF16)
            rope_sel = [0]
            pe2 = [small([128, 1024], BF16) for _ in range(2)]; pm2 = [small([128, 1024], BF16) for _ in range(2)]
            den4 = small([128, 4]); rden4 = small([128, 4]); o_t = sq_t; oa_t = small([128, 1024], BF16)
            osq = qr; ssa = small([128, 1]);

            def qk_norm_rope(src, nh, w_t, out_bf):
                n = nh * 64
                s3 = lambda v: v.r("p (h d) -> p h d", d=64)
                P.tt("vector", sq_t[:, 0:n], src, src, ALU.mult)
                P.red(ssq[:, 0:nh], s3(sq_t[:, 0:n]))
                P.ts("vector", ssq[:, 0:nh], ssq[:, 0:nh], 1.0 / 64, ALU.mult, 1e-6, ALU.add)
                P.tt("gpsimd", rq[:, 0:nh], ssq[:, 0:nh], mhalf[:].bc([128, nh]), ALU.pow)
                P.tt("vector", s3(qn_t[:, 0:n]), s3(src), rq[:, 0:nh].us(2).bc([128, nh, 64]), ALU.mult)
                P.tt("vector", s3(qn_t[:, 0:n]), s3(qn_t[:, 0:n]), w_t[:].us(1).bc([128, nh, 64]), ALU.mult)
                q4 = qn_t[:, 0:n].r("p (h t d) -> p h t d", t=2, d=32)
                o4 = out_bf.r("p (h t d) -> p h t d", t=2, d=32)
                cb_ = cos_all[:, rope_sel[0], :].us(1).bc([128, nh, 32])
                sb_ = sin_all[:, rope_sel[0], :].us(1).bc([128, nh, 32])
                m = nh * 32
                v3 = lambda t: t[:, 0:m].r("p (h d) -> p h d", d=32)
                P.tt("vector", v3(ta), q4[:, :, 0, :], cb_, ALU.mult)
                P.tt("vector", v3(tb), q4[:, :, 1, :], sb_, ALU.mult)
                P.tt("vector", o4[:, :, 0, :], v3(ta), v3(tb), ALU.subtract)
                P.tt("vector", v3(tc_), q4[:, :, 1, :], cb_, ALU.mult)
                P.tt("vector", v3(td), q4[:, :, 0, :], sb_, ALU.mult)
                P.tt("vector", o4[:, :, 1, :], v3(tc_), v3(td), ALU.add)

            if True:

                _ph = {}

                def phasor_small(f, cos_out, sin_out):
                    if "fi" not in _ph:
                        _ph["fi"] = P.sb([128, 32], I32); _ph["ff"] = P.sb([128, 32], F32)
                        _ph["f1"] = P.sb([128, 32], F32); _ph["f2"] = P.sb([128, 32], F32)
                    fi, ff, f1, f2 = _ph["fi"], _ph["ff"], _ph["f1"], _ph["f2"]
                    P.cp("vector", fi[:], f)
                    P.cp("vector", ff[:], fi[:])
                    P.tt("vector", f1[:], f, ff[:], ALU.subtract)
                    P.act(sin_out, f1[:], AF.Sin, scale=S2PI)
                    P.ts("vector", f2[:], f1[:], 0.25, ALU.add)
                    P.cp("vector", fi[:], f2[:])
                    P.cp("vector", ff[:], fi[:])
                    P.tt("vector", f1[:], f2[:], ff[:], ALU.subtract)
                    P.act(cos_out, f1[:], AF.Sin, scale=S2PI)

                def rope_tables2(pos_col):
                    pass

                def prep_kv(kv, gtt, slot):
                    rope_sel[0] = gtt + 1
                    qk_norm_rope(kv[:, 0:256], 4, kw, kr[:])
                    k3 = kr[:].r("p (g d) -> p g d", d=64)
                    P.cp("vector", kdup[:, :, 0:64], k3)
                    P.cp("gpsimd", kdup[:, :, 64:128], k3)
                    for g in range(4):
                        P.tr(pT[:, g * 128:(g + 1) * 128], kdup[:, g, :], idb[:])
                    P.cp("scalar", kT[slot][:].r("p a b -> p (a b)"), pT[:, 0:512])
                    P.cp("vector", vext[slot][:, :, 0:64], kv[:, 256:512].r("p (g d) -> p g d", d=64))

                def attention_tile(tt, slot_cur, slot_prev, first):
                    qk_norm_rope(q_sb[tt][:], 16, qw, qr[:])
                    stage('qrope')
                    for j in range(8):
                        P.tr(pT2[:, j * 128:(j + 1) * 128], qr[:, j * 128:(j + 1) * 128], idb[:])
                    P.cp("scalar", qT[:].r("p a b -> p (a b)"), pT2[:])
                    stage('qT')
                    slots = (slot_prev, slot_cur)
                    mk2 = mask2f if first else mask2
                    sets = ((pS, pM), (pA, pB))

                    def scores(g):
                        bE, bO = sets[g % 2]
                        for kb in range(2):
                            for j in range(4):
                                h = 4 * g + j
                                base = 64 * (h % 2)
                                bank = bE if (h % 2 == 0) else bO
                                c0 = (kb * 2 + j // 2) * 128
                                P.mm(bank[:, c0:c0 + 128], kT[slots[kb]][base:base + 64, g, :], qT[base:base + 64, h // 2, :],
                                     start=True, stop=True)

                    scores(0)
                    for g in range(4):
                        bE, bO = sets[g % 2]
                        pe_ = pe2[g % 2]
                        pm_ = pm2[g % 2]
                        oB = (pO, pY)[g % 2]
                        for bi, bank in enumerate((bE, bO)):
                            P.act(pe_[:, bi * 512:(bi + 1) * 512], bank[:], AF.Exp, scale=0.125)
                        for bi in range(2):
                            P.tt("vector", pm_[:, bi * 512:(bi + 1) * 512], pe_[:, bi * 512:(bi + 1) * 512],
                                 mk2[:].r("p k j q -> p (k j q)"), ALU.mult)
                        if g + 1 < 4:
                            scores(g + 1)
                        n_ = 0
                        for j in range(4):
                            for kb in range(2):
                                c0 = (j % 2) * 512 + (kb * 2 + j // 2) * 128
                                P.mm(oB[:, j * 65:(j + 1) * 65], pm_[:, c0:c0 + 128], vext[slots[kb]][:, g, :],
                                     start=(n_ == 0), stop=(n_ == 7))
                                n_ += 1
                        o3 = oB[:, 0:260].r("p (j d) -> p j d", d=65)
                        P.tt("vector", den4[:], o3[:, :, 64], esink[:, 4 * g:4 * g + 4], ALU.add)
                        P.recip(rden4[:], den4[:])
                        P.tt("vector", o_t[:, g * 256:(g + 1) * 256].r("p (j d) -> p j d", d=64), o3[:, :, 0:64],
                             rden4[:].us(2).bc([128, 4, 64]), ALU.mult)
                    stage('onorm')
                    P.tt("vector", oa_t[:], o_t[:], sz_sb[tt][:], ALU.mult)
                    P.act(osq[:], oa_t[:], AF.Square, accum=ssa[:])
                    P.ts("gpsimd", ssa[:], ssa[:], 1.0 / 1024, ALU.mult, 1e-6, ALU.add)
                    P.tt("gpsimd", rstd_a[:, tt:tt + 1], ssa[:], mhalf[:], ALU.pow)
                    for j in range(8):
                        P.tr(pT[:, j * 128:(j + 1) * 128], oa_t[:, j * 128:(j + 1) * 128], idb[:])
                    P.cp("scalar", maT[:, :, tt * 128:(tt + 1) * 128], pT[:].r("p (a b) -> p a b", b=128))

                NBX = NB + 1
                Sft = [small([128, 4, 2 * NB]) for _ in range(2)]
                Xre = [small([128, 4, NBX])] * 2
                Xim = [small([128, 4, NBX])] * 2
                Gre = [small([128, 4, NBX])] * 2
                Gim = [small([128, 4, NBX])] * 2
                hre = [small([128, 4, NB], BF16) for _ in range(2)]
                him = [small([128, 4, NB], BF16) for _ in range(2)]
                m1 = small([128, 4, NB]); m2 = small([128, 4, NB]); m3 = small([128, 4, NB]); m4 = small([128, 4, NB])
                ysb = small([128, SEG]); y2 = small([128, SEG]); y3 = y2; sg = small([128, SEG]); sg2 = small([128, SEG])
                cn1 = small([128, 4]); cn2 = small([128, 4]); cn3 = small([128, 4]); cn4 = small([128, 4])
                fctr = [0]

                def load_blob(ft, full):
                    s = sctr[0] % 2
                    sctr[0] += 1
                    P.dma(bs_t[s][:].r("p a b -> p (a b)"), s_bs[ft])
                    P.dma(tab_t[(sctr[0] - 1) % NTAB][:].r("p c a b -> p (c a b)"), s_tab[ft])
                    if full:
                        P.dma(cs_t[s][:].r("p k i r h -> p (k i r h)"), s_cs[ft])
                        P.dma(fir_t[s][:].r("p a b -> p (a b)"), s_fir[ft])
                    return s

                def ssm_ft(ft, full):
                    ssm_back(ssm_front(ft, full))

                def ssm_front(ft, full):
                    s = load_blob(ft, full)
                    z = fctr[0] % 2
                    fctr[0] += 1
                    yb = (pY, pM)[z]
                    u3 = uT[:, ft, :].r("p (j i) -> p j i", i=LB_)
                    y3v = yb[:].r("p (j i) -> p j i", i=LB_)
                    if full:
                        for d in range(LB_):
                            for i in range(d, LB_):
                                P.mm(y3v[:, :, i], fir_t[s][:, d, :], u3[:, :, i - d], start=(d == 0 and i == 0), stop=False)
                    sbanks = (pO, pS, pA, pB)
                    for ri in range(2):
                        for i in range(LB_):
                            for k in range(4):
                                base = 32 * k
                                P.mm(sbanks[k][:, 256 + ri * NB:256 + (ri + 1) * NB], bs_t[s][base:base + 32, i * 2 + ri, :],
                                     u3[base:base + 32, :, i], start=(i == 0), stop=(i == LB_ - 1), tp=(base, 0))
                    for k in range(4):
                        P.cp("scalar", Sft[z][:, k, :], sbanks[k][:, 256:256 + 2 * NB])
                    return (ft, full, s, z, tab_t[(sctr[0] - 1) % NTAB])

                def ssm_back(st_):
                    ft, full, s, z, tb_ = st_
                    yb = (pY, pM)[z]
                    y3v = yb[:].r("p (j i) -> p j i", i=LB_)
                    g4 = slice(4 * ft, 4 * ft + 4)
                    sre_p = Sft[z][:, :, 0:NB]
                    sim_p = Sft[z][:, :, NB:2 * NB]
                    cM = tb_[:, 0, :, 1:NB + 1]
                    sM = tb_[:, 1, :, 1:NB + 1]
                    cD = tb_[:, 0, :, 0:NB]
                    sD = tb_[:, 1, :, 0:NB]
                    P.tt("vector", m1[:], sre_p, cM, ALU.mult)
                    P.tt("vector", m2[:], sim_p, sM, ALU.mult)
                    P.tt("vector", m3[:], sim_p, cM, ALU.mult)
                    P.tt("vector", m4[:], sre_p, sM, ALU.mult)
                    P.cp("vector", Xre[z][:, :, 0], g0re_t[ft][:])
                    P.cp("vector", Xim[z][:, :, 0], g0im_t[ft][:])
                    P.tt("vector", Xre[z][:, :, 1:NBX], m1[:], m2[:], ALU.add)
                    P.tt("vector", Xim[z][:, :, 1:NBX], m3[:], m4[:], ALU.subtract)
                    fl = lambda v: v.r("p a b -> p (a b)")
                    cf = fl(tb_[:, 2, :, :])
                    P.scan(fl(Gre[z][:]), cf, fl(Xre[z][:]), 0.0)
                    P.scan(fl(Gim[z][:]), cf, fl(Xim[z][:]), 0.0)
                    P.tt("gpsimd", cn1[:], Gre[z][:, :, NB], c512[:, g4], ALU.mult)
                    P.tt("gpsimd", cn2[:], Gim[z][:, :, NB], s512[:, g4], ALU.mult)
                    P.tt("gpsimd", cn3[:], Gre[z][:, :, NB], s512[:, g4], ALU.mult)
                    P.tt("gpsimd", cn4[:], Gim[z][:, :, NB], c512[:, g4], ALU.mult)
                    P.tt("gpsimd", g0re_t[ft][:], cn1[:], cn2[:], ALU.subtract)
                    P.tt("gpsimd", g0im_t[ft][:], cn3[:], cn4[:], ALU.add)
                    if not full:
                        return
                    P.tt("vector", m1[:], Gre[z][:, :, 0:NB], cD, ALU.mult)
                    P.tt("vector", m2[:], Gim[z][:, :, 0:NB], sD, ALU.mult)
                    P.tt("vector", m3[:], Gre[z][:, :, 0:NB], sD, ALU.mult)
                    P.tt("vector", m4[:], Gim[z][:, :, 0:NB], cD, ALU.mult)
                    P.tt("vector", hre[z][:], m1[:], m2[:], ALU.subtract)
                    P.tt("vector", him[z][:], m3[:], m4[:], ALU.add)
                    for i in range(LB_):
                        for k in range(4):
                            base = 32 * k
                            P.mm(y3v[base:base + 32, :, i], cs_t[s][:, k, i, 0, :], hre[z][:, k, :], start=False, stop=False, tp=(0, base))
                        for k in range(4):
                            base = 32 * k
                            P.mm(y3v[base:base + 32, :, i], cs_t[s][:, k, i, 1, :], him[z][:, k, :], start=False, stop=(k == 3 and i == LB_ - 1), tp=(0, base))
                    P.stt(ysb[:], uT[:, ft, :], dskip[:, ft:ft + 1], yb[:], ALU.mult, ALU.add)
                    P.tt("vector", y2[:], ysb[:], ysb[:], ALU.mult)
                    P.ts("vector", y2[:], y2[:], 0.044715, ALU.mult, 1.0, ALU.add)
                    P.tt("vector", y3[:], y2[:], ysb[:], ALU.mult)
                    P.act(sg[:], y3[:], AF.Sigmoid, scale=2.0 * math.sqrt(2.0 / math.pi))
                    P.tt("vector", ygT[:, ft, :], ysb[:], sg[:], ALU.mult)

                def glu_and_gate():
                    pend = []

                    def flush_n1():
                        for (fo_, oq__) in pend:
                            for tt in range(NT):
                                P.mm(pM[:, 256 + tt:256 + tt + 1], oq__[:, tt * 128:(tt + 1) * 128], ones_b[:],
                                     start=(fo_ == 0 and tt == 0), stop=(fo_ == 7))
                        del pend[:]

                    for cb in range(4):
                        w = load_w(s_wglu, cb * WCB, 8)
                        for f2 in range(2):
                            fo = cb * 2 + f2
                            bank = mm_bank()
                            for fi_ in range(8):
                                P.mm(bank[:], w[:, fi_, f2 * 128:(f2 + 1) * 128], ygT[:, fi_, :], start=(fi_ == 0), stop=(fi_ == 7))
                            flush_n1()
                            gi_ = (fo % 2) if 'b' in KF else 0
                            sg_ = (sg, sg2)[gi_]
                            ys_ = (ysb, y2)[gi_]
                            oq_ = (pm2[0], pm2[1])[gi_]
                            P.act(sg_[:], bank[:], AF.Sigmoid, bias=bglu[:, fo:fo + 1])
                            P.tt("vector", ys_[:], sg_[:], ygT[:, fo, :], ALU.mult)
                            P.tt("vector", szsT[:, fo, :], ys_[:], szsT[:, fo, :], ALU.mult)
                            P.act(oq_[:, 0:SEG], szsT[:, fo, :], AF.Square)
                            pend.append((fo, oq_))
                    flush_n1()
                    P.ts("vector", rstd_s[:], pM[:, 256:256 + NT], 1.0 / 1024, ALU.mult, 1e-6, ALU.add)
                    P.tt("gpsimd", rstd_s[:], rstd_s[:], mhalf[:].bc([128, NT]), ALU.pow)

                xres = [small([128, WCB]) for _ in range(4)]
                ot = [small([128, WCB]) for _ in range(4)]
                octr = [0]

                def out_proj(seg):
                    for cb in range(D // WCB):
                        w = load_w(s_wout, cb * WCB, 16)
                        for tt in range(NT):
                            row0 = seg * SEG + tt * 128
                            b = octr[0] % 4
                            octr[0] += 1
                            P.dma(xres[b][:], d_xmain[row0:row0 + 128, cb * WCB:(cb + 1) * WCB])
                            obanks = (pA, pB, pS, pO, pY, pM) if 'a' in KF else (pA, pB, pA, pB, pA, pB)
                            b1 = obanks[(2 * octr[0]) % 6]
                            for ft in range(8):
                                P.mm(b1[:, 0:WCB], maT[:, ft, tt * 128:(tt + 1) * 128], w[:, ft, :], start=(ft == 0), stop=(ft == 7))
                            b2 = obanks[(2 * octr[0] + 1) % 6]
                            for ft in range(8):
                                P.mm(b2[:, 0:WCB], szsT[:, ft, tt * 128:(tt + 1) * 128], w[:, 8 + ft, :], start=(ft == 0), stop=(ft == 7))
                            P.stt(ot[b][:], b1[:, 0:WCB], rstd_a[:, tt:tt + 1], xres[b][:], ALU.mult, ALU.add)
                            P.stt(ot[b][:], b2[:, 0:WCB], rstd_s[:, tt:tt + 1], ot[b][:], ALU.mult, ALU.add)
                            P.dma(d_out[row0:row0 + 128, cb * WCB:(cb + 1) * WCB], ot[b][:], q="scalar")

                def proj_feature_major(col0, dst, silu):
                    w = load_w(s_win, col0, 16)
                    for f2 in range(2):
                        bank = mm_bank()
                        for dk in range(16):
                            P.mm(bank[:], w[:, dk, f2 * 128:(f2 + 1) * 128], xnTs[dk // 8][:, dk % 8, :], start=(dk == 0), stop=(dk == 15))
                        if silu:
                            P.act(dst[f2], bank[:], AF.Silu)
                        else:
                            P.cp("vector", dst[f2], bank[:])

                def proj_token_major(col0, dst_fn, silu, tts):
                    w = load_w(s_win, col0, 16)
                    for tt in tts:
                        bank = mm_bank()
                        for dk in range(16):
                            P.mm(bank[:, 0:WCB], xnTs[dk // 8][:, dk % 8, tt * 128:(tt + 1) * 128], w[:, dk, :], start=(dk == 0), stop=(dk == 15))
                        if silu:
                            P.act(dst_fn(tt), bank[:, 0:WCB], AF.Silu)
                        else:
                            P.cp("scalar" if tt % 2 else "vector", dst_fn(tt), bank[:, 0:WCB])

                stage('ssmpre')
                def nt_pre(sgi):
                    for tt in range(NT):
                        r0 = sgi * SEG + tt * 128
                        norm_transpose(d_xpre[r0:r0 + 128, :], tt)

                def nt_main(sgi):
                    for tt in range(NT):
                        r0 = sgi * SEG + tt * 128
                        norm_transpose(d_xmain[r0:r0 + 128, :], tt)

                nt_pre(0)
                for sgi in range(NPRE):
                    for c in range(4):
                        proj_feature_major(2560 + c * WCB, [uT[:, 2 * c, :], uT[:, 2 * c + 1, :]], False)
                    if sgi + 1 == NPRE:
                        norm_transpose(d_xhalo[:, :], 0)
                    elif 'd' not in KF:
                        nt_pre(sgi + 1)
                    for ft in range(8):
                        if ft == 0:
                            nxt_ = ssm_front(0, False)
                        cur_ = nxt_
                        if ft + 1 < 8:
                            nxt_ = ssm_front(ft + 1, False)
                        ssm_back(cur_)
                        if sgi + 1 < NPRE and ft % 2 == 0 and 'd' in KF:
                            r0n = (sgi + 1) * SEG + (ft // 2) * 128
                            norm_stage1(d_xpre[r0n:r0n + 128, :])
                        if sgi + 1 < NPRE and ft % 2 == 1 and 'd' in KF:
                            norm_stage2(ft // 2)

                stage('prepass')
                proj_token_major(1024, lambda tt: kv_sb[0][:, 0:256], False, [0])
                proj_token_major(1280, lambda tt: kv_sb[0][:, 256:512], False, [0])
                nt_main(0)
                prep_kv(kv_sb[0], -1, 1)
                cur_slot = 1

                stage('halo')
                for sgi in range(NSEG):
                    allt = list(range(NT))
                    for c in range(4):
                        proj_token_major(c * WCB, lambda tt, c=c: q_sb[tt][:, c * WCB:(c + 1) * WCB], False, allt)
                    proj_token_major(1024, lambda tt: kv_sb[tt][:, 0:256], False, allt)
                    proj_token_major(1280, lambda tt: kv_sb[tt][:, 256:512], False, allt)
                    for c in range(4):
                        proj_token_major(1536 + c * WCB, lambda tt, c=c: sz_sb[tt][:, c * WCB:(c + 1) * WCB], True, allt)
                    for c in range(4):
                        proj_feature_major(2560 + c * WCB, [uT[:, 2 * c, :], uT[:, 2 * c + 1, :]], False)
                    for c in range(4):
                        proj_feature_major(3584 + c * WCB, [szsT[:, 2 * c, :], szsT[:, 2 * c + 1, :]], True)
                    if dbg and sgi == 0:
                        dq = qn_t
                        P.cp("vector", dq[:], q_sb[0][:])
                        P.dma(dbg_t["proj_q"][:], dq[:], q="sync")
                    if sgi + 1 < NSEG and 'c' not in KF:
                        nt_main(sgi + 1)
                    if sgi == 0:
                        stage('proj')
                    for tt in range(NT):
                        gtt = sgi * NT + tt
                        if sgi + 1 < NSEG and 'c' in KF:
                            r0n = (sgi + 1) * SEG + tt * 128
                            norm_stage1(d_xmain[r0n:r0n + 128, :])
                        prev_slot = cur_slot
                        cur_slot = 1 - cur_slot
                        prep_kv(kv_sb[tt], gtt, cur_slot)
                        stage('kv0')
                        attention_tile(tt, cur_slot, prev_slot, first=(gtt == 0))
                        if sgi + 1 < NSEG and 'c' in KF:
                            norm_stage2(tt)
                    if sgi == 0:
                        stage('attn')
                    nxt_ = ssm_front(0, True)
                    for ft in range(8):
                        cur_ = nxt_
                        if ft + 1 < 8:
                            nxt_ = ssm_front(ft + 1, True)
                        ssm_back(cur_)
                    if sgi == 0:
                        stage('ssm')
                    glu_and_gate()
                    if sgi == 0:
                        stage('glu')
                    out_proj(sgi)
                    if sgi == 0:
                        stage('seg0')

        except _Stop:
            P.es = es

        finals = [d_out.lw] + list(d_out.rd)
        for t_ in dbg_t.values():
            finals.append(t_.lw)
        P.barrier()
        with nc.Block() as block:
            P.emit(block, [f for f in finals if f is not None])
    return nc


def _host_inputs(x, positions, norm_w, w_in, q_norm_w, k_norm_w, sinks, a_re, a_im, log_step,
                 b_re, b_im, c_re, c_im, d_skip, w_glu, b_glu, attn_out_norm_w, ssm_out_norm_w, w_out):
    f32 = np.float32
    rep = lambda v: np.ascontiguousarray(np.broadcast_to(np.asarray(v, f32)[None, :], (128, len(v))))
    pl = lambda v, n: np.ascontiguousarray(np.asarray(v, f32).reshape(n, 128).T)

    def p_layout(a):
        a = np.asarray(a, f32).reshape(32, 2, 64)
        return np.ascontiguousarray(a.transpose(1, 2, 0).reshape(128, 32))

    def bc_layout(m):
        m = np.asarray(m, f32).reshape(32, 2, 64, 16)
        o = np.zeros((2, 64, 32, 2, 16), f32)
        for g2 in range(2):
            o[g2, :, :, g2, :] = m[:, g2].transpose(1, 0, 2)
        return np.ascontiguousarray(o.reshape(128, 32 * 32))

    common = {
        "w_in": np.ascontiguousarray(w_in, f32), "w_out": np.ascontiguousarray(w_out, f32), "w_glu": np.ascontiguousarray(w_glu, f32),
        "normw": pl(norm_w, 16), "outnw": pl(np.concatenate([np.asarray(attn_out_norm_w), np.asarray(ssm_out_norm_w)]), 16),
        "bglu": pl(b_glu, 8), "dskip": pl(d_skip, 8),
        "qw": rep(q_norm_w), "kw": rep(k_norm_w), "sinks": rep(sinks),
        "invf": rep((10000.0 ** (-np.arange(0, 64, 2, dtype=np.float64) / 64) / (2 * np.pi)).astype(f32)),
        "are": p_layout(a_re), "aim": p_layout(a_im),
        "lst": p_layout(np.broadcast_to(np.asarray(log_step, f32)[:, None], (64, 64))),
        "bre": bc_layout(b_re), "bim": bc_layout(b_im),
        "cre": bc_layout(np.asarray(c_re).transpose(0, 2, 1)), "cim": bc_layout(np.asarray(c_im).transpose(0, 2, 1)),
        "ident": np.eye(128, dtype=f32),
        "maskc": np.triu(np.ones((128, 128), f32)),
        "maskp": np.tril(np.ones((128, 128), f32), -1),
        "kvec": rep(np.array(KS, f32)), "jrow": rep(np.arange(-1, NB).astype(f32)),
    }
    x = np.asarray(x, f32)
    positions = np.asarray(positions, np.int32)
    maps = []
    for c in range(8):
        b, half = c // 2, c % 2
        t0 = half * TC
        m = dict(common)
        m["x_main"] = np.ascontiguousarray(x[b, t0:t0 + TC])
        pos = positions[b, t0:t0 + TC]
        m["pos_main"] = np.ascontiguousarray(pos.reshape(TC // 128, 128).T)
        if half == 1:
            m["x_pre"] = np.ascontiguousarray(x[b, 0:TC])
            m["x_halo"] = np.ascontiguousarray(x[b, t0 - 128:t0])
            m["pos_halo"] = np.ascontiguousarray(positions[b, t0 - 128:t0].reshape(128, 1))
            m["maskf"] = common["maskp"]
        else:
            m["x_pre"] = np.zeros((TC, D), f32)
            m["x_halo"] = np.zeros((128, D), f32)
            m["pos_halo"] = np.zeros((128, 1), np.int32)
            m["maskf"] = np.zeros((128, 128), f32)
        maps.append(m)
    return maps


_NC = {}


def kernel(**inputs):
    maps = _host_inputs(**inputs)
    if "nc" not in _NC:
        _NC["nc"] = build_program()
    res = run_bass_kernel_spmd(_NC["nc"], maps, core_ids=list(range(8)))
    out = np.empty((4, 4096, D), np.float32)
    for c in range(8):
        b, half = c // 2, c % 2
        out[b, half * TC:(half + 1) * TC] = res.results[c]["out"]
    return out
```
